# Optimizing a Trainium2 kernel written in Bass

```python
import math
import jax, jax.numpy as jnp
from jax import lax
import numpy as np

D_MODEL = 2048
BATCH = 4
SEQ = 2048
DEPTH = 1
DEC_BATCH = 128
DEC_SEQ = 1
PAST_LEN = 16384
PAGE_SIZE = 128

SSD_HEADS = 32
SSD_HEAD_DIM = 64
SSD_WIDTH = SSD_HEADS * SSD_HEAD_DIM
SSD_STATE = 128
SSD_GROUPS = 4
SSD_HPG = SSD_HEADS // SSD_GROUPS
SSD_CONV = 4
SSD_CHUNK = 128
SSD_CONV_DIM = SSD_WIDTH + 2 * SSD_GROUPS * SSD_STATE

SC_WIDTH = 2048
SC_CONV = 3

PEER_HEADS = 8
PEER_KEYS = 128
PEER_EXPERTS = PEER_KEYS * PEER_KEYS
PEER_DKEY = 256
PEER_TOPK = 16
PEER_BLOCK = 128

PLE_DIM = 256
ALPHA = (2 * DEPTH) ** 0.25
BETA = (8 * DEPTH) ** -0.25
LN_EPS = 1e-5
RMS_EPS = 1e-5

IN_SPLITS = [SSD_WIDTH, SSD_CONV_DIM, SSD_HEADS, SC_WIDTH, SC_WIDTH, SC_WIDTH, D_MODEL, D_MODEL]
IN_COLS = sum(IN_SPLITS)
IN_OFFSETS = list(np.cumsum(IN_SPLITS)[:-1].tolist())

kernel_name = "hybrid_ssd_shortconv_peer_step"


def layer_norm(x, g, b):
    xf = x.astype(jnp.float32)
    mu = jnp.mean(xf, -1, keepdims=True)
    var = jnp.mean(jnp.square(xf - mu), -1, keepdims=True)
    return ((xf - mu) * lax.rsqrt(var + LN_EPS) * g.astype(jnp.float32) + b.astype(jnp.float32)).astype(x.dtype)


def causal_dwconv(x, buf, w):
    k = w.shape[0]
    t = x.shape[1]
    xx = jnp.concatenate([buf.astype(x.dtype), x], axis=1)
    y = xx[:, 0:t] * w[0]
    for j in range(1, k):
        y = y + xx[:, j:j + t] * w[j]
    return y, xx[:, t:]


def ssd_scan(X, A, Bm, Cm, h0):
    b, T = X.shape[:2]
    L = SSD_CHUNK if T % SSD_CHUNK == 0 else T
    c = T // L
    G, R, P, N = SSD_GROUPS, SSD_HPG, SSD_HEAD_DIM, SSD_STATE
    X = X.reshape(b, c, L, G, R, P)
    A = A.reshape(b, c, L, G, R)
    Bc = Bm.reshape(b, c, L, G, N)
    Cc = Cm.reshape(b, c, L, G, N)
    Acs = jnp.cumsum(A, axis=2)
    At = jnp.moveaxis(Acs, 2, -1)
    seg = At[..., :, None] - At[..., None, :]
    causal = jnp.tril(jnp.ones((L, L), dtype=bool))
    Lmat = jnp.exp(jnp.where(causal, seg, -jnp.inf))
    CB = jnp.einsum('bclgn,bcsgn->bcgls', Cc, Bc)
    y_diag = jnp.einsum('bcgls,bcgrls,bcsgrp->bclgrp', CB, Lmat, X)
    decay = jnp.exp(Acs[:, :, -1:] - Acs)
    states = jnp.einsum('bclgn,bclgr,bclgrp->bcgrpn', Bc, decay, X)
    chunk_decay = jnp.exp(Acs[:, :, -1])

    def step(h, inp):
        s_c, d_c = inp
        return h * d_c[..., None, None] + s_c, h

    h_final, h_prev = lax.scan(step, h0, (jnp.moveaxis(states, 1, 0), jnp.moveaxis(chunk_decay, 1, 0)))
    h_prev = jnp.moveaxis(h_prev, 0, 1)
    y_off = jnp.einsum('bclgn,bcgrpn,bclgr->bclgrp', Cc, h_prev, jnp.exp(Acs))
    return (y_diag + y_off).reshape(b, T, G, R, P), h_final


def ssd_branch(z, xbc, dt, conv_buf, h0, conv_w, conv_b, dt_bias, a_log, d_skip, norm_w):
    b, T = z.shape[:2]
    G, R, P, N = SSD_GROUPS, SSD_HPG, SSD_HEAD_DIM, SSD_STATE
    xbc, new_buf = causal_dwconv(xbc, conv_buf, conv_w)
    xbc = jax.nn.silu(xbc + conv_b).astype(jnp.float32)
    xs = xbc[..., :SSD_WIDTH].reshape(b, T, G, R, P)
    Bm = xbc[..., SSD_WIDTH:SSD_WIDTH + G * N].reshape(b, T, G, N)
    Cm = xbc[..., SSD_WIDTH + G * N:].reshape(b, T, G, N)
    dtf = jax.nn.softplus(dt.astype(jnp.float32) + dt_bias.astype(jnp.float32)).reshape(b, T, G, R)
    A = -jnp.exp(a_log.astype(jnp.float32)).reshape(G, R)
    h0g = h0.astype(jnp.float32).reshape(b, G, R, P, N)
    y, h = ssd_scan(xs * dtf[..., None], dtf * A, Bm, Cm, h0g)
    y = y + xs * d_skip.astype(jnp.float32).reshape(G, R)[:, :, None]
    y = y.reshape(b, T, G, R * P) * jax.nn.silu(z.astype(jnp.float32)).reshape(b, T, G, R * P)
    y = y * lax.rsqrt(jnp.mean(y * y, -1, keepdims=True) + RMS_EPS)
    y = y.reshape(b, T, SSD_WIDTH) * norm_w.astype(jnp.float32)
    return y.astype(z.dtype), new_buf, h.reshape(b, SSD_HEADS, P, N)


def shortconv_branch(gate_b, gate_c, h, conv_buf, conv_w):
    u, new_buf = causal_dwconv(gate_c * h, conv_buf, conv_w)
    return gate_b * u, new_buf


def peer(x, w_q, keys1, keys2, u_tab, v_tab):
    b, T, D = x.shape
    n = b * T
    n_pad = -(-n // PEER_BLOCK) * PEER_BLOCK
    xt = jnp.pad(x.reshape(n, D), ((0, n_pad - n), (0, 0)))
    q = (xt @ w_q).astype(jnp.float32).reshape(n_pad, PEER_HEADS, 2, PEER_DKEY // 2)
    s1 = jnp.einsum('thd,hkd->thk', q[:, :, 0], keys1.astype(jnp.float32))
    s2 = jnp.einsum('thd,hkd->thk', q[:, :, 1], keys2.astype(jnp.float32))
    t1, i1 = lax.top_k(s1, PEER_TOPK)
    t2, i2 = lax.top_k(s2, PEER_TOPK)
    cand = (t1[..., :, None] + t2[..., None, :]).reshape(n_pad, PEER_HEADS, PEER_TOPK * PEER_TOPK)
    cand_idx = (i1[..., :, None] * PEER_KEYS + i2[..., None, :]).reshape(n_pad, PEER_HEADS, PEER_TOPK * PEER_TOPK)
    top, pos = lax.top_k(cand, PEER_TOPK)
    idx = jnp.take_along_axis(cand_idx, pos, axis=-1)
    gate = jax.nn.softmax(top, axis=-1)

    def block(args):
        xb, ib, gb = args
        hid = jax.nn.gelu(jnp.einsum('td,thkd->thk', xb, u_tab[ib]).astype(jnp.float32), approximate=False)
        coef = (gb * hid).astype(xb.dtype)
        return jnp.einsum('thk,thkd->td', coef, v_tab[ib])

    nb = n_pad // PEER_BLOCK
    out = lax.map(block, (xt.reshape(nb, PEER_BLOCK, D),
                          idx.reshape(nb, PEER_BLOCK, PEER_HEADS, PEER_TOPK),
                          gate.reshape(nb, PEER_BLOCK, PEER_HEADS, PEER_TOPK)))
    return out.reshape(n_pad, D)[:n].reshape(b, T, D)


def trunk_layer(x, p, ssd_conv_buf, ssm_h, sc_buf,
                w_in, ssd_conv_w, ssd_conv_b, ssd_dt_bias, ssd_a_log, ssd_d, ssd_norm_w,
                sc_conv_w, w_branch_ssd, w_branch_sc, w_out, ln1_g, ln1_b,
                peer_wq, peer_keys1, peer_keys2, peer_u, peer_v, ln2_g, ln2_b,
                ple_gate_w, ple_proj_w):
    proj = x @ w_in
    z, xbc, dt, sc_b, sc_c, sc_h, g_a, g_b = jnp.split(proj, IN_OFFSETS, axis=-1)
    y_a, new_ssd_conv, new_h = ssd_branch(z, xbc, dt, ssd_conv_buf, ssm_h, ssd_conv_w, ssd_conv_b,
                                          ssd_dt_bias, ssd_a_log, ssd_d, ssd_norm_w)
    y_b, new_sc = shortconv_branch(sc_b, sc_c, sc_h, sc_buf, sc_conv_w)
    mix = jax.nn.sigmoid(g_a) * (y_a @ w_branch_ssd) + jax.nn.sigmoid(g_b) * (y_b @ w_branch_sc)
    x1 = layer_norm(ALPHA * x + mix @ w_out, ln1_g, ln1_b)
    x2 = layer_norm(ALPHA * x1 + peer(x1, peer_wq, peer_keys1, peer_keys2, peer_u, peer_v), ln2_g, ln2_b)
    y = x2 + jax.nn.sigmoid(x2 @ ple_gate_w) * (p @ ple_proj_w)
    return y, new_ssd_conv, new_h, new_sc


def setup_inputs(seed: int = 0) -> dict:
    key = jax.random.key(seed)
    ks = jax.random.split(key, 32)
    f32 = jnp.float32
    nrm = lambda k, shape, s: jax.random.normal(k, shape, f32) * s
    dt0 = jnp.exp(jax.random.uniform(ks[10], (DEPTH, SSD_HEADS), f32) * (math.log(0.1) - math.log(0.001)) + math.log(0.001))
    return {
        "x_prompt": nrm(ks[0], (BATCH, SEQ, D_MODEL), 1.0),
        "x_sample": nrm(ks[1], (DEC_BATCH, DEC_SEQ, D_MODEL), 1.0),
        "p_prompt": nrm(ks[2], (DEPTH, BATCH, SEQ, PLE_DIM), 1.0),
        "p_sample": nrm(ks[3], (DEPTH, DEC_BATCH, DEC_SEQ, PLE_DIM), 1.0),
        "state_ssm": nrm(ks[4], (DEPTH, DEC_BATCH, SSD_HEADS, SSD_HEAD_DIM, SSD_STATE), 0.1),
        "state_ssd_conv": nrm(ks[5], (DEPTH, DEC_BATCH, SSD_CONV - 1, SSD_CONV_DIM), 1.0),
        "state_shortconv": nrm(ks[6], (DEPTH, DEC_BATCH, SC_CONV - 1, SC_WIDTH), 1.0),
        "w_in": nrm(ks[7], (DEPTH, D_MODEL, IN_COLS), D_MODEL ** -0.5),
        "ssd_conv_w": nrm(ks[8], (DEPTH, SSD_CONV, SSD_CONV_DIM), SSD_CONV ** -0.5),
        "ssd_conv_b": nrm(ks[9], (DEPTH, SSD_CONV_DIM), 0.02),
        "ssd_dt_bias": dt0 + jnp.log(-jnp.expm1(-dt0)),
        "ssd_a_log": jnp.log(jax.random.uniform(ks[11], (DEPTH, SSD_HEADS), f32, 1.0, 16.0)),
        "ssd_d": 1.0 + nrm(ks[12], (DEPTH, SSD_HEADS), 0.02),
        "ssd_norm_w": 1.0 + nrm(ks[13], (DEPTH, SSD_WIDTH), 0.02),
        "sc_conv_w": nrm(ks[14], (DEPTH, SC_CONV, SC_WIDTH), SC_CONV ** -0.5),
        "w_branch_ssd": nrm(ks[15], (DEPTH, SSD_WIDTH, D_MODEL), SSD_WIDTH ** -0.5),
        "w_branch_sc": nrm(ks[16], (DEPTH, SC_WIDTH, D_MODEL), SC_WIDTH ** -0.5),
        "w_out": nrm(ks[17], (DEPTH, D_MODEL, D_MODEL), BETA * D_MODEL ** -0.5),
        "ln1_g": 1.0 + nrm(ks[18], (DEPTH, D_MODEL), 0.02),
        "ln1_b": nrm(ks[19], (DEPTH, D_MODEL), 0.02),
        "peer_wq": nrm(ks[20], (DEPTH, D_MODEL, PEER_HEADS * PEER_DKEY), D_MODEL ** -0.5),
        "peer_keys1": nrm(ks[21], (DEPTH, PEER_HEADS, PEER_KEYS, PEER_DKEY // 2), (PEER_DKEY // 2) ** -0.5),
        "peer_keys2": nrm(ks[22], (DEPTH, PEER_HEADS, PEER_KEYS, PEER_DKEY // 2), (PEER_DKEY // 2) ** -0.5),
        "peer_u": nrm(ks[23], (DEPTH, PEER_EXPERTS, D_MODEL), D_MODEL ** -0.5),
        "peer_v": nrm(ks[24], (DEPTH, PEER_EXPERTS, D_MODEL), BETA * PEER_HEADS ** -0.5),
        "ln2_g": 1.0 + nrm(ks[25], (DEPTH, D_MODEL), 0.02),
        "ln2_b": nrm(ks[26], (DEPTH, D_MODEL), 0.02),
        "ple_gate_w": nrm(ks[27], (DEPTH, D_MODEL, D_MODEL), D_MODEL ** -0.5),
        "ple_proj_w": nrm(ks[28], (DEPTH, PLE_DIM, D_MODEL), PLE_DIM ** -0.5),
    }


def reference(x_prompt, x_sample, p_prompt, p_sample, state_ssm, state_ssd_conv, state_shortconv,
              w_in, ssd_conv_w, ssd_conv_b, ssd_dt_bias, ssd_a_log, ssd_d, ssd_norm_w,
              sc_conv_w, w_branch_ssd, w_branch_sc, w_out, ln1_g, ln1_b,
              peer_wq, peer_keys1, peer_keys2, peer_u, peer_v, ln2_g, ln2_b,
              ple_gate_w, ple_proj_w):
    bp = x_prompt.shape[0]
    yp, ys = x_prompt, x_sample
    hp_l, cp_l, sp_l, hs_l, cs_l, ss_l = [], [], [], [], [], []
    for i in range(DEPTH):
        lw = (w_in[i], ssd_conv_w[i], ssd_conv_b[i], ssd_dt_bias[i], ssd_a_log[i], ssd_d[i], ssd_norm_w[i],
              sc_conv_w[i], w_branch_ssd[i], w_branch_sc[i], w_out[i], ln1_g[i], ln1_b[i],
              peer_wq[i], peer_keys1[i], peer_keys2[i], peer_u[i], peer_v[i], ln2_g[i], ln2_b[i],
              ple_gate_w[i], ple_proj_w[i])
        yp, cp, hp, sp = trunk_layer(
            yp, p_prompt[i],
            jnp.zeros((bp, SSD_CONV - 1, SSD_CONV_DIM), x_prompt.dtype),
            jnp.zeros((bp, SSD_HEADS, SSD_HEAD_DIM, SSD_STATE), jnp.float32),
            jnp.zeros((bp, SC_CONV - 1, SC_WIDTH), x_prompt.dtype), *lw)
        ys, cs, hs, ss = trunk_layer(ys, p_sample[i], state_ssd_conv[i], state_ssm[i], state_shortconv[i], *lw)
        hp_l.append(hp); cp_l.append(cp); sp_l.append(sp)
        hs_l.append(hs); cs_l.append(cs); ss_l.append(ss)
    return (yp, ys, jnp.stack(hp_l), jnp.stack(cp_l), jnp.stack(sp_l), jnp.stack(hs_l), jnp.stack(cs_l), jnp.stack(ss_l))
```

```python
import numpy as np
from contextlib import ExitStack
import concourse.bass as bass
import concourse.mybir as mybir
from concourse.bass_utils import run_bass_kernel_spmd

F32 = mybir.dt.float32
BF16 = mybir.dt.bfloat16
U32 = mybir.dt.uint32
ALU = mybir.AluOpType
AF = mybir.ActivationFunctionType
AX = mybir.AxisListType

ENGS = ("tensor", "vector", "scalar", "gpsimd", "sync")
NCORES = 8
D = 2048
NOWN = 1024
NS = 16
NT = NOWN + NS
NEXT = NT + 3
ALPHA = 2.0 ** 0.25
LN_EPS = 1e-5
RMS_EPS = 1e-5
IN_COLS = 15392
OFF_Z, OFF_XBC, OFF_DT, OFF_SCB, OFF_SCC, OFF_SCH, OFF_GA, OFF_GB = 0, 2048, 5120, 5152, 7200, 9248, 11296, 13344

DEBUG = {}
CUT = 0


EXCL = {"bank", "psmm", "psdt", "pstr", "psst", "S1", "S2"}


class Res:
    __slots__ = ("name", "lw", "rd", "excl")

    def __init__(self, name):
        self.name = name
        self.lw = None
        self.rd = []
        self.excl = bool(name) and name[0] in EXCL


class Op:
    __slots__ = ("eng", "fn", "deps", "signal", "dma_key", "dma_val", "sigidx", "call")


class _Rec:
    def __init__(self):
        self.call = None

    def __getattr__(self, name):
        def f(*a, **k):
            self.call = (name, a, k)
            return self
        return f


class Prog:
    UID = 0
    G = {}

    def __init__(self, nc):
        self.nc = nc
        self.ops = {e: [] for e in ENGS}
        self.dma_cnt = {}
        self.res_cache = {}

    def R(self, *key):
        r = self.res_cache.get(key)
        if r is None:
            r = Res(key)
            self.res_cache[key] = r
        return r

    def op(self, eng, fn, r=(), w=(), dma=None, signal=True):
        o = Op()
        o.eng = eng
        o.fn = None
        rec = _Rec()
        fn(rec)
        o.call = rec.call
        o.signal = signal or (dma is not None)
        o.dma_key = dma
        o.dma_val = None
        o.sigidx = None
        if dma is not None:
            c = self.dma_cnt.get(dma, 0) + 16
            self.dma_cnt[dma] = c
            o.dma_val = c
        if any(res.excl for res in r):
            w = list(w) + [res for res in r if res.excl and res not in w]
            r = [res for res in r if not res.excl]
        deps = []
        for res in r:
            if res.lw is not None:
                deps.append(res.lw)
        for res in w:
            if res.lw is not None:
                deps.append(res.lw)
            deps.extend(res.rd)
        seen = set()
        dd = []
        for d in deps:
            if id(d) in seen:
                continue
            seen.add(id(d))
            if d.eng == eng and d.dma_key is None and dma is None and eng == "tensor":
                continue
            dd.append(d)
        o.deps = dd
        self.ops[eng].append(o)
        for res in r:
            res.rd.append(o)
        for res in w:
            res.lw = o
            res.rd = []
        return o

    def emit(self):
        nc = self.nc
        G = Prog.G
        if "esem" not in G:
            G["esem"] = {e: G["stack"].enter_context(nc.semaphore(f"se_{e}")) for e in ENGS}
            G["bar"] = G["stack"].enter_context(nc.semaphore("sbar"))
            G["ebase"] = {e: 0 for e in ENGS}
            G["barbase"] = 0
        esem = G["esem"]
        bar = G["bar"]
        ebase = dict(G["ebase"])
        final_cnt = {}
        for e in ENGS:
            last = None
            for o in self.ops[e]:
                if o.dma_key is None:
                    last = o
            if last is not None:
                last.signal = True
            cnt = ebase[e]
            pending = []
            for o in self.ops[e]:
                if o.dma_key is not None:
                    continue
                pending.append(o)
                if o.signal:
                    cnt += 1
                    for p in pending:
                        p.sigidx = cnt
                    pending = []
            final_cnt[e] = cnt
            G["ebase"][e] = cnt
        G["barbase"] += len(ENGS)
        bar_target = G["barbase"]
        Prog.UID += 1
        u = Prog.UID
        dsem = {k: G["stack"].enter_context(nc.semaphore(f"sd{u}_{k}")) for k in self.dma_cnt}
        with nc.Block() as block:

            def run(e_name, eng):
                seen = {}
                for o in self.ops[e_name]:
                    need = {}
                    for d in o.deps:
                        if d.dma_key is not None:
                            k = ("d", d.dma_key)
                            v = d.dma_val
                        else:
                            k = ("e", d.eng)
                            v = d.sigidx
                        if need.get(k, 0) < v:
                            need[k] = v
                    for k, v in need.items():
                        if seen.get(k, 0) >= v:
                            continue
                        seen[k] = v
                        eng.wait_ge(dsem[k[1]] if k[0] == "d" else esem[k[1]], v)
                    name_, a_, k_ = o.call
                    ins = getattr(eng, name_)(*a_, **k_)
                    if o.dma_key is not None:
                        ins.then_inc(dsem[o.dma_key], 16)
                    elif o.signal:
                        ins.then_inc(esem[e_name], 1)
                if final_cnt[e_name] > ebase[e_name]:
                    eng.wait_ge(esem[e_name], final_cnt[e_name])
                fin = {}
                for o in self.ops[e_name]:
                    if o.dma_key is not None:
                        fin[o.dma_key] = self.dma_cnt[o.dma_key]
                for k, v in fin.items():
                    eng.wait_ge(dsem[k], v)
                eng.sem_inc(bar, 1)
                eng.wait_ge(bar, bar_target)

            @block.tensor
            def _(eng):
                run("tensor", eng)

            @block.vector
            def _(eng):
                run("vector", eng)

            @block.scalar
            def _(eng):
                run("scalar", eng)

            @block.gpsimd
            def _(eng):
                run("gpsimd", eng)

            @block.sync
            def _(eng):
                run("sync", eng)


def split_even(n, maxsz=512):
    k = -(-n // maxsz)
    base, rem = divmod(n, k)
    out = []
    s = 0
    for i in range(k):
        sz = base + (1 if i < rem else 0)
        out.append((s, sz))
        s += sz
    return out


class Phase:
    def __init__(self, kb, name):
        self.kb = kb
        self.nc = kb.nc
        self.name = name
        self.es = ExitStack()
        self.P = Prog(kb.nc)
        self.wcnt = 0
        self.uid = 0

    def sb(self, name, shape, dt=F32):
        return self.es.enter_context(self.nc.sbuf_tensor(f"{self.name}_{name}", list(shape), dt))

    def ps(self, name, shape, dt=F32):
        return self.es.enter_context(self.nc.psum_tensor(f"{self.name}_{name}", list(shape), dt))

    def R(self, *k):
        return self.P.R(*k)

    def op(self, *a, **k):
        return self.P.op(*a, **k)

    def end(self):
        self.P.emit()
        self.es.close()

    def wload(self, src2d, ncols, rows=2048):
        kb = self.kb
        i = self.wcnt % kb.NW
        self.wcnt += 1
        wb = kb.wbufs[i]
        res = self.R("wb", i)
        kt = rows // 128
        src = src2d.rearrange("(k p) c -> p k c", p=128)
        half = kt // 2 if kt >= 2 else kt
        self.op("gpsimd", lambda e: e.dma_start(out=wb[:, 0:half, 0:ncols], in_=src[:, 0:half, :]),
                w=[res], dma=f"wb{i}")
        if half < kt:
            o = self.P.op("gpsimd", lambda e: e.dma_start(out=wb[:, half:kt, 0:ncols], in_=src[:, half:kt, :]),
                          r=[], w=[], dma=f"wb{i}")
            res.lw = o
        return wb, res


class KB:
    pass


def build_program(stage=99):
    nc = bass.Bass("TRN2", target_bir_lowering=False)
    kb = KB()
    kb.nc = nc
    din = {}
    dout = {}

    def DI(name, shape, dt=F32):
        din[name] = nc.dram_tensor(name, list(shape), dt, kind="ExternalInput").ap()
        return din[name]

    def DO(name, shape, dt=F32):
        dout[name] = nc.dram_tensor(name, list(shape), dt, kind="ExternalOutput").ap()
        return dout[name]

    xprevT = DI("xprevT", [D, NOWN])
    xextT = DI("xextT", [D, NEXT])
    xown = DI("xown", [NT, D])
    pT = DI("pT", [256, NT])
    flag = DI("flag", [128, 1])
    ssm0 = DI("ssm0", [NS, 2048, 128])
    cv0 = DI("cv0", [NS, 3, 3072])
    cv0T = DI("cv0T", [128, 24 * NS * 3])
    sc0 = DI("sc0", [NS, 2, 2048])
    sc0T = DI("sc0T", [128, 16 * NS * 2])
    w_in = DI("w_in", [D, IN_COLS])
    convwT = DI("convwT", [128, 24 * 4])
    convb = DI("convb", [128, 24])
    dtb = DI("dtb", [1, 32])
    alog = DI("alog", [1, 32])
    dsk = DI("dsk", [1, 32])
    normw = DI("normw", [1, 2048])
    scwT = DI("scwT", [128, 16 * 3])
    if stage >= 4:
        DI("w_bssd", [D, D])
        DI("w_bsc", [D, D])
        DI("w_out", [D, D])
    DI("ln1gT", [128, 16])
    DI("ln1bT", [128, 16])
    if stage >= 6:
        DI("wq", [D, D])
        DI("k1T", [8, 128, 128])
        DI("k2T", [8, 128, 128])
        DI("uT", [D, 16384])
        DI("vtab", [16384, D])
    DI("ln2gT", [128, 16])
    DI("ln2bT", [128, 16])
    if stage >= 8:
        DI("wple", [D, D])
        DI("wpp", [256, D])
    c_ident = DI("c_ident", [128, 128])
    c_tri = DI("c_tri", [128, 128])
    c_iota = DI("c_iota", [1, 256])
    c_negm = DI("c_negm", [128, 128])

    y_o = DO("y", [NT, D])
    hfin_o = DO("hfin", [2048, 128])
    cvp_o = DO("cvp", [3, 3072])
    scp_o = DO("scp", [2, 2048])
    hs_o = DO("hs", [NS, 2048, 128])
    cvs_o = DO("cvs", [NS, 3, 3072])
    scs_o = DO("scs", [NS, 2, 2048])
    dbg = {}
    for name, (shape, dt) in DEBUG.items():
        dbg[name] = DO(name, shape, dt)

    x1s = nc.dram_tensor("x1s", [D, NT], F32).ap()
    gsc = nc.dram_tensor("gsc", [128, 9, 128, 128], BF16).ap()

    with ExitStack() as glob:
        Prog.G = {"stack": glob}

        def gsb(name, shape, dt=F32):
            return glob.enter_context(nc.sbuf_tensor("g_" + name, list(shape), dt))

        kb.NW = 3
        kb.wbufs = [gsb(f"wb{i}", [128, 16, 256], BF16) for i in range(kb.NW)]
        identf = gsb("identf", [128, 128])
        identb = gsb("identb", [128, 128], BF16)
        trif = gsb("trif", [128, 128])
        onesf = gsb("onesf", [128, 128])
        negm = gsb("negm", [128, 128])
        onesb = gsb("onesb", [128, 128], BF16)
        flag_sb = gsb("flag", [128, 1])
        hmid = gsb("hmid", [128, 2048])
        cwT = gsb("cwT", [128, 24, 4])
        cbT = gsb("cbT", [128, 24])
        swT = gsb("swT", [128, 16, 3])
        dtb_bc = gsb("dtb_bc", [128, 32])
        A_bc = gsb("A_bc", [128, 32])
        D_bc = gsb("D_bc", [128, 32])
        cvc = gsb("cvc", [128, 24, 19])
        ucc = gsb("ucc", [128, 16, 18])

        ph = Phase(kb, "p0")
        ph.op("sync", lambda e: e.dma_start(out=identf[:], in_=c_ident[:, :]), w=[ph.R("identf")], dma="c0")
        ph.op("sync", lambda e: e.dma_start(out=trif[:], in_=c_tri[:, :]), w=[ph.R("trif")], dma="c1")
        ph.op("sync", lambda e: e.dma_start(out=negm[:], in_=c_negm[:, :]), w=[ph.R("negm")], dma="c1b")
        ph.op("gpsimd", lambda e: e.dma_start(out=identb[:], in_=c_ident[:, :]), w=[ph.R("identb")], dma="c2")
        ph.op("sync", lambda e: e.dma_start(out=flag_sb[:], in_=flag[:, :]), w=[ph.R("flag")], dma="c3")
        ph.op("sync", lambda e: e.dma_start(out=cwT[:], in_=convwT.rearrange("p (k j) -> p k j", j=4)), w=[ph.R("cwT")], dma="c4")
        ph.op("sync", lambda e: e.dma_start(out=cbT[:], in_=convb[:, :]), w=[ph.R("cbT")], dma="c5")
        ph.op("sync", lambda e: e.dma_start(out=swT[:], in_=scwT.rearrange("p (k j) -> p k j", j=3)), w=[ph.R("swT")], dma="c6")
        ph.op("sync", lambda e: e.dma_start(out=dtb_bc[:], in_=dtb.partition_broadcast(128)), w=[ph.R("dtb")], dma="c7")
        ph.op("sync", lambda e: e.dma_start(out=A_bc[:], in_=alog.partition_broadcast(128)), w=[ph.R("A")], dma="c8")
        ph.op("sync", lambda e: e.dma_start(out=D_bc[:], in_=dsk.partition_broadcast(128)), w=[ph.R("Dsk")], dma="c9")
        ph.op("vector", lambda e: e.memset(onesf[:], 1.0), w=[ph.R("onesf")])
        ph.op("vector", lambda e: e.memset(onesb[:], 1.0), w=[ph.R("onesb")])
        ph.op("scalar", lambda e: e.activation(out=A_bc[:], in_=A_bc[:], func=AF.Exp), r=[ph.R("A")], w=[ph.R("A")])
        ph.op("vector", lambda e: e.tensor_scalar(out=A_bc[:], in0=A_bc[:], scalar1=-1.0, scalar2=None, op0=ALU.mult),
              r=[ph.R("A")], w=[ph.R("A")])
        ph.end()

        consts = dict(identf=identf, identb=identb, trif=trif, onesf=onesf, onesb=onesb, flag=flag_sb, negm=negm,
                      cwT=cwT, cbT=cbT, swT=swT, dtb_bc=dtb_bc, A_bc=A_bc, D_bc=D_bc)
        kb.c = consts

        mixT = nc.dram_tensor("mixs", [D, NT], BF16).ap()
        with ExitStack() as s1:
            def s1sb(name, shape, dt=F32):
                return s1.enter_context(nc.sbuf_tensor("g_" + name, list(shape), dt))
            kb.xT = s1sb("xT", [128, 16, NEXT], BF16)
            y_aT = s1sb("y_aT", [128, 16, NT], BF16)
            if stage >= 1:
                phase_prefix(kb, din, hmid)
            if stage >= 2:
                phase_ssd(kb, din, dout, dbg, hmid, y_aT, cvc)
            if "d_yaT" in dbg:
                dbg_dump(kb, dbg["d_yaT"], y_aT)
            y_bT = s1sb("y_bT", [128, 16, NT], BF16)
            if stage >= 3:
                phase_sc(kb, din, y_bT, ucc)
            if "d_ybT" in dbg:
                dbg_dump(kb, dbg["d_ybT"], y_bT)
            if stage >= 4:
                phase_mix(kb, din, y_aT, y_bT, mixT)
            if "d_mixT" in dbg:
                ph = Phase(kb, "dbgmx")
                ph.op("sync", lambda e: e.dma_start(out=dbg["d_mixT"][:, :], in_=mixT[:, :]), dma="st")
                ph.end()
        accT = gsb("accT", [128, 16, NT])
        x1T = gsb("x1T", [128, 16, NT], BF16)
        if stage >= 5:
            phase_ln1(kb, din, mixT, x1T, x1s, accT)
        if "d_x1T" in dbg:
            ph = Phase(kb, "dbgx1")
            ph.op("sync", lambda e: e.dma_start(out=dbg["d_x1T"][:, :], in_=x1s[:, :]), dma="st")
            ph.end()
        if stage >= 6:
            phase_route(kb, din, x1T, gsc, accT)
        if stage >= 7:
            phase_peer(kb, din, x1T, gsc, accT)
        if "d_peT" in dbg:
            dbg_dump(kb, dbg["d_peT"], accT)
        if stage >= 8:
            phase_final(kb, din, dout, accT, x1T, x1s, cvc, ucc)
    return nc, din, dout


def dt_chain(ph, kb, tag, n, dt_ps, dt_ps_res, dtf, a_t, res_prefix):
    c = kb.c
    R = ph.R
    t0 = (res_prefix, "dtf")
    ph.op("vector", lambda e: e.tensor_tensor(out=dtf, in0=dt_ps, in1=c["dtb_bc"][0:n, :], op=ALU.add),
          r=[dt_ps_res], w=[R(*t0)])
    ph.op("scalar", lambda e: e.activation(out=dtf, in_=dtf, func=AF.Exp), r=[R(*t0)], w=[R(*t0)])
    ph.op("scalar", lambda e: e.activation(out=dtf, in_=dtf, func=AF.Ln, bias=1.0), r=[R(*t0)], w=[R(*t0)])
    ph.op("vector", lambda e: e.tensor_tensor(out=a_t, in0=dtf, in1=c["A_bc"][0:n, :], op=ALU.mult),
          r=[R(*t0)], w=[R(res_prefix, "a")])


def conv_silu(ph, kb, raw, raw_res, ntok, wcols, bias_col, out_bf, out_res, acc, acc_res, taps, lead):
    ph.op("vector", lambda e: e.tensor_scalar(out=acc, in0=raw[:, lead:lead + ntok], scalar1=wcols[:, taps - 1:taps],
                                              scalar2=None, op0=ALU.mult), r=[raw_res], w=[acc_res])
    for j in range(taps - 1):
        ph.op("vector", lambda e, j=j: e.scalar_tensor_tensor(out=acc, in0=raw[:, j + lead - (taps - 1):j + lead - (taps - 1) + ntok],
                                                           scalar=wcols[:, j:j + 1], in1=acc, op0=ALU.mult, op1=ALU.add),
              r=[raw_res, acc_res], w=[acc_res])
    if bias_col is not None:
        ph.op("scalar", lambda e: e.activation(out=out_bf, in_=acc, func=AF.Silu, bias=bias_col, scale=1.0),
              r=[acc_res], w=[out_res])


def phase_prefix(kb, din, hmid):
    nc = kb.nc
    c = kb.c
    ph = Phase(kb, "pf")
    R = ph.R
    w_in = din["w_in"]
    xT = ph.sb("xT", [128, 16, NOWN], BF16)
    xsrc = din["xprevT"].rearrange("(k p) t -> p k t", p=128)
    for q in range(4):
        ph.op("gpsimd", lambda e, q=q: e.dma_start(out=xT[:, 4 * q:4 * q + 4, :], in_=xsrc[:, 4 * q:4 * q + 4, :]),
              w=[R("xT", q)], dma=f"xT{q}")
    xT_res = [R("xT", q) for q in range(4)]

    dtf = ph.sb("dtf", [128, 8, 32])
    a_t = ph.sb("a", [128, 8, 32])
    acs = ph.sb("acs", [128, 8, 32])
    tot = ph.sb("tot", [128, 8, 32])
    dd = ph.sb("dd", [128, 8, 32])
    cdec = ph.sb("cdec", [128, 8, 32])
    ps_dt = [ph.ps(f"psdt{i}", [128, 512]) for i in range(2)]
    wdt, wdt_res = ph.wload(w_in[:, OFF_DT:OFF_DT + 32], 32)
    for tt in range(8):
        pb = ps_dt[tt % 2]
        pres = R("psdt", tt % 2)
        for k in range(16):
            ph.op("tensor", lambda e, k=k, tt=tt, pb=pb: e.matmul(pb[:, 0:32], xT[:, k, tt * 128:(tt + 1) * 128], wdt[:, k, 0:32],
                                                                  start=(k == 0), stop=(k == 15)),
                  r=[wdt_res] + xT_res, w=[pres], signal=(k == 15))
        dt_chain(ph, kb, "pf", 128, pb[:, 0:32], pres, dtf[:, tt, :], a_t[:, tt, :], ("dt", tt))
        ph.op("tensor", lambda e, tt=tt, pb=pb: e.matmul(pb[:, 32:64], c["trif"][:], a_t[:, tt, :], start=True, stop=True),
              r=[R(("dt", tt), "a")], w=[pres])
        ph.op("tensor", lambda e, tt=tt, pb=pb: e.matmul(pb[:, 64:96], c["onesf"][:], a_t[:, tt, :], start=True, stop=True),
              r=[R(("dt", tt), "a")], w=[pres])
        ph.op("vector", lambda e, tt=tt, pb=pb: e.tensor_copy(out=tot[:, tt, :], in_=pb[:, 64:96]), r=[pres], w=[R("tot", tt)])
        ph.op("vector", lambda e, tt=tt, pb=pb: e.tensor_tensor(out=acs[:, tt, :], in0=tot[:, tt, :], in1=pb[:, 32:64], op=ALU.subtract),
              r=[pres, R("tot", tt)], w=[R("acs", tt)])
        ph.op("scalar", lambda e, tt=tt: e.activation(out=dd[:, tt, :], in_=acs[:, tt, :], func=AF.Exp), r=[R("acs", tt)], w=[R("dd", tt)])
        ph.op("vector", lambda e, tt=tt: e.tensor_tensor(out=dd[:, tt, :], in0=dd[:, tt, :], in1=dtf[:, tt, :], op=ALU.mult),
              r=[R("dd", tt), R(("dt", tt), "dtf")], w=[R("dd", tt)])
        ph.op("scalar", lambda e, tt=tt: e.activation(out=cdec[:, tt, :], in_=tot[:, tt, :], func=AF.Exp), r=[R("tot", tt)], w=[R("cdec", tt)])

    if CUT == 1:
        ph.end()
        return
    raw_ = [ph.sb(f"raw{i}", [128, 3 + NOWN]) for i in range(2)]
    acc_ = [ph.sb(f"acc{i}", [128, NOWN]) for i in range(2)]
    ftc = [0]
    xc = ph.sb("xc", [128, 4, NOWN], BF16)
    bT = ph.sb("bT", [128, NOWN], BF16)
    xdec = [ph.sb(f"xdec{i}", [128, 512], BF16) for i in range(2)]
    btm = [ph.sb(f"btm{i}", [128, 128], BF16) for i in range(2)]
    hT = ph.sb("hT", [128, 512])
    ps_mm = [ph.ps(f"psmm{i}", [128, 512]) for i in range(2)]
    ps_tr = [ph.ps(f"pstr{i}", [128, 1024], BF16) for i in range(2)]
    ps_st = [ph.ps(f"psst{i}", [128, 512]) for i in range(2)]
    for i in range(2):
        ph.op("vector", lambda e, i=i: e.memset(raw_[i][:, 0:3], 0.0), w=[R("raw", i)])
    mmc = 0
    trc = 0

    def feat_tile(col0, ch_tile, out_bf, out_res):
        nonlocal mmc
        fi = ftc[0] % 2
        ftc[0] += 1
        raw, acc = raw_[fi], acc_[fi]
        wb, wres = ph.wload(w_in[:, col0:col0 + 128], 128)
        for nt in range(2):
            pb = ps_mm[mmc % 2]
            pres = R("psmm", mmc % 2)
            mmc += 1
            for k in range(16):
                ph.op("tensor", lambda e, k=k, nt=nt, pb=pb, wb=wb: e.matmul(pb[:, :], wb[:, k, 0:128], xT[:, k, nt * 512:(nt + 1) * 512],
                                                                           start=(k == 0), stop=(k == 15)),
                      r=[wres] + xT_res, w=[pres], signal=(k == 15))
            ph.op("scalar", lambda e, nt=nt, pb=pb: e.activation(out=raw[:, 3 + nt * 512:3 + (nt + 1) * 512], in_=pb[:, :], func=AF.Copy),
                  r=[pres], w=[R("raw", fi)])
        conv_silu(ph, kb, raw, R("raw", fi), NOWN, c["cwT"][:, ch_tile, :], c["cbT"][:, ch_tile:ch_tile + 1], out_bf, out_res,
                  acc[:, :], R("acc", fi), 4, 3)

    for g in range(4):
        for m in range(4):
            feat_tile(OFF_XBC + 512 * g + 128 * m, 4 * g + m, xc[:, m, :], R("xc", m))
        feat_tile(OFF_XBC + 2048 + 128 * g, 16 + g, bT[:, :], R("bT"))
        if g == 2:
            xsrc2 = din["xextT"].rearrange("(k p) t -> p k t", p=128)
            for q in range(4):
                ph.op("gpsimd", lambda e, q=q: e.dma_start(out=kb.xT[:, 4 * q:4 * q + 4, :], in_=xsrc2[:, 4 * q:4 * q + 4, :]),
                      w=[R("xTe", q)], dma=f"xTe{q}")
        if CUT == 2:
            break
        def pf_front(ck, par, g=g):
            tsl = slice(ck * 128, (ck + 1) * 128)
            pt, ptres = ps_tr[par], R("pstr", par)
            xd, xdres = xdec[par], R("xdec", par)
            bt, btres = btm[par], R("btm", par)
            for m in range(4):
                ph.op("tensor", lambda e, m=m: e.transpose(pt[:, m * 128:(m + 1) * 128], xc[:, m, tsl], c["identb"][:]),
                      r=[R("xc", m)], w=[ptres], signal=False)
            ph.op("tensor", lambda e: e.transpose(pt[:, 512:640], bT[:, tsl], c["identb"][:]), r=[R("bT")], w=[ptres])
            ph.op("vector", lambda e: e.tensor_tensor(
                out=xd[:].rearrange("p (h q) -> p h q", h=8), in0=pt[:, 0:512].rearrange("p (h q) -> p h q", h=8),
                in1=dd[:, ck, 8 * g:8 * g + 8].unsqueeze(2).to_broadcast([128, 8, 64]), op=ALU.mult),
                r=[ptres, R("dd", ck)], w=[xdres])
            ph.op("scalar", lambda e: e.activation(out=bt[:], in_=pt[:, 512:640], func=AF.Copy), r=[ptres], w=[btres, ptres])

        def pf_back(ck, par, g=g):
            xd, xdres = xdec[par], R("xdec", par)
            bt, btres = btm[par], R("btm", par)
            pst, pstres = ps_st[par], R("psst", par)
            ph.op("tensor", lambda e: e.matmul(pst[:, :], bt[:], xd[:], start=True, stop=True), r=[btres, xdres], w=[pstres])
            if ck == 0:
                ph.op("vector", lambda e: e.tensor_copy(out=hT[:], in_=pst[:, :]), r=[pstres], w=[R("hT")])
            else:
                ph.op("vector", lambda e: e.tensor_tensor(
                    out=hT[:].rearrange("p (h q) -> p h q", h=8), in0=hT[:].rearrange("p (h q) -> p h q", h=8),
                    in1=cdec[:, ck, 8 * g:8 * g + 8].unsqueeze(2).to_broadcast([128, 8, 64]), op=ALU.mult),
                    r=[R("hT"), R("cdec", ck)], w=[R("hT")])
                ph.op("vector", lambda e: e.tensor_tensor(out=hT[:], in0=hT[:], in1=pst[:, :], op=ALU.add),
                      r=[R("hT"), pstres], w=[R("hT")])

        pf_front(0, 0)
        for ck in range(8):
            if ck < 7:
                pf_front(ck + 1, (ck + 1) % 2)
            pf_back(ck, ck % 2)
        ph.op("vector", lambda e, g=g: e.tensor_scalar(out=hmid[:, 512 * g:512 * (g + 1)], in0=hT[:], scalar1=c["flag"][:, 0:1],
                                                       scalar2=None, op0=ALU.mult), r=[R("hT")], w=[R("hmid", g)])
    ph.end()


def phase_ssd(kb, din, dout, dbg, hmid, y_aT, cvc):
    nc = kb.nc
    c = kb.c
    ph = Phase(kb, "sd")
    R = ph.R
    w_in = din["w_in"]
    xT = kb.xT
    xT_res = []
    normw_bc = ph.sb("normw", [128, 2048])
    ph.op("sync", lambda e: e.dma_start(out=normw_bc[:], in_=din["normw"].partition_broadcast(128)), w=[R("normw")], dma="nw")
    cvT = ph.sb("cvT", [128, 24, NS, 3])
    ph.op("sync", lambda e: e.dma_start(out=cvT[:].rearrange("p k b j -> p k (b j)"),
                                        in_=din["cv0T"].rearrange("p (k f) -> p k f", k=24)), w=[R("cvT")], dma="cvT")

    banks = [ph.ps(f"bk{i}", [128, 512]) for i in (0, 1, 4, 5, 6, 7)]
    bk = {0: banks[0], 1: banks[1], 4: banks[2], 5: banks[3], 6: banks[4], 7: banks[5]}
    bkt = {2: ph.ps("bk2", [128, 1024], BF16), 3: ph.ps("bk3", [128, 1024], BF16)}
    BR = lambda i: R("bank", i)

    tiles = [(3 + 128 * t, 128) for t in range(8)] + [(3 + NOWN, NS)]

    dtf = ph.sb("dtf", [128, 9, 32])
    a_t = ph.sb("a", [128, 9, 32])
    acs = ph.sb("acs", [128, 9, 32])
    eacs = ph.sb("eacs", [128, 9, 32])
    dd = ph.sb("dd", [128, 9, 32])
    cdec = ph.sb("cdec", [128, 9, 32])
    tmp32 = ph.sb("tmp32", [128, 32])
    na_t = ph.sb("na", [128, 9, 32])
    wdt, wdt_res = ph.wload(w_in[:, OFF_DT:OFF_DT + 32], 32)
    for tt, (t0, n) in enumerate(tiles):
        pb = bk[tt % 2]
        pres = BR(tt % 2)
        for k in range(16):
            ph.op("tensor", lambda e, k=k, t0=t0, n=n, pb=pb: e.matmul(pb[0:n, 0:32], xT[:, k, t0:t0 + n], wdt[:, k, 0:32],
                                                                        start=(k == 0), stop=(k == 15)),
                  r=[wdt_res] + xT_res, w=[pres], signal=(k == 15))
        dt_chain(ph, kb, "sd", n, pb[0:n, 0:32], pres, dtf[0:n, tt, :], a_t[0:n, tt, :], ("dt", tt))
        if tt < 8:
            ph.op("tensor", lambda e, tt=tt, pb=pb: e.matmul(pb[:, 32:64], c["trif"][:], a_t[:, tt, :], start=True, stop=True),
                  r=[R(("dt", tt), "a")], w=[pres])
            ph.op("tensor", lambda e, tt=tt, pb=pb: e.matmul(pb[:, 64:96], c["onesf"][:], a_t[:, tt, :], start=True, stop=True),
                  r=[R(("dt", tt), "a")], w=[pres])
            ph.op("vector", lambda e, tt=tt: e.tensor_scalar(out=na_t[:, tt, :], in0=a_t[:, tt, :], scalar1=-1.0, scalar2=None, op0=ALU.mult),
                  r=[R(("dt", tt), "a")], w=[R("na", tt)])
            ph.op("vector", lambda e, tt=tt, pb=pb: e.tensor_copy(out=acs[:, tt, :], in_=pb[:, 32:64]), r=[pres], w=[R("acs", tt)])
            ph.op("scalar", lambda e, tt=tt: e.activation(out=eacs[:, tt, :], in_=acs[:, tt, :], func=AF.Exp), r=[R("acs", tt)], w=[R("eacs", tt)])
            ph.op("scalar", lambda e, tt=tt, pb=pb: e.activation(out=cdec[:, tt, :], in_=pb[:, 64:96], func=AF.Exp), r=[pres], w=[R("cdec", tt)])
            ph.op("vector", lambda e, tt=tt, pb=pb: e.tensor_tensor(out=tmp32[:], in0=pb[:, 64:96], in1=acs[:, tt, :], op=ALU.subtract),
                  r=[pres, R("acs", tt)], w=[R("tmp32")])
            ph.op("scalar", lambda e, tt=tt: e.activation(out=dd[:, tt, :], in_=tmp32[:], func=AF.Exp), r=[R("tmp32")], w=[R("dd", tt)])
            ph.op("vector", lambda e, tt=tt: e.tensor_tensor(out=dd[:, tt, :], in0=dd[:, tt, :], in1=dtf[:, tt, :], op=ALU.mult),
                  r=[R("dd", tt), R(("dt", tt), "dtf")], w=[R("dd", tt)])
        else:
            ph.op("scalar", lambda e: e.activation(out=cdec[0:NS, 8, :], in_=a_t[0:NS, 8, :], func=AF.Exp), r=[R(("dt", 8), "a")], w=[R("cdec", 8)])

    rawst = ph.sb("rawst", [128, 2 * NEXT])
    raw_ = [rawst[:, 0:NEXT], rawst[:, NEXT:2 * NEXT]]
    ftc = [0]
    ph.st2 = rawst[:, 0:2048].rearrange("p (b k n) -> p b k n", b=4, k=4)
    acc = ph.sb("acc", [128, NT])
    accs = ph.sb("accs", [128, NS])
    xc = ph.sb("xc", [128, 4, NT], BF16)
    bT = ph.sb("bT", [128, NT], BF16)
    cT = ph.sb("cT", [128, NT], BF16)
    zs = ph.sb("zs", [128, 9, 512], BF16)
    Xb = [ph.sb(f"X{i}", [128, 512], BF16) for i in range(2)]
    Xd = [ph.sb(f"Xd{i}", [128, 512], BF16) for i in range(2)]
    ysk = [ph.sb(f"ysk{i}", [128, 512]) for i in range(2)]
    btm = [ph.sb(f"btm{i}", [128, 128], BF16) for i in range(2)]
    cbm_ = [ph.sb(f"cbm{i}", [128, 128]) for i in range(2)]
    Ebuf_ = [ph.sb(f"E{i}", [128, 8, 128]) for i in range(2)]
    MT_ = [ph.sb(f"MT{i}", [128, 8, 128], BF16) for i in range(2)]
    yt_ = [ph.sb(f"yt{i}", [128, 512]) for i in range(2)]
    junk = ph.sb("junk", [128, 512], BF16)
    ssum_ = [ph.sb(f"ssum{i}", [128, 1]) for i in range(2)]
    yn_ = [ph.sb(f"yn{i}", [128, 512], BF16) for i in range(2)]
    gn_cnt = [0]
    hT = ph.sb("hT", [128, 512])
    hTb = ph.sb("hTb", [128, 512], BF16)
    nsplit = split_even(NEXT, 512)
    mmc = 0

    def feat_tile(col0, ch_tile, out_bf, out_res):
        nonlocal mmc
        fi = ftc[0] % 2
        ftc[0] += 1
        raw = raw_[fi]
        rraw = R("raw", fi)
        wb, wres = ph.wload(w_in[:, col0:col0 + 128], 128)
        for (s0, sz) in nsplit:
            pb = bk[mmc % 2]
            pres = BR(mmc % 2)
            mmc += 1
            for k in range(16):
                ph.op("tensor", lambda e, k=k, s0=s0, sz=sz, pb=pb, wb=wb: e.matmul(pb[:, 0:sz], wb[:, k, 0:128], xT[:, k, s0:s0 + sz],
                                                                                  start=(k == 0), stop=(k == 15)),
                      r=[wres] + xT_res, w=[pres], signal=(k == 15))
            ph.op("scalar", lambda e, s0=s0, sz=sz, pb=pb: e.activation(out=raw[:, s0:s0 + sz], in_=pb[:, 0:sz], func=AF.Copy),
                  r=[pres], w=[rraw])
        ph.op("vector", lambda e: e.tensor_copy(out=cvc[:, ch_tile, :], in_=raw[:, NOWN:NEXT]), r=[rraw], w=[R("cvc", ch_tile)])
        conv_silu(ph, kb, raw, rraw, NT, c["cwT"][:, ch_tile, :], c["cbT"][:, ch_tile:ch_tile + 1], out_bf[:, 0:NT], out_res,
                  acc[:, :], R("acc"), 4, 3)
        wc = c["cwT"][:, ch_tile, :]
        ph.op("vector", lambda e: e.tensor_scalar(out=accs[:], in0=raw[:, 3 + NOWN:NEXT], scalar1=wc[:, 3:4], scalar2=None, op0=ALU.mult),
              r=[rraw], w=[R("accs")])
        for j in range(3):
            ph.op("vector", lambda e, j=j: e.scalar_tensor_tensor(out=accs[:], in0=cvT[:, ch_tile, :, j], scalar=wc[:, j:j + 1], in1=accs[:],
                                                               op0=ALU.mult, op1=ALU.add), r=[R("cvT"), R("accs")], w=[R("accs")])
        ph.op("scalar", lambda e: e.activation(out=out_bf[:, NOWN:NT], in_=accs[:], func=AF.Silu, bias=c["cbT"][:, ch_tile:ch_tile + 1], scale=1.0),
              r=[R("accs"), out_res], w=[out_res])

    def gate_norm(n, g, tt, y_ap, y_res, tok_lo):
        gi = gn_cnt[0] % 2
        gn_cnt[0] += 1
        ssum, yn = ssum_[gi], yn_[gi]
        rs, ry = R("ssum", gi), R("yn", gi)
        ph.op("vector", lambda e: e.tensor_tensor(out=y_ap, in0=y_ap, in1=zs[0:n, tt, :], op=ALU.mult), r=[y_res, R("zs", tt)], w=[y_res])
        ph.op("scalar", lambda e: e.activation(out=junk[0:n, :], in_=y_ap, func=AF.Square, accum_out=ssum[0:n, :]), r=[y_res], w=[rs])
        ph.op("vector", lambda e: e.tensor_scalar(out=ssum[0:n, :], in0=ssum[0:n, :], scalar1=1.0 / 512, scalar2=RMS_EPS, op0=ALU.mult, op1=ALU.add),
              r=[rs], w=[rs])
        ph.op("scalar", lambda e: e.activation(out=ssum[0:n, :], in_=ssum[0:n, :], func=AF.Sqrt), r=[rs], w=[rs])
        ph.op("vector", lambda e: e.reciprocal(out=ssum[0:n, :], in_=ssum[0:n, :]), r=[rs], w=[rs])
        ph.op("vector", lambda e: e.scalar_tensor_tensor(out=yn[0:n, :], in0=y_ap, scalar=ssum[0:n, 0:1], in1=normw_bc[0:n, 512 * g:512 * (g + 1)],
                                                         op0=ALU.mult, op1=ALU.mult), r=[y_res, rs, R("normw")], w=[ry])
        pt = bkt[3]
        for m in range(4):
            ph.op("tensor", lambda e, m=m: e.transpose(pt[:, m * 128:m * 128 + n], yn[0:n, m * 128:(m + 1) * 128], c["identb"][0:n, 0:n]),
                  r=[ry], w=[BR(3)], signal=(m == 3))
        ph.op("scalar", lambda e: e.activation(out=y_aT[:, 4 * g:4 * g + 4, tok_lo:tok_lo + n],
                                               in_=pt[:, 0:512].rearrange("p (m t) -> p m t", m=4)[:, :, 0:n], func=AF.Copy),
              r=[BR(3)], w=[R("yaT", g, tt)])

    for g in range(4):
        for m in range(4):
            feat_tile(OFF_XBC + 512 * g + 128 * m, 4 * g + m, xc[:, m, :], R("xc", m))
        feat_tile(OFF_XBC + 2048 + 128 * g, 16 + g, bT, R("bT"))
        feat_tile(OFF_XBC + 2560 + 128 * g, 20 + g, cT, R("cT"))
        wz = [ph.wload(w_in[:, OFF_Z + 512 * g + 256 * j:OFF_Z + 512 * g + 256 * (j + 1)], 256) for j in range(2)]
        for tt, (t0, n) in enumerate(tiles):
            pb = bk[mmc % 2]
            pres = BR(mmc % 2)
            mmc += 1
            for j in range(2):
                for k in range(16):
                    ph.op("tensor", lambda e, k=k, j=j, t0=t0, n=n, pb=pb: e.matmul(pb[0:n, 256 * j:256 * (j + 1)], xT[:, k, t0:t0 + n], wz[j][0][:, k, 0:256],
                                                                                   start=(k == 0), stop=(k == 15)),
                          r=[wz[j][1]] + xT_res, w=[pres], signal=(k == 15 and j == 1))
            ph.op("scalar", lambda e, tt=tt, n=n, pb=pb: e.activation(out=zs[0:n, tt, :], in_=pb[0:n, :], func=AF.Silu), r=[pres], w=[R("zs", tt)])
        ph.op("vector", lambda e, g=g: e.tensor_copy(out=hT[:], in_=hmid[:, 512 * g:512 * (g + 1)]), w=[R("hT")])
        ph.op("scalar", lambda e: e.activation(out=hTb[:], in_=hT[:], func=AF.Copy), r=[R("hT")], w=[R("hTb")])
        def front(ck, g=g):
            tsl = slice(ck * 128, (ck + 1) * 128)
            i2 = ck % 2
            pt = bkt[2]
            cbm, Ebuf, MT = cbm_[i2], Ebuf_[i2], MT_[i2]
            rcbm, rE, rMT = R("cbm", i2), R("E", i2), R("MT", i2)
            for m in range(4):
                ph.op("tensor", lambda e, m=m: e.transpose(pt[:, m * 128:(m + 1) * 128], xc[:, m, tsl], c["identb"][:]),
                      r=[R("xc", m)], w=[BR(2)], signal=False)
            ph.op("tensor", lambda e: e.transpose(pt[:, 512:640], bT[:, tsl], c["identb"][:]), r=[R("bT")], w=[BR(2)])
            ptx = pt[:, 0:512].rearrange("p (h q) -> p h q", h=8)

            def bc(t):
                return t[:, ck, 8 * g:8 * g + 8].unsqueeze(2).to_broadcast([128, 8, 64])
            ph.op("vector", lambda e: e.tensor_tensor(out=Xb[i2][:].rearrange("p (h q) -> p h q", h=8), in0=ptx, in1=bc(dtf), op=ALU.mult),
                  r=[BR(2), R(("dt", ck), "dtf")], w=[R("X", i2)])
            ph.op("vector", lambda e: e.tensor_tensor(out=Xd[i2][:].rearrange("p (h q) -> p h q", h=8), in0=ptx, in1=bc(dd), op=ALU.mult),
                  r=[BR(2), R("dd", ck)], w=[R("Xd", i2)])
            ph.op("vector", lambda e: e.tensor_tensor(out=ysk[i2][:].rearrange("p (h q) -> p h q", h=8), in0=ptx,
                                                      in1=c["D_bc"][:, 8 * g:8 * g + 8].unsqueeze(2).to_broadcast([128, 8, 64]), op=ALU.mult),
                  r=[BR(2)], w=[R("ysk", i2)])
            ph.op("scalar", lambda e: e.activation(out=btm[i2][:], in_=pt[:, 512:640], func=AF.Copy), r=[BR(2)], w=[R("btm", i2)])
            ph.op("tensor", lambda e: e.matmul(bk[1][:, 0:128], bT[:, tsl], cT[:, tsl], start=True, stop=True),
                  r=[R("bT"), R("cT")], w=[BR(1)])
            ph.op("vector", lambda e: e.tensor_tensor(out=cbm[:], in0=bk[1][:, 0:128], in1=c["trif"][:], op=ALU.mult), r=[BR(1)], w=[rcbm])
            for hh in range(8):
                b_i = 4 + hh // 4
                reg = bk[b_i][:, (hh % 4) * 128:(hh % 4 + 1) * 128]
                col = 8 * g + hh
                ph.op("tensor", lambda e, reg=reg, col=col: e.matmul(reg, a_t[:, ck, col:col + 1].to_broadcast([128, 128]), c["trif"][:],
                                                                     start=True, stop=False),
                      r=[R(("dt", ck), "a")], w=[BR(b_i)], signal=False)
                ph.op("tensor", lambda e, reg=reg, col=col: e.matmul(reg, c["trif"][:], na_t[:, ck, col:col + 1].to_broadcast([128, 128]),
                                                                     start=False, stop=False),
                      r=[R("na", ck)], w=[BR(b_i)], signal=False)
                ph.op("tensor", lambda e, reg=reg: e.matmul(reg, c["identf"][:], c["negm"][:], start=False, stop=True),
                      w=[BR(b_i)], signal=(hh % 4 == 3))
            for q in range(2):
                ph.op("scalar", lambda e, q=q: e.activation(out=Ebuf[:, 4 * q:4 * q + 4, :].rearrange("p h l -> p (h l)"), in_=bk[4 + q][:, :], func=AF.Exp),
                      r=[BR(4 + q)], w=[rE])
            ph.op("vector", lambda e: e.tensor_tensor(out=MT[:], in0=Ebuf[:], in1=cbm[:].unsqueeze(1).to_broadcast([128, 8, 128]), op=ALU.mult),
                  r=[rE, rcbm], w=[rMT])

        def back(ck, g=g):
            tsl = slice(ck * 128, (ck + 1) * 128)
            i2 = ck % 2
            MT, yt = MT_[i2], yt_[i2]
            rMT, ryt = R("MT", i2), R("yt", i2)

            def bc(t):
                return t[:, ck, 8 * g:8 * g + 8].unsqueeze(2).to_broadcast([128, 8, 64])
            for hh in range(8):
                ph.op("tensor", lambda e, hh=hh: e.matmul(bk[6][:, hh * 64:(hh + 1) * 64], MT[:, hh, :], Xb[i2][:, hh * 64:(hh + 1) * 64],
                                                          start=True, stop=True), r=[rMT, R("X", i2)], w=[BR(6)], signal=(hh == 7))
            ph.op("tensor", lambda e: e.matmul(bk[7][:, :], cT[:, tsl], hTb[:], start=True, stop=True), r=[R("cT"), R("hTb")], w=[BR(7)])
            ph.op("vector", lambda e: e.tensor_tensor(out=yt[:].rearrange("p (h q) -> p h q", h=8), in0=bk[7][:, :].rearrange("p (h q) -> p h q", h=8),
                                                      in1=bc(eacs), op=ALU.mult), r=[BR(7), R("eacs", ck)], w=[ryt])
            ph.op("vector", lambda e: e.tensor_tensor(out=yt[:], in0=yt[:], in1=bk[6][:, :], op=ALU.add), r=[ryt, BR(6)], w=[ryt])
            ph.op("vector", lambda e: e.tensor_tensor(out=yt[:], in0=yt[:], in1=ysk[i2][:], op=ALU.add), r=[ryt, R("ysk", i2)], w=[ryt])
            gate_norm(128, g, ck, yt[:, :], ryt, ck * 128)
            ph.op("tensor", lambda e: e.matmul(bk[0][:, :], btm[i2][:], Xd[i2][:], start=True, stop=True), r=[R("btm", i2), R("Xd", i2)], w=[BR(0)])
            ph.op("vector", lambda e: e.tensor_tensor(out=hT[:].rearrange("p (h q) -> p h q", h=8), in0=hT[:].rearrange("p (h q) -> p h q", h=8),
                                                      in1=bc(cdec), op=ALU.mult), r=[R("hT"), R("cdec", ck)], w=[R("hT")])
            ph.op("vector", lambda e: e.tensor_tensor(out=hT[:], in0=hT[:], in1=bk[0][:, :], op=ALU.add), r=[R("hT"), BR(0)], w=[R("hT")])
            ph.op("scalar", lambda e: e.activation(out=hTb[:], in_=hT[:], func=AF.Copy), r=[R("hT")], w=[R("hTb")])

        front(0)
        for ck in range(8):
            if ck < 7:
                front(ck + 1)
            back(ck)
        for m in range(4):
            ph.op("tensor", lambda e, m=m: e.transpose(bk[6][:, m * 128:(m + 1) * 128], hT[:, m * 128:(m + 1) * 128], c["identf"][:]),
                  r=[R("hT")], w=[BR(6)], signal=(m == 3))
        hfs = yt_[1]
        ph.op("vector", lambda e: e.tensor_copy(out=hfs[:], in_=bk[6][:, :]), r=[BR(6)], w=[R("yt", 1)])
        ph.op("sync", lambda e, g=g: e.dma_start(out=dout["hfin"][512 * g:512 * (g + 1), :].rearrange("(m p) n -> p m n", p=128),
                                                 in_=hfs[:].rearrange("p (m n) -> p m n", m=4)),
              r=[R("yt", 1)], w=[R("yt", 1)], dma="hfst")
        sample_ssd(ph, kb, din, dout, g, xc, bT, cT, dtf, cdec, bk, bkt, BR, gate_norm)
    ph.end()


def sample_ssd(ph, kb, din, dout, g, xc, bT, cT, dtf, cdec, bk, bkt, BR, gate_norm):
    c = kb.c
    R = ph.R
    S = slice(NOWN, NT)
    if g == 0:
        ph.sXs = ph.sb("sXs", [NS, 512])
        ph.sdA = ph.sb("sdA", [NS, 512])
        ph.sysk = ph.sb("sysk", [NS, 512])
        ph.sB = ph.sb("sB", [NS, 128], BF16)
        ph.sC = ph.sb("sC", [NS, 128], BF16)
        ph.sXT = ph.sb("sXT", [128, 4, NS])
        ph.sdAT = ph.sb("sdAT", [128, 4, NS])
        ph.sbd = ph.sb("sbd", [NS, 2, 4, 128], BF16)
        ph.sh0 = ph.sb("sh0", [128, 4, 4, 128])
        ph.syT = ph.sb("syT", [128, 4, NS])
        ph.sy = ph.sdA
    sXs, sdA, sysk, sB, sC, sXT, sdAT, sbd, sh0, st2, syT, sy = (ph.sXs, ph.sdA, ph.sysk, ph.sB, ph.sC, ph.sXT, ph.sdAT,
                                                                ph.sbd, ph.sh0, ph.st2, ph.syT, ph.sy)
    pt = bkt[2]
    for m in range(4):
        ph.op("tensor", lambda e, m=m: e.transpose(pt[0:NS, m * 128:(m + 1) * 128], xc[:, m, S], c["identb"][:]), r=[R("xc", m)], w=[BR(2)], signal=False)
    ph.op("tensor", lambda e: e.transpose(pt[0:NS, 512:640], bT[:, S], c["identb"][:]), r=[R("bT")], w=[BR(2)], signal=False)
    ph.op("tensor", lambda e: e.transpose(pt[0:NS, 640:768], cT[:, S], c["identb"][:]), r=[R("cT")], w=[BR(2)])
    ptx = pt[0:NS, 0:512].rearrange("p (h q) -> p h q", h=8)

    def bc(t):
        return t[0:NS, 8, 8 * g:8 * g + 8].unsqueeze(2).to_broadcast([NS, 8, 64])
    ph.op("vector", lambda e: e.tensor_tensor(out=sXs[:].rearrange("p (h q) -> p h q", h=8), in0=ptx, in1=bc(dtf), op=ALU.mult),
          r=[BR(2), R(("dt", 8), "dtf")], w=[R("sXs")])
    ph.op("vector", lambda e: e.tensor_tensor(out=sysk[:].rearrange("p (h q) -> p h q", h=8), in0=ptx,
                                              in1=c["D_bc"][0:NS, 8 * g:8 * g + 8].unsqueeze(2).to_broadcast([NS, 8, 64]), op=ALU.mult),
          r=[BR(2)], w=[R("sysk")])
    ph.op("vector", lambda e: e.tensor_copy(out=sdA[:].rearrange("p (h q) -> p h q", h=8), in_=bc(cdec)), r=[R("cdec", 8)], w=[R("sdA")])
    ph.op("scalar", lambda e: e.activation(out=sB[:], in_=pt[0:NS, 512:640], func=AF.Copy), r=[BR(2)], w=[R("sB")])
    ph.op("scalar", lambda e: e.activation(out=sC[:], in_=pt[0:NS, 640:768], func=AF.Copy), r=[BR(2)], w=[R("sC")])
    for m in range(4):
        ph.op("tensor", lambda e, m=m: e.transpose(bk[4][:, m * NS:(m + 1) * NS], sXs[:, m * 128:(m + 1) * 128], c["identf"][0:NS, 0:NS]),
              r=[R("sXs")], w=[BR(4)], signal=False)
    for m in range(4):
        ph.op("tensor", lambda e, m=m: e.transpose(bk[4][:, 64 + m * NS:64 + (m + 1) * NS], sdA[:, m * 128:(m + 1) * 128], c["identf"][0:NS, 0:NS]),
              r=[R("sdA")], w=[BR(4)], signal=(m == 3))
    ph.op("vector", lambda e: e.tensor_copy(out=sXT[:].rearrange("p m b -> p (m b)"), in_=bk[4][:, 0:64]), r=[BR(4)], w=[R("sXT")])
    ph.op("vector", lambda e: e.tensor_copy(out=sdAT[:].rearrange("p m b -> p (m b)"), in_=bk[4][:, 64:128]), r=[BR(4)], w=[R("sdAT")])
    for hb in range(4):
        b0 = hb * 4
        for which, src in ((0, sB), (1, sC)):
            ph.op("vector", lambda e, which=which, src=src, b0=b0: e.tensor_tensor(
                out=sbd[:, which, :, :], in0=src[:].unsqueeze(1).to_broadcast([NS, 4, 128]),
                in1=c["identf"][0:NS, b0:b0 + 4].unsqueeze(2).to_broadcast([NS, 4, 128]), op=ALU.mult),
                r=[R("sB" if which == 0 else "sC")], w=[R("sbd", which)])
        for which, bi in ((0, 4), (1, 6)):
            ph.op("tensor", lambda e, which=which, bi=bi: e.matmul(bk[bi][:, :], c["onesb"][0:NS, :],
                                                                 sbd[:, which, :, :].rearrange("p b n -> p (b n)"),
                                                                 start=True, stop=True), r=[R("sbd", which)], w=[BR(bi)])
        for bb in range(4):
            o_ = ph.op("sync", lambda e, b0=b0, bb=bb: e.dma_start(out=sh0[:, bb, :, :], in_=din["ssm0"][b0 + bb, 512 * g:512 * (g + 1), :].rearrange("(k p) n -> p k n", p=128)),
                       w=([R("sh0")] if bb == 0 else []), dma="sh0")
        R("sh0").lw = o_

        def v84(t, b0=b0):
            return t[:, :, b0:b0 + 4].rearrange("p k b -> p b k").unsqueeze(3).to_broadcast([128, 4, 4, 128])

        def pbc(bi):
            return bk[bi][:, :].rearrange("p (b n) -> p b n", b=4).unsqueeze(2).to_broadcast([128, 4, 4, 128])
        ph.op("gpsimd", lambda e, v84=v84: e.tensor_tensor(out=sh0[:], in0=sh0[:], in1=v84(sdAT), op=ALU.mult), r=[R("sh0"), R("sdAT")], w=[R("sh0")])
        ph.op("vector", lambda e, v84=v84, pbc=pbc: e.tensor_tensor(out=st2[:], in0=pbc(4), in1=v84(sXT), op=ALU.mult),
              r=[BR(4), R("sXT")], w=[R("raw", 0), R("raw", 1)])
        ph.op("gpsimd", lambda e: e.tensor_tensor(out=sh0[:], in0=sh0[:], in1=st2[:], op=ALU.add), r=[R("sh0"), R("raw", 0), R("raw", 1)], w=[R("sh0")])
        for bb in range(4):
            ph.op("sync", lambda e, b0=b0, bb=bb: e.dma_start(out=dout["hs"][b0 + bb, 512 * g:512 * (g + 1), :].rearrange("(k p) n -> p k n", p=128), in_=sh0[:, bb, :, :]),
                  r=[R("sh0")], dma="hsst")
        ph.op("vector", lambda e, pbc=pbc: e.tensor_tensor(out=st2[:], in0=pbc(6), in1=sh0[:], op=ALU.mult),
              r=[BR(6), R("sh0")], w=[R("raw", 0), R("raw", 1)])
        ph.op("vector", lambda e, b0=b0: e.reduce_sum(out=syT[:, :, b0:b0 + 4].rearrange("p k b -> p b k"), in_=st2[:], axis=AX.X),
              r=[R("raw", 0), R("raw", 1)], w=[R("syT")])
    for m in range(4):
        ph.op("tensor", lambda e, m=m: e.transpose(bk[0][0:NS, m * 128:(m + 1) * 128], syT[:, m, :], c["identf"][:]), r=[R("syT")], w=[BR(0)], signal=(m == 3))
    ph.op("vector", lambda e: e.tensor_tensor(out=sy[:], in0=bk[0][0:NS, :], in1=sysk[:], op=ALU.add), r=[BR(0), R("sysk")], w=[R("sdA")])
    gate_norm(NS, g, 8, sy[:, :], R("sdA"), NOWN)


def dbg_dump(kb, dst, src):
    ph = Phase(kb, "dbg%d" % Prog.UID)
    for k in range(16):
        ph.op("sync", lambda e, k=k: e.dma_start(out=dst[k * 128:(k + 1) * 128, :], in_=src[:, k, :]), dma="st")
    ph.end()


def load_xT(ph, din):
    return ph.kb.xT, []


def mm16(ph, out_ap, out_res, wb, wres, ncols, rhs_fn, rhs_res):
    for k in range(16):
        ph.op("tensor", lambda e, k=k: e.matmul(out_ap, wb[:, k, 0:ncols], rhs_fn(k), start=(k == 0), stop=(k == 15)),
              r=[wres] + list(rhs_res), w=[out_res], signal=(k == 15))


def phase_sc(kb, din, y_bT, ucc):
    c = kb.c
    ph = Phase(kb, "sc")
    R = ph.R
    w_in = din["w_in"]
    xT, xT_res = load_xT(ph, din)
    scT = ph.sb("scT", [128, 16, NS, 2])
    ph.op("sync", lambda e: e.dma_start(out=scT[:].rearrange("p k b j -> p k (b j)"), in_=din["sc0T"].rearrange("p (k f) -> p k f", k=16)),
          w=[R("scT")], dma="scT")
    craw = ph.sb("craw", [128, NEXT])
    ubuf = ph.sb("ubuf", [128, NEXT])
    acc = ph.sb("acc", [128, NT])
    banks = [ph.ps(f"bk{i}", [128, 512]) for i in range(4)]
    bc_ = 0
    ns_ext = split_even(NEXT, 512)
    ns_own = split_even(NT, 512)
    for i in range(16):
        wc, wc_r = ph.wload(w_in[:, OFF_SCC + 128 * i:OFF_SCC + 128 * (i + 1)], 128)
        wh, wh_r = ph.wload(w_in[:, OFF_SCH + 128 * i:OFF_SCH + 128 * (i + 1)], 128)
        for (s0, sz) in ns_ext:
            pb, pr = banks[bc_ % 4], R("bank", bc_ % 4)
            bc_ += 1
            mm16(ph, pb[:, 0:sz], pr, wc, wc_r, 128, lambda k, s0=s0, sz=sz: xT[:, k, s0:s0 + sz], xT_res)
            ph.op("scalar", lambda e: e.activation(out=craw[:, s0:s0 + sz], in_=pb[:, 0:sz], func=AF.Copy), r=[pr], w=[R("craw")])
        for (s0, sz) in ns_ext:
            pb, pr = banks[bc_ % 4], R("bank", bc_ % 4)
            bc_ += 1
            mm16(ph, pb[:, 0:sz], pr, wh, wh_r, 128, lambda k, s0=s0, sz=sz: xT[:, k, s0:s0 + sz], xT_res)
            ph.op("vector", lambda e: e.tensor_tensor(out=ubuf[:, s0:s0 + sz], in0=craw[:, s0:s0 + sz], in1=pb[:, 0:sz], op=ALU.mult),
                  r=[pr, R("craw")], w=[R("ubuf")])
        wb_, wb_r = ph.wload(w_in[:, OFF_SCB + 128 * i:OFF_SCB + 128 * (i + 1)], 128)
        ph.op("scalar", lambda e: e.activation(out=ucc[:, i, :], in_=ubuf[:, NOWN + 1:NEXT], func=AF.Copy), r=[R("ubuf")], w=[R("ucc", i)])
        conv_silu(ph, kb, ubuf, R("ubuf"), NT, c["swT"][:, i, :], None, None, None, acc[:, :], R("acc"), 3, 3)
        wsc = c["swT"][:, i, :]
        ph.op("vector", lambda e: e.tensor_scalar(out=acc[:, NOWN:NT], in0=ubuf[:, 3 + NOWN:NEXT], scalar1=wsc[:, 2:3], scalar2=None, op0=ALU.mult),
              r=[R("ubuf"), R("acc")], w=[R("acc")])
        for j in range(2):
            ph.op("vector", lambda e, j=j: e.scalar_tensor_tensor(out=acc[:, NOWN:NT], in0=scT[:, i, :, j], scalar=wsc[:, j:j + 1], in1=acc[:, NOWN:NT],
                                                               op0=ALU.mult, op1=ALU.add), r=[R("scT"), R("acc")], w=[R("acc")])
        for (s0, sz) in ns_own:
            pb, pr = banks[bc_ % 4], R("bank", bc_ % 4)
            bc_ += 1
            mm16(ph, pb[:, 0:sz], pr, wb_, wb_r, 128, lambda k, s0=s0, sz=sz: xT[:, k, 3 + s0:3 + s0 + sz], xT_res)
            ph.op("vector", lambda e: e.tensor_tensor(out=y_bT[:, i, s0:s0 + sz], in0=acc[:, s0:s0 + sz], in1=pb[:, 0:sz], op=ALU.mult),
                  r=[pr, R("acc")], w=[R("ybT", i)])
    ph.end()


def phase_mix(kb, din, y_aT, y_bT, mixT):
    ph = Phase(kb, "mx")
    R = ph.R
    w_in = din["w_in"]
    xT, xT_res = load_xT(ph, din)
    sga = ph.sb("sga", [128, NT])
    sgb = ph.sb("sgb", [128, NT])
    mtmp = ph.sb("mtmp", [128, NT])
    mout = [ph.sb(f"mout{i}", [128, NT], BF16) for i in range(2)]
    banks = [ph.ps(f"bk{i}", [128, 512]) for i in range(4)]
    bc_ = 0
    ns_own = split_even(NT, 512)
    for i in range(16):
        cs = slice(128 * i, 128 * (i + 1))
        wga, wga_r = ph.wload(w_in[:, OFF_GA + 128 * i:OFF_GA + 128 * (i + 1)], 128)
        wgb, wgb_r = ph.wload(w_in[:, OFF_GB + 128 * i:OFF_GB + 128 * (i + 1)], 128)
        for (wt, wr, dst, dres) in ((wga, wga_r, sga, "sga"), (wgb, wgb_r, sgb, "sgb")):
            for (s0, sz) in ns_own:
                pb, pr = banks[bc_ % 4], R("bank", bc_ % 4)
                bc_ += 1
                mm16(ph, pb[:, 0:sz], pr, wt, wr, 128, lambda k, s0=s0, sz=sz: xT[:, k, 3 + s0:3 + s0 + sz], xT_res)
                ph.op("scalar", lambda e: e.activation(out=dst[:, s0:s0 + sz], in_=pb[:, 0:sz], func=AF.Sigmoid), r=[pr], w=[R(dres)])
        wa, wa_r = ph.wload(din["w_bssd"][:, cs], 128)
        for (s0, sz) in ns_own:
            pb, pr = banks[bc_ % 4], R("bank", bc_ % 4)
            bc_ += 1
            mm16(ph, pb[:, 0:sz], pr, wa, wa_r, 128, lambda k, s0=s0, sz=sz: y_aT[:, k, s0:s0 + sz], [])
            ph.op("vector", lambda e: e.tensor_tensor(out=mtmp[:, s0:s0 + sz], in0=sga[:, s0:s0 + sz], in1=pb[:, 0:sz], op=ALU.mult),
                  r=[pr, R("sga")], w=[R("mtmp")])
        wb_, wb_r = ph.wload(din["w_bsc"][:, cs], 128)
        for (s0, sz) in ns_own:
            pb, pr = banks[bc_ % 4], R("bank", bc_ % 4)
            bc_ += 1
            mm16(ph, pb[:, 0:sz], pr, wb_, wb_r, 128, lambda k, s0=s0, sz=sz: y_bT[:, k, s0:s0 + sz], [])
            ph.op("vector", lambda e: e.tensor_tensor(out=sgb[:, s0:s0 + sz], in0=sgb[:, s0:s0 + sz], in1=pb[:, 0:sz], op=ALU.mult),
                  r=[pr, R("sgb")], w=[R("sgb")])
        mo, mr = mout[i % 2], R("mout", i % 2)
        ph.op("vector", lambda e: e.tensor_tensor(out=mo[:], in0=mtmp[:], in1=sgb[:], op=ALU.add), r=[R("mtmp"), R("sgb")], w=[mr])
        ph.op("sync", lambda e: e.dma_start(out=mixT[128 * i:128 * (i + 1), :], in_=mo[:]), r=[mr], dma=f"mo{i % 2}")
    ph.end()


def ln_feature_major(ph, kb, pre, pre_res_fn, S1b, S2b, tbuf, tres, emit_fn, tag):
    R = ph.R
    ns = split_even(NT, 512)
    mean = ph.sb(tag + "mean", [128, NT])
    rstd = ph.sb(tag + "rstd", [128, NT])
    for j, (s0, sz) in enumerate(ns):
        ph.op("vector", lambda e: e.tensor_scalar(out=mean[:, s0:s0 + sz], in0=S1b[j][0][:, 0:sz], scalar1=1.0 / D, scalar2=None, op0=ALU.mult),
              r=[S1b[j][1]], w=[R(tag + "mean")])
        ph.op("vector", lambda e: e.tensor_tensor(out=rstd[:, s0:s0 + sz], in0=mean[:, s0:s0 + sz], in1=mean[:, s0:s0 + sz], op=ALU.mult),
              r=[R(tag + "mean")], w=[R(tag + "rstd")])
        ph.op("vector", lambda e: e.scalar_tensor_tensor(out=rstd[:, s0:s0 + sz], in0=S2b[j][0][:, 0:sz], scalar=1.0 / D, in1=rstd[:, s0:s0 + sz],
                                                         op0=ALU.mult, op1=ALU.subtract), r=[S2b[j][1], R(tag + "rstd")], w=[R(tag + "rstd")])
    ph.op("vector", lambda e: e.tensor_scalar(out=rstd[:], in0=rstd[:], scalar1=LN_EPS, scalar2=None, op0=ALU.add), r=[R(tag + "rstd")], w=[R(tag + "rstd")])
    ph.op("scalar", lambda e: e.activation(out=rstd[:], in_=rstd[:], func=AF.Sqrt), r=[R(tag + "rstd")], w=[R(tag + "rstd")])
    ph.op("vector", lambda e: e.reciprocal(out=rstd[:], in_=rstd[:]), r=[R(tag + "rstd")], w=[R(tag + "rstd")])
    for i in range(16):
        t = tbuf[i % 2]
        tr = tres[i % 2]
        ph.op("vector", lambda e: e.tensor_tensor(out=t[:], in0=pre[:, i, :], in1=mean[:], op=ALU.subtract), r=[pre_res_fn(i), R(tag + "mean")], w=[tr])
        ph.op("vector", lambda e: e.tensor_tensor(out=t[:], in0=t[:], in1=rstd[:], op=ALU.mult), r=[tr, R(tag + "rstd")], w=[tr])
        emit_fn(i, t, tr)


def ln_stats_accum(ph, kb, i, src_ap_fn, src_res, sq, sq_res, S1b, S2b):
    c = kb.c
    ns = split_even(NT, 512)
    ph.op("scalar", lambda e: e.activation(out=sq[:], in_=src_ap_fn(0, NT), func=AF.Square), r=[src_res], w=[sq_res])
    for j, (s0, sz) in enumerate(ns):
        ph.op("tensor", lambda e: e.matmul(S1b[j][0][:, 0:sz], c["onesf"][:], src_ap_fn(s0, sz), start=(i == 0), stop=(i == 15)),
              r=[src_res], w=[S1b[j][1]], signal=(i == 15))
        ph.op("tensor", lambda e: e.matmul(S2b[j][0][:, 0:sz], c["onesf"][:], sq[:, s0:s0 + sz], start=(i == 0), stop=(i == 15)),
              r=[sq_res], w=[S2b[j][1]], signal=True)


def phase_ln1(kb, din, mixs, x1T, x1s, pre):
    ph = Phase(kb, "l1")
    R = ph.R
    mixT = ph.sb("mixT", [128, 16, NT], BF16)
    for q in range(4):
        ph.op("sync", lambda e, q=q: e.dma_start(out=mixT[:, 4 * q:4 * q + 4, :], in_=mixs.rearrange("(k p) t -> p k t", p=128)[:, 4 * q:4 * q + 4, :]),
              w=[R("mixT", q)], dma=f"mx{q}")
    mix_res = [R("mixT", q) for q in range(4)]
    xf = [ph.sb(f"xf{i}", [128, NT]) for i in range(2)]
    sq = [ph.sb(f"sq{i}", [128, NT]) for i in range(2)]
    gb = ph.sb("gb", [128, 4, 16])
    ph.op("sync", lambda e: e.dma_start(out=gb[:, 0, :], in_=din["ln1gT"][:, :]), w=[R("gb")], dma="g0")
    ph.op("sync", lambda e: e.dma_start(out=gb[:, 1, :], in_=din["ln1bT"][:, :]), w=[R("gb")], dma="g1")
    ph.op("vector", lambda e: e.tensor_scalar(out=gb[:, 2:4, :], in0=gb[:, 0:2, :], scalar1=ALPHA, scalar2=None, op0=ALU.mult), r=[R("gb")], w=[R("gb")])
    banks = [ph.ps(f"bk{i}", [128, 512]) for i in range(2)]
    S1b = [(ph.ps(f"s1_{j}", [128, 512]), R("S1", j)) for j in range(3)]
    S2b = [(ph.ps(f"s2_{j}", [128, 512]), R("S2", j)) for j in range(3)]
    ns = split_even(NT, 512)
    bc_ = 0
    for i in range(16):
        w, wr = ph.wload(din["w_out"][:, 128 * i:128 * (i + 1)], 128)
        xfi, xfr = xf[i % 2], R("xf", i % 2)
        ph.op("sync", lambda e: e.dma_start(out=xfi[:], in_=din["xextT"][128 * i:128 * (i + 1), 3:NEXT]), w=[xfr], dma=f"xf{i % 2}")
        for (s0, sz) in ns:
            pb, pr = banks[bc_ % 2], R("bank", bc_ % 2)
            bc_ += 1
            mm16(ph, pb[:, 0:sz], pr, w, wr, 128, lambda k, s0=s0, sz=sz: mixT[:, k, s0:s0 + sz], mix_res)
            ph.op("vector", lambda e: e.scalar_tensor_tensor(out=pre[:, i, s0:s0 + sz], in0=xfi[:, s0:s0 + sz], scalar=ALPHA, in1=pb[:, 0:sz],
                                                             op0=ALU.mult, op1=ALU.add), r=[pr, xfr], w=[R("pre", i)])
        ln_stats_accum(ph, kb, i, lambda s0, sz: pre[:, i, s0:s0 + sz], R("pre", i), sq[i % 2], R("sq", i % 2), S1b, S2b)
    stg = sq

    def emit_fn(i, t, tr):
        ph.op("scalar", lambda e: e.activation(out=x1T[:, i, :], in_=t[:], func=AF.Identity, scale=gb[:, 0, i:i + 1], bias=gb[:, 1, i:i + 1]),
              r=[tr, R("gb")], w=[R("x1T", i)])
        st, sr = stg[i % 2], R("sq", i % 2)
        ph.op("scalar", lambda e: e.activation(out=st[:], in_=t[:], func=AF.Identity, scale=gb[:, 2, i:i + 1], bias=gb[:, 3, i:i + 1]),
              r=[tr, R("gb")], w=[sr])
        ph.op("sync", lambda e: e.dma_start(out=x1s[128 * i:128 * (i + 1), :], in_=st[:]), r=[sr], dma=f"stg{i % 2}")
    ln_feature_major(ph, kb, pre, lambda i: R("pre", i), S1b, S2b, xf, [R("xf", 0), R("xf", 1)], emit_fn, "ln")
    ph.end()


def phase_route(kb, din, x1T, gsc, accT):
    c = kb.c
    ph = Phase(kb, "rt")
    R = ph.R
    tiles = [(128 * t, 128) for t in range(8)] + [(NOWN, NS)]
    ns = split_even(NT, 512)
    keys = ph.sb("keys", [128, 8, 2, 128], BF16)
    ph.op("gpsimd", lambda e: e.dma_start(out=keys[:, :, 0, :], in_=din["k1T"].rearrange("h d k -> d h k")), w=[R("keys")], dma="k1")
    ph.op("gpsimd", lambda e: e.dma_start(out=keys[:, :, 1, :], in_=din["k2T"].rearrange("h d k -> d h k")), w=[R("keys")], dma="k2")
    iota = ph.sb("iota", [128, 256])
    ph.op("sync", lambda e: e.dma_start(out=iota[:], in_=din["c_iota"].partition_broadcast(128)), w=[R("iota")], dma="io")
    iota_rep = ph.sb("iota_rep", [128, 32, 128], BF16)
    ph.op("vector", lambda e: e.tensor_copy(out=iota_rep[:], in_=iota[:, 0:128].unsqueeze(1).to_broadcast([128, 32, 128])), r=[R("iota")], w=[R("iota_rep")])
    thr16 = ph.sb("thr16", [128, 16])
    ph.op("vector", lambda e: e.tensor_scalar(out=thr16[:], in0=iota[:, 0:16], scalar1=1.0, scalar2=16.0, op0=ALU.add, op1=ALU.mult), r=[R("iota")], w=[R("thr16")])
    TK = ph.sb("TK", [128, 9, 8, 4, 16])
    qT = ph.sb("qT", [128, 2, NT], BF16)
    sc_ = [ph.sb(f"sc{i}", [128, 256]) for i in range(2)]
    sc2_ = [ph.sb(f"sc2{i}", [128, 256]) for i in range(2)]
    jua = ph.sb("jua", [128, 9, 8, 2, 16], U32)
    juc = ph.sb("juc", [128, 8, 16], U32)
    bk = [ph.ps(f"bk{i}", [128, 512]) for i in range(8)]
    BR = lambda i: R("bank", i)
    qc = 0
    scn = 0

    def top16(n, src, src2, tv, jo, res_src, res_src2, res_out, res_j):
        ph.op("vector", lambda e: e.max(out=tv[:, 0:8], in_=src), r=[res_src], w=[res_out])
        ph.op("vector", lambda e: e.max_index(out=jo[:, 0:8], in_max=tv[:, 0:8], in_values=src), r=[res_src, res_out], w=[res_j])
        ph.op("vector", lambda e: e.match_replace(out=src2, in_to_replace=tv[:, 0:8], in_values=src, imm_value=-1e30), r=[res_src, res_out], w=[res_src2])
        ph.op("vector", lambda e: e.max(out=tv[:, 8:16], in_=src2), r=[res_src2], w=[res_out])
        ph.op("vector", lambda e: e.max_index(out=jo[:, 8:16], in_max=tv[:, 8:16], in_values=src2), r=[res_src2, res_out], w=[res_j])

    for h in range(8):
        wqb, wqr = ph.wload(din["wq"][:, 256 * h:256 * (h + 1)], 256)
        for half in range(2):
            for (s0, sz) in ns:
                pb, pr = bk[qc % 2], BR(qc % 2)
                qc += 1
                for k in range(16):
                    ph.op("tensor", lambda e, k=k: e.matmul(pb[:, 0:sz], wqb[:, k, 128 * half:128 * (half + 1)], x1T[:, k, s0:s0 + sz],
                                                            start=(k == 0), stop=(k == 15)), r=[wqr], w=[pr], signal=(k == 15))
                ph.op("scalar", lambda e: e.activation(out=qT[:, half, s0:s0 + sz], in_=pb[:, 0:sz], func=AF.Copy), r=[pr], w=[R("qT", half)])
        for tt, (t0, n) in enumerate(tiles):
            pb, pr = bk[2 + scn % 2], BR(2 + scn % 2)
            scn += 1
            for half in range(2):
                ph.op("tensor", lambda e, half=half: e.matmul(pb[0:n, 128 * half:128 * (half + 1)], qT[:, half, t0:t0 + n], keys[:, h, half, :],
                                                              start=True, stop=True), r=[R("qT", half), R("keys")], w=[pr])
            si = scn % 2
            sc, sc2 = sc_[si], sc2_[si]
            ph.op("scalar", lambda e: e.activation(out=sc[0:n, :], in_=pb[0:n, 0:256], func=AF.Copy), r=[pr], w=[R("sc", si)])
            for half in range(2):
                top16(n, sc[0:n, 128 * half:128 * (half + 1)], sc2[0:n, 128 * half:128 * (half + 1)],
                      TK[0:n, tt, h, 2 * half, :], jua[0:n, tt, h, half, :], R("sc", si), R("sc2", si), R("TK", tt), R("jua", tt))

    cand = ph.sb("cand", [128, 8, 256])
    cand2 = ph.sb("cand2", [128, 8, 256])
    top = ph.sb("top", [128, 8, 16])
    posf = ph.sb("posf", [128, 8, 16])
    af = ph.sb("af", [128, 8, 16])
    bf = ph.sb("bf", [128, 8, 16])
    rz = ph.sb("rz", [128, 8])
    PK = ph.sb("PK", [128, 3, 128])
    T3 = ph.sb("T3", [128, 3, 128])
    accflat = accT[:].rearrange("p a t -> p (a t)")
    OH2_ = [accflat[:, 2048 * i:2048 * (i + 1)].bitcast(BF16).rearrange("p (t i) -> p t i", i=128) for i in range(2)]
    gOH_ = [accflat[:, 4096 + 2048 * i:4096 + 2048 * (i + 1)].bitcast(BF16).rearrange("p (t i) -> p t i", i=128) for i in range(2)]
    obn = [0]
    GS_ = [accflat[:, 8192 + 4096 * i:8192 + 4096 * (i + 1)].bitcast(BF16).rearrange("p (i t) -> p i t", t=64) for i in range(2)]
    gcn = 0
    evn = 0
    T3_ = [T3, ph.sb("T3b", [128, 3, 128])]

    def r2top(tt):
        t0, n = tiles[tt]
        T3c, rT3 = T3_[tt % 2], R("T3", tt % 2)
        t1 = TK[0:n, tt, :, 0, :]
        j1 = TK[0:n, tt, :, 1, :]
        t2 = TK[0:n, tt, :, 2, :]
        j2 = TK[0:n, tt, :, 3, :]
        c4 = cand[0:n].rearrange("p h (a b) -> p h a b", a=16)
        c24 = cand2[0:n].rearrange("p h (a b) -> p h a b", a=16)

        def heads(h0, h1):
            for h in range(h0, h1):
                top16(n, cand[0:n, h, :], cand2[0:n, h, :], top[0:n, h, :], juc[0:n, h, :], R("cand"), R("cand2"), R("top"), R("juc"))

        def ca():
            ph.op("vector", lambda e: e.tensor_copy(out=TK[0:n, tt, :, 1::2, :], in_=jua[0:n, tt, :, :, :]), r=[R("jua", tt)], w=[R("TK", tt)])
            ph.op("vector", lambda e: e.tensor_tensor(out=c4, in0=t1.unsqueeze(3).to_broadcast([n, 8, 16, 16]),
                                                      in1=t2.unsqueeze(2).to_broadcast([n, 8, 16, 16]), op=ALU.add), r=[R("TK", tt)], w=[R("cand")])
            heads(0, 4)

        def cb():
            heads(4, 8)

        def cc():
            ph.op("vector", lambda e: e.tensor_copy(out=posf[0:n], in_=juc[0:n]), r=[R("juc")], w=[R("top")])
            thr = thr16[0:n, :].unsqueeze(1).unsqueeze(1).to_broadcast([n, 8, 16, 16])
            ph.op("vector", lambda e: e.tensor_tensor(out=c24, in0=posf[0:n].unsqueeze(3).to_broadcast([n, 8, 16, 16]), in1=thr, op=ALU.is_ge),
                  r=[R("top"), R("thr16")], w=[R("cand2")])
            ph.op("vector", lambda e: e.reduce_sum(out=af[0:n], in_=c24, axis=AX.X), r=[R("cand2")], w=[R("ab")])
            ph.op("vector", lambda e: e.scalar_tensor_tensor(out=bf[0:n], in0=af[0:n], scalar=-16.0, in1=posf[0:n], op0=ALU.mult, op1=ALU.add),
                  r=[R("ab"), R("top")], w=[R("ab")])
            io16 = iota[0:n, 0:16].unsqueeze(1).unsqueeze(1).to_broadcast([n, 8, 16, 16])
            for (sel, jt, q) in ((af, j1, 0), (bf, j2, 1)):
                ph.op("vector", lambda e, sel=sel: e.tensor_tensor(out=c24, in0=sel[0:n].unsqueeze(3).to_broadcast([n, 8, 16, 16]), in1=io16, op=ALU.is_equal),
                      r=[R("ab"), R("iota")], w=[R("cand2")])
                ph.op("vector", lambda e, jt=jt: e.tensor_tensor(out=c24, in0=c24, in1=jt.unsqueeze(2).to_broadcast([n, 8, 16, 16]), op=ALU.mult),
                      r=[R("cand2"), R("TK", tt)], w=[R("cand2")])
                ph.op("vector", lambda e, q=q: e.reduce_sum(out=PK[0:n, q, :].rearrange("p (h k) -> p h k", h=8), in_=c24, axis=AX.X),
                      r=[R("cand2")], w=[R("PK")])

        def cd():
            pk2 = PK[0:n, 2, :].rearrange("p (h k) -> p h k", h=8)
            ph.op("vector", lambda e: e.tensor_tensor(out=pk2, in0=top[0:n], in1=top[0:n, :, 0:1].to_broadcast([n, 8, 16]), op=ALU.subtract),
                  r=[R("top")], w=[R("PK")])
            ph.op("scalar", lambda e: e.activation(out=pk2, in_=pk2, func=AF.Exp), r=[R("PK")], w=[R("PK")])
            ph.op("vector", lambda e: e.reduce_sum(out=rz[0:n], in_=pk2, axis=AX.X), r=[R("PK")], w=[R("rz")])
            ph.op("vector", lambda e: e.reciprocal(out=rz[0:n], in_=rz[0:n]), r=[R("rz")], w=[R("rz")])
            ph.op("vector", lambda e: e.tensor_tensor(out=pk2, in0=pk2, in1=rz[0:n].unsqueeze(2).to_broadcast([n, 8, 16]), op=ALU.mult),
                  r=[R("PK"), R("rz")], w=[R("PK")])
            for q in range(3):
                ph.op("tensor", lambda e, q=q: e.transpose(bk[4][:, q * 128:q * 128 + n], PK[0:n, q, :], c["identf"][0:n, 0:n]), r=[R("PK")], w=[BR(4)],
                      signal=(q == 2))
            ph.op("vector", lambda e: e.tensor_copy(out=T3c[:, :, 0:n], in_=bk[4][:, 0:384].rearrange("p (q t) -> p q t", q=3)[:, :, 0:n]), r=[BR(4)], w=[rT3])
        return [ca, cb, cc, cd]

    def scatter(tt, hooks):
        nonlocal gcn, evn
        t0, n = tiles[tt]
        T3c, rT3 = T3_[tt % 2], R("T3", tt % 2)
        subs = [(th0, min(32, n - th0)) for th0 in range(0, n, 32)]
        obs = []
        for _ in subs:
            obs.append(obn[0] % 2)
            obn[0] += 1

        def prep(j):
            th0, nh = subs[j]
            ob = obs[j]
            OH2, gOH = OH2_[ob], gOH_[ob]
            rO, rG = R("OH2", ob), R("gOH", ob)
            ph.op("scalar", lambda e: e.activation(out=OH2[:, 0:nh, :], in_=T3c[:, 1, th0:th0 + nh].unsqueeze(2).to_broadcast([128, nh, 128]), func=AF.Copy),
                  r=[rT3], w=[rO])
            ph.op("scalar", lambda e: e.activation(out=gOH[:, 0:nh, :], in_=T3c[:, 0, th0:th0 + nh].unsqueeze(2).to_broadcast([128, nh, 128]), func=AF.Copy),
                  r=[rT3], w=[rG])
            ph.op("vector", lambda e: e.tensor_tensor(out=OH2[:, 0:nh, :], in0=OH2[:, 0:nh, :], in1=iota_rep[:, 0:nh, :], op=ALU.is_equal),
                  r=[R("iota_rep")], w=[rO])
            ph.op("vector", lambda e: e.tensor_tensor(out=gOH[:, 0:nh, :], in0=gOH[:, 0:nh, :], in1=iota_rep[:, 0:nh, :], op=ALU.is_equal),
                  r=[R("iota_rep")], w=[rG])
            ph.op("gpsimd", lambda e: e.tensor_tensor(out=gOH[:, 0:nh, :], in0=gOH[:, 0:nh, :], in1=T3c[:, 2, th0:th0 + nh].unsqueeze(2).to_broadcast([128, nh, 128]),
                                                      op=ALU.mult), r=[rT3, rG], w=[rG])

        def consume(j):
            nonlocal gcn, evn
            th0, nh = subs[j]
            ob = obs[j]
            OH2, gOH = OH2_[ob], gOH_[ob]
            rO, rG = R("OH2", ob), R("gOH", ob)
            for tq in range(0, nh, 4):
                pb, pr = bk[5 + gcn % 3], BR(5 + gcn % 3)
                gcn += 1
                for t in range(4):
                    ph.op("tensor", lambda e, t=t: e.matmul(pb[:, t * 128:(t + 1) * 128], OH2[:, tq + t, :], gOH[:, tq + t, :], start=True, stop=True),
                          r=[rO, rG], w=[pr], signal=(t == 3))
                src = pb[:, :].rearrange("p (t i) -> p i t", t=4)
                gh = (th0 + tq) // 64
                dst = GS_[gh][:, :, (th0 + tq) % 64:(th0 + tq) % 64 + 4]
                ph.op("scalar", lambda e: e.activation(out=dst, in_=src, func=AF.Copy), r=[pr], w=[R("GS", gh)])
                evn += 1
            if (th0 + nh) % 64 == 0 or th0 + nh == n:
                gh = th0 // 64
                for q in range(4):
                    ph.op("sync", lambda e, q=q: e.dma_start(out=gsc[32 * q:32 * (q + 1), tt, :, 64 * gh:64 * (gh + 1)].rearrange("i p t -> p i t"),
                                                             in_=GS_[gh][:, 32 * q:32 * (q + 1), :]), r=[R("GS", gh)], w=[R("GS", gh)], dma=f"gs{gh}")

        hooks = list(hooks)
        prep(0)
        for j in range(len(subs)):
            if j + 1 < len(subs):
                prep(j + 1)
            if hooks:
                hooks.pop(0)()
            consume(j)
        for hk in hooks:
            hk()

    for ch_ in r2top(0):
        ch_()
    for tt in range(len(tiles)):
        scatter(tt, r2top(tt + 1) if tt + 1 < len(tiles) else [])
    ph.end()


def phase_peer(kb, din, x1T, gsc, accT):
    ph = Phase(kb, "pe")
    R = ph.R
    ns = split_even(NT, 512)
    GC = 4
    NV = 2 * GC
    vb = [ph.sb(f"vb{i}", [128, D], BF16) for i in range(NV)]
    cf = [ph.sb(f"cf{i}", [128, NT], BF16) for i in range(NV)]
    gb = [ph.sb(f"gb{i}", [128, 9, 128], BF16) for i in range(3)]
    hg = [ph.sb(f"hg{i}", [128, NT]) for i in range(2)]
    bk = [ph.ps(f"bk{i}", [128, 512]) for i in range(8)]
    hc = 0
    oc = 0
    for grp in range(128 // GC):
        for ci in range(GC):
            ch = grp * GC + ci
            slot = ch % NV
            ub, ur = ph.wload(din["uT"][:, 128 * ch:128 * (ch + 1)], 128)
            ph.op("gpsimd", lambda e: e.dma_start(out=vb[slot][:], in_=din["vtab"][128 * ch:128 * (ch + 1), :]), w=[R("vb", slot)], dma=f"vb{slot}")
            g_, gr = gb[ch % 3], R("gb", ch % 3)
            ph.op("sync", lambda e: e.dma_start(out=g_[:, 0:8, :], in_=gsc[ch, 0:8, :, :].rearrange("tt p t -> p tt t")), w=[gr], dma=f"gb{ch % 3}")
            o_ = ph.op("sync", lambda e: e.dma_start(out=g_[:, 8, 0:64], in_=gsc[ch, 8, :, 0:64]), dma=f"gb{ch % 3}")
            gr.lw = o_
            gflat = g_[:].rearrange("p a t -> p (a t)")
            hgi, hgr = hg[ch % 2], R("hg", ch % 2)
            for (s0, sz) in ns:
                pb, pr = bk[hc % 4], R("bank", hc % 4)
                hc += 1
                for k in range(16):
                    ph.op("tensor", lambda e, k=k: e.matmul(pb[:, 0:sz], ub[:, k, 0:128], x1T[:, k, s0:s0 + sz], start=(k == 0), stop=(k == 15)),
                          r=[ur], w=[pr], signal=(k == 15))
                ph.op("scalar", lambda e: e.activation(out=hgi[:, s0:s0 + sz], in_=pb[:, 0:sz], func=AF.Gelu), r=[pr], w=[hgr])
            ph.op("vector", lambda e: e.tensor_tensor(out=cf[slot][:], in0=hgi[:], in1=gflat[:, 0:NT], op=ALU.mult), r=[hgr, gr], w=[R("cf", slot)])
        slots = [(grp * GC + ci) % NV for ci in range(GC)]
        for j in range(16):
            for (s0, sz) in ns:
                pb, pr = bk[4 + oc % 4], R("bank", 4 + oc % 4)
                oc += 1
                for ci, slot in enumerate(slots):
                    ph.op("tensor", lambda e, slot=slot, ci=ci: e.matmul(pb[:, 0:sz], vb[slot][:, 128 * j:128 * (j + 1)], cf[slot][:, s0:s0 + sz],
                                                                       start=(ci == 0), stop=(ci == GC - 1)),
                          r=[R("vb", slot), R("cf", slot)], w=[pr], signal=(ci == GC - 1))
                if grp == 0:
                    ph.op("vector", lambda e: e.tensor_copy(out=accT[:, j, s0:s0 + sz], in_=pb[:, 0:sz]), r=[pr], w=[R("acc", j)])
                else:
                    ph.op("vector", lambda e: e.tensor_tensor(out=accT[:, j, s0:s0 + sz], in0=accT[:, j, s0:s0 + sz], in1=pb[:, 0:sz], op=ALU.add),
                          r=[pr, R("acc", j)], w=[R("acc", j)])
    ph.end()


def phase_final(kb, din, dout, accT, x2T, x1s, cvc, ucc):
    c = kb.c
    ph = Phase(kb, "fn")
    R = ph.R
    ns = split_even(NT, 512)
    tiles = [(128 * t, 128) for t in range(8)] + [(NOWN, NS)]
    xl = [ph.sb(f"xl{i}", [128, NT]) for i in range(2)]
    sq = [ph.sb(f"sq{i}", [128, NT]) for i in range(2)]
    gb = ph.sb("gb", [128, 2, 16])
    ph.op("sync", lambda e: e.dma_start(out=gb[:, 0, :], in_=din["ln2gT"][:, :]), w=[R("gb")], dma="g0")
    ph.op("sync", lambda e: e.dma_start(out=gb[:, 1, :], in_=din["ln2bT"][:, :]), w=[R("gb")], dma="g1")
    pTs = ph.sb("pTs", [128, 2, NT], BF16)
    ph.op("gpsimd", lambda e: e.dma_start(out=pTs[:], in_=din["pT"].rearrange("(k p) t -> p k t", p=128)), w=[R("pTs")], dma="pT")
    wpps = ph.sb("wpps", [128, 2, D], BF16)
    ph.op("gpsimd", lambda e: e.dma_start(out=wpps[:], in_=din["wpp"].rearrange("(k p) d -> p k d", p=128)), w=[R("wpps")], dma="wpp")
    banks = [ph.ps(f"bk{i}", [128, 512]) for i in range(2)]
    S1b = [(ph.ps(f"s1_{j}", [128, 512]), R("S1", j)) for j in range(3)]
    S2b = [(ph.ps(f"s2_{j}", [128, 512]), R("S2", j)) for j in range(3)]
    for i in range(16):
        xli, xlr = xl[i % 2], R("xl", i % 2)
        ph.op("sync", lambda e: e.dma_start(out=xli[:], in_=x1s[128 * i:128 * (i + 1), :]), w=[xlr], dma=f"xl{i % 2}")
        ph.op("vector", lambda e: e.tensor_tensor(out=accT[:, i, :], in0=accT[:, i, :], in1=xli[:], op=ALU.add), r=[xlr], w=[R("acc", i)])
        ln_stats_accum(ph, kb, i, lambda s0, sz: accT[:, i, s0:s0 + sz], R("acc", i), sq[i % 2], R("sq", i % 2), S1b, S2b)

    def emit_fn(i, t, tr):
        ph.op("scalar", lambda e: e.activation(out=accT[:, i, :], in_=t[:], func=AF.Identity, scale=gb[:, 0, i:i + 1], bias=gb[:, 1, i:i + 1]),
              r=[tr, R("gb")], w=[R("acc", i)])
        ph.op("vector", lambda e: e.tensor_copy(out=x2T[:, i, :], in_=accT[:, i, :]), r=[R("acc", i)], w=[R("x2T", i)])
    ln_feature_major(ph, kb, accT, lambda i: R("acc", i), S1b, S2b, xl, [R("xl", 0), R("xl", 1)], emit_fn, "l2")
    x2_res = [R("x2T", i) for i in range(16)]
    sg = sq[0]
    yt = sq[1]
    stage = [ph.sb(f"stage{i}", [128, 9, 128]) for i in range(2)]
    bc_ = 0
    for i in range(16):
        w, wr = ph.wload(din["wple"][:, 128 * i:128 * (i + 1)], 128)
        for (s0, sz) in ns:
            pb, pr = banks[bc_ % 2], R("bank", bc_ % 2)
            bc_ += 1
            mm16(ph, pb[:, 0:sz], pr, w, wr, 128, lambda k, s0=s0, sz=sz: x2T[:, k, s0:s0 + sz], x2_res)
            ph.op("scalar", lambda e: e.activation(out=sg[:, s0:s0 + sz], in_=pb[:, 0:sz], func=AF.Sigmoid), r=[pr], w=[R("sq", 0)])
        for j, (s0, sz) in enumerate(ns):
            pb, pr = S2b[j]
            for kk in range(2):
                ph.op("tensor", lambda e, kk=kk: e.matmul(pb[:, 0:sz], wpps[:, kk, 128 * i:128 * (i + 1)], pTs[:, kk, s0:s0 + sz], start=(kk == 0), stop=(kk == 1)),
                      r=[R("wpps"), R("pTs")], w=[pr], signal=(kk == 1))
            ph.op("vector", lambda e: e.tensor_tensor(out=yt[:, s0:s0 + sz], in0=sg[:, s0:s0 + sz], in1=pb[:, 0:sz], op=ALU.mult), r=[pr, R("sq", 0)], w=[R("sq", 1)])
        ph.op("vector", lambda e: e.tensor_tensor(out=yt[:], in0=yt[:], in1=accT[:, i, :], op=ALU.add), r=[R("sq", 1), R("acc", i)], w=[R("sq", 1)])
        st, sr = stage[i % 2], R("stage", i % 2)
        for tt, (t0, n) in enumerate(tiles):
            pb, pr = S1b[tt // 4]
            ph.op("tensor", lambda e: e.transpose(pb[0:n, (tt % 4) * 128:(tt % 4 + 1) * 128], yt[:, t0:t0 + n], c["identf"][:]), r=[R("sq", 1)], w=[pr])
        for q in range(2):
            ph.op("vector" if q == 0 else "scalar",
                  (lambda e: e.tensor_copy(out=st[:, 0:4, :].rearrange("p a c -> p (a c)"), in_=S1b[0][0][:, :])) if q == 0 else
                  (lambda e: e.activation(out=st[:, 4:8, :].rearrange("p a c -> p (a c)"), in_=S1b[1][0][:, :], func=AF.Copy)),
                  r=[S1b[q][1]], w=[R("stage", i % 2, q)])
        ph.op("vector", lambda e: e.tensor_copy(out=st[0:NS, 8, :], in_=S1b[2][0][0:NS, 0:128]), r=[S1b[2][1]], w=[R("stage", i % 2, 2)])
        ph.op("sync", lambda e: e.dma_start(out=dout["y"][0:NOWN, 128 * i:128 * (i + 1)].rearrange("(a p) c -> p a c", p=128), in_=st[:, 0:8, :]),
              r=[R("stage", i % 2, 0), R("stage", i % 2, 1)], w=[R("stage", i % 2, 0), R("stage", i % 2, 1)], dma=f"yo{i % 2}")
        ph.op("sync", lambda e: e.dma_start(out=dout["y"][NOWN:NT, 128 * i:128 * (i + 1)], in_=st[0:NS, 8, :]),
              r=[R("stage", i % 2, 2)], w=[R("stage", i % 2, 2)], dma=f"yos{i % 2}")
    rows = ph.sb("rows", [32, 3072])
    for q in range(6):
        pb, pr = banks[q % 2], R("bank", q % 2)
        for m in range(4):
            ph.op("tensor", lambda e, m=m: e.transpose(pb[0:19, m * 128:(m + 1) * 128], cvc[:, 4 * q + m, :], c["identf"][:]), w=[pr], signal=(m == 3))
        ph.op("vector", lambda e: e.tensor_copy(out=rows[0:19, 512 * q:512 * (q + 1)], in_=pb[0:19, :]), r=[pr], w=[R("rows")])
    ph.op("sync", lambda e: e.dma_start(out=dout["cvp"][:, :], in_=rows[0:3, :]), r=[R("rows")], dma="cv")
    ph.op("sync", lambda e: e.dma_start(out=dout["cvs"][:, 2, :], in_=rows[3:19, :]), r=[R("rows")], dma="cv")
    ph.op("sync", lambda e: e.dma_start(out=dout["cvs"][:, 0:2, :], in_=din["cv0"][:, 1:3, :]), dma="cv")
    rows2 = ph.sb("rows2", [32, 2048])
    for q in range(4):
        pb, pr = banks[q % 2], R("bank", q % 2)
        for m in range(4):
            ph.op("tensor", lambda e, m=m: e.transpose(pb[0:18, m * 128:(m + 1) * 128], ucc[:, 4 * q + m, :], c["identf"][:]), w=[pr], signal=(m == 3))
        ph.op("vector", lambda e: e.tensor_copy(out=rows2[0:18, 512 * q:512 * (q + 1)], in_=pb[0:18, :]), r=[pr], w=[R("rows2")])
    ph.op("sync", lambda e: e.dma_start(out=dout["scp"][:, :], in_=rows2[0:2, :]), r=[R("rows2")], dma="cv")
    ph.op("sync", lambda e: e.dma_start(out=dout["scs"][:, 1, :], in_=rows2[2:18, :]), r=[R("rows2")], dma="cv")
    ph.op("sync", lambda e: e.dma_start(out=dout["scs"][:, 0, :], in_=din["sc0"][:, 1, :]), dma="cv")
    ph.end()


def _prep_inputs(inp):
    f = np.float32
    A = lambda x: np.ascontiguousarray(np.asarray(x, dtype=f))
    xp = A(inp["x_prompt"])
    xs = A(inp["x_sample"])[:, 0]
    pp = A(inp["p_prompt"])[0]
    psm = A(inp["p_sample"])[0][:, 0]
    ssm = A(inp["state_ssm"])[0]
    cv = A(inp["state_ssd_conv"])[0]
    sc = A(inp["state_shortconv"])[0]
    cw = A(inp["ssd_conv_w"])[0]
    scw = A(inp["sc_conv_w"])[0]
    shared = dict(
        w_in=A(inp["w_in"])[0],
        convwT=A(cw.T.reshape(24, 128, 4).transpose(1, 0, 2).reshape(128, 96)),
        convb=A(A(inp["ssd_conv_b"])[0].reshape(24, 128).T),
        dtb=A(inp["ssd_dt_bias"]).reshape(1, 32), alog=A(inp["ssd_a_log"]).reshape(1, 32), dsk=A(inp["ssd_d"]).reshape(1, 32),
        normw=A(inp["ssd_norm_w"]).reshape(1, 2048),
        scwT=A(scw.T.reshape(16, 128, 3).transpose(1, 0, 2).reshape(128, 48)),
        w_bssd=A(inp["w_branch_ssd"])[0], w_bsc=A(inp["w_branch_sc"])[0], w_out=A(inp["w_out"])[0],
        ln1gT=A(A(inp["ln1_g"]).reshape(16, 128).T), ln1bT=A(A(inp["ln1_b"]).reshape(16, 128).T),
        wq=A(inp["peer_wq"])[0],
        k1T=A(A(inp["peer_keys1"])[0].transpose(0, 2, 1)), k2T=A(A(inp["peer_keys2"])[0].transpose(0, 2, 1)),
        uT=A(A(inp["peer_u"])[0].T), vtab=A(inp["peer_v"])[0],
        ln2gT=A(A(inp["ln2_g"]).reshape(16, 128).T), ln2bT=A(A(inp["ln2_b"]).reshape(16, 128).T),
        wple=A(inp["ple_gate_w"])[0], wpp=A(inp["ple_proj_w"])[0],
        c_ident=np.eye(128, dtype=f), c_tri=np.triu(np.ones((128, 128), dtype=f)),
        c_iota=np.arange(256, dtype=f).reshape(1, 256),
        c_negm=np.ascontiguousarray(np.tril(np.full((128, 128), -30000.0, dtype=f), -1)),
    )
    maps = []
    for c in range(NCORES):
        s_, hf = c // 2, c % 2
        own = xp[s_, hf * NOWN:(hf + 1) * NOWN]
        smp = xs[c * NS:(c + 1) * NS]
        if hf:
            prev = xp[s_, 0:NOWN]
            halo = xp[s_, NOWN - 3:NOWN]
        else:
            prev = np.zeros((NOWN, D), f)
            halo = np.zeros((3, D), f)
        cvc_ = cv[c * NS:(c + 1) * NS]
        scc_ = sc[c * NS:(c + 1) * NS]
        m = dict(shared)
        m.update(
            xprevT=A(prev.T), xextT=A(np.concatenate([halo, own, smp], 0).T),
            xown=A(np.concatenate([own, smp], 0)),
            pT=A(np.concatenate([pp[s_, hf * NOWN:(hf + 1) * NOWN], psm[c * NS:(c + 1) * NS]], 0).T),
            flag=np.full((128, 1), float(hf), f),
            ssm0=A(ssm[c * NS:(c + 1) * NS].reshape(NS, 2048, 128)),
            cv0=A(cvc_), sc0=A(scc_),
            cv0T=A(cvc_.reshape(NS, 3, 24, 128).transpose(3, 2, 0, 1).reshape(128, 24 * NS * 3)),
            sc0T=A(scc_.reshape(NS, 2, 16, 128).transpose(3, 2, 0, 1).reshape(128, 16 * NS * 2)),
        )
        maps.append(m)
    return maps


_CACHE = {}


def run_cores(inp, stage=99, trace=False, only=None):
    key = (stage, tuple(sorted(DEBUG)), CUT)
    if key not in _CACHE:
        _CACHE[key] = build_program(stage)
    nc, din, dout = _CACHE[key]
    maps = _prep_inputs(inp)
    maps = [{k: v for k, v in m.items() if k in din} for m in maps]
    if only is not None:
        return run_bass_kernel_spmd(nc, [maps[only]], core_ids=[0], trace=trace)
    res = run_bass_kernel_spmd(nc, maps, core_ids=list(range(NCORES)), trace=trace)
    return res


def kernel(**inp):
    res = run_cores(inp).results
    f = np.float32
    yp = np.zeros((4, 2048, D), f)
    ys = np.zeros((128, 1, D), f)
    hp = np.zeros((1, 4, 32, 64, 128), f)
    cp = np.zeros((1, 4, 3, 3072), f)
    sp = np.zeros((1, 4, 2, 2048), f)
    hs = np.zeros((1, 128, 32, 64, 128), f)
    cs = np.zeros((1, 128, 3, 3072), f)
    ss = np.zeros((1, 128, 2, 2048), f)
    for c in range(NCORES):
        r = res[c]
        s_, hf = c // 2, c % 2
        y = np.asarray(r["y"])
        yp[s_, hf * NOWN:(hf + 1) * NOWN] = y[:NOWN]
        ys[c * NS:(c + 1) * NS, 0] = y[NOWN:]
        if hf:
            hp[0, s_] = np.asarray(r["hfin"]).reshape(32, 64, 128)
            cp[0, s_] = np.asarray(r["cvp"])
            sp[0, s_] = np.asarray(r["scp"])
        hs[0, c * NS:(c + 1) * NS] = np.asarray(r["hs"]).reshape(NS, 32, 64, 128)
        cs[0, c * NS:(c + 1) * NS] = np.asarray(r["cvs"])
        ss[0, c * NS:(c + 1) * NS] = np.asarray(r["scs"])
    return (yp, ys, hp, cp, sp, hs, cs, ss)
```

```python
import numpy as np
from contextlib import ExitStack
import concourse.bass as bass
import concourse.mybir as mybir
from concourse.bass_utils import run_bass_kernel_spmd

F32 = mybir.dt.float32
BF16 = mybir.dt.bfloat16
U32 = mybir.dt.uint32
ALU = mybir.AluOpType
AF = mybir.ActivationFunctionType
AX = mybir.AxisListType

ENGS = ("tensor", "vector", "scalar", "gpsimd", "sync")
NCORES = 8
D = 2048
NOWN = 1024
NS = 16
NT = NOWN + NS
NEXT = NT + 3
ALPHA = 2.0 ** 0.25
LN_EPS = 1e-5
RMS_EPS = 1e-5
IN_COLS = 15392
OFF_Z, OFF_XBC, OFF_DT, OFF_SCB, OFF_SCC, OFF_SCH, OFF_GA, OFF_GB = 0, 2048, 5120, 5152, 7200, 9248, 11296, 13344

DEBUG = {}
CUT = 0


EXCL = {"bank", "psmm", "psdt", "pstr", "psst", "S1", "S2"}


class Res:
    __slots__ = ("name", "lw", "rd", "excl")

    def __init__(self, name):
        self.name = name
        self.lw = None
        self.rd = []
        self.excl = bool(name) and name[0] in EXCL


class Op:
    __slots__ = ("eng", "fn", "deps", "signal", "dma_key", "dma_val", "sigidx", "call")


class _Rec:
    def __init__(self):
        self.call = None

    def __getattr__(self, name):
        def f(*a, **k):
            self.call = (name, a, k)
            return self
        return f


class Prog:
    UID = 0
    G = {}

    def __init__(self, nc):
        self.nc = nc
        self.ops = {e: [] for e in ENGS}
        self.dma_cnt = {}
        self.res_cache = {}

    def R(self, *key):
        r = self.res_cache.get(key)
        if r is None:
            r = Res(key)
            self.res_cache[key] = r
        return r

    def op(self, eng, fn, r=(), w=(), dma=None, signal=True):
        o = Op()
        o.eng = eng
        o.fn = None
        rec = _Rec()
        fn(rec)
        o.call = rec.call
        o.signal = signal or (dma is not None)
        o.dma_key = dma
        o.dma_val = None
        o.sigidx = None
        if dma is not None:
            c = self.dma_cnt.get(dma, 0) + 16
            self.dma_cnt[dma] = c
            o.dma_val = c
        if any(res.excl for res in r):
            w = list(w) + [res for res in r if res.excl and res not in w]
            r = [res for res in r if not res.excl]
        deps = []
        for res in r:
            if res.lw is not None:
                deps.append(res.lw)
        for res in w:
            if res.lw is not None:
                deps.append(res.lw)
            deps.extend(res.rd)
        seen = set()
        dd = []
        for d in deps:
            if id(d) in seen:
                continue
            seen.add(id(d))
            if d.eng == eng and d.dma_key is None and dma is None and eng == "tensor":
                continue
            dd.append(d)
        o.deps = dd
        self.ops[eng].append(o)
        for res in r:
            res.rd.append(o)
        for res in w:
            res.lw = o
            res.rd = []
        return o

    def emit(self):
        nc = self.nc
        G = Prog.G
        if "esem" not in G:
            G["esem"] = {e: G["stack"].enter_context(nc.semaphore(f"se_{e}")) for e in ENGS}
            G["bar"] = G["stack"].enter_context(nc.semaphore("sbar"))
            G["ebase"] = {e: 0 for e in ENGS}
            G["barbase"] = 0
        esem = G["esem"]
        bar = G["bar"]
        ebase = dict(G["ebase"])
        final_cnt = {}
        for e in ENGS:
            last = None
            for o in self.ops[e]:
                if o.dma_key is None:
                    last = o
            if last is not None:
                last.signal = True
            cnt = ebase[e]
            pending = []
            for o in self.ops[e]:
                if o.dma_key is not None:
                    continue
                pending.append(o)
                if o.signal:
                    cnt += 1
                    for p in pending:
                        p.sigidx = cnt
                    pending = []
            final_cnt[e] = cnt
            G["ebase"][e] = cnt
        G["barbase"] += len(ENGS)
        bar_target = G["barbase"]
        Prog.UID += 1
        u = Prog.UID
        dsem = {k: G["stack"].enter_context(nc.semaphore(f"sd{u}_{k}")) for k in self.dma_cnt}
        with nc.Block() as block:

            def run(e_name, eng):
                seen = {}
                for o in self.ops[e_name]:
                    need = {}
                    for d in o.deps:
                        if d.dma_key is not None:
                            k = ("d", d.dma_key)
                            v = d.dma_val
                        else:
                            k = ("e", d.eng)
                            v = d.sigidx
                        if need.get(k, 0) < v:
                            need[k] = v
                    for k, v in need.items():
                        if seen.get(k, 0) >= v:
                            continue
                        seen[k] = v
                        eng.wait_ge(dsem[k[1]] if k[0] == "d" else esem[k[1]], v)
                    name_, a_, k_ = o.call
                    ins = getattr(eng, name_)(*a_, **k_)
                    if o.dma_key is not None:
                        ins.then_inc(dsem[o.dma_key], 16)
                    elif o.signal:
                        ins.then_inc(esem[e_name], 1)
                if final_cnt[e_name] > ebase[e_name]:
                    eng.wait_ge(esem[e_name], final_cnt[e_name])
                fin = {}
                for o in self.ops[e_name]:
                    if o.dma_key is not None:
                        fin[o.dma_key] = self.dma_cnt[o.dma_key]
                for k, v in fin.items():
                    eng.wait_ge(dsem[k], v)
                eng.sem_inc(bar, 1)
                eng.wait_ge(bar, bar_target)

            @block.tensor
            def _(eng):
                run("tensor", eng)

            @block.vector
            def _(eng):
                run("vector", eng)

            @block.scalar
            def _(eng):
                run("scalar", eng)

            @block.gpsimd
            def _(eng):
                run("gpsimd", eng)

            @block.sync
            def _(eng):
                run("sync", eng)


def split_even(n, maxsz=512):
    k = -(-n // maxsz)
    base, rem = divmod(n, k)
    out = []
    s = 0
    for i in range(k):
        sz = base + (1 if i < rem else 0)
        out.append((s, sz))
        s += sz
    return out


class Phase:
    def __init__(self, kb, name):
        self.kb = kb
        self.nc = kb.nc
        self.name = name
        self.es = ExitStack()
        self.P = Prog(kb.nc)
        self.wcnt = 0
        self.uid = 0

    def sb(self, name, shape, dt=F32):
        return self.es.enter_context(self.nc.sbuf_tensor(f"{self.name}_{name}", list(shape), dt))

    def ps(self, name, shape, dt=F32):
        return self.es.enter_context(self.nc.psum_tensor(f"{self.name}_{name}", list(shape), dt))

    def R(self, *k):
        return self.P.R(*k)

    def op(self, *a, **k):
        return self.P.op(*a, **k)

    def end(self):
        self.P.emit()
        self.es.close()

    def wload(self, src2d, ncols, rows=2048):
        kb = self.kb
        i = self.wcnt % kb.NW
        self.wcnt += 1
        wb = kb.wbufs[i]
        res = self.R("wb", i)
        kt = rows // 128
        src = src2d.rearrange("(k p) c -> p k c", p=128)
        half = kt // 2 if kt >= 2 else kt
        self.op("gpsimd", lambda e: e.dma_start(out=wb[:, 0:half, 0:ncols], in_=src[:, 0:half, :]),
                w=[res], dma=f"wb{i}")
        if half < kt:
            o = self.P.op("gpsimd", lambda e: e.dma_start(out=wb[:, half:kt, 0:ncols], in_=src[:, half:kt, :]),
                          r=[], w=[], dma=f"wb{i}")
            res.lw = o
        return wb, res


class KB:
    pass


def build_program(stage=99):
    nc = bass.Bass("TRN2", target_bir_lowering=False)
    kb = KB()
    kb.nc = nc
    din = {}
    dout = {}

    def DI(name, shape, dt=F32):
        din[name] = nc.dram_tensor(name, list(shape), dt, kind="ExternalInput").ap()
        return din[name]

    def DO(name, shape, dt=F32):
        dout[name] = nc.dram_tensor(name, list(shape), dt, kind="ExternalOutput").ap()
        return dout[name]

    xprevT = DI("xprevT", [D, NOWN])
    xextT = DI("xextT", [D, NEXT])
    xown = DI("xown", [NT, D])
    pT = DI("pT", [256, NT])
    flag = DI("flag", [128, 1])
    ssm0 = DI("ssm0", [NS, 2048, 128])
    cv0 = DI("cv0", [NS, 3, 3072])
    cv0T = DI("cv0T", [128, 24 * NS * 3])
    sc0 = DI("sc0", [NS, 2, 2048])
    sc0T = DI("sc0T", [128, 16 * NS * 2])
    w_in = DI("w_in", [D, IN_COLS])
    convwT = DI("convwT", [128, 24 * 4])
    convb = DI("convb", [128, 24])
    dtb = DI("dtb", [1, 32])
    alog = DI("alog", [1, 32])
    dsk = DI("dsk", [1, 32])
    normw = DI("normw", [1, 2048])
    scwT = DI("scwT", [128, 16 * 3])
    if stage >= 4:
        DI("w_bssd", [D, D])
        DI("w_bsc", [D, D])
        DI("w_out", [D, D])
    DI("ln1gT", [128, 16])
    DI("ln1bT", [128, 16])
    if stage >= 6:
        DI("wq", [D, D])
        DI("k1T", [8, 128, 128])
        DI("k2T", [8, 128, 128])
        DI("uT", [D, 16384])
        DI("vtab", [16384, D])
    DI("ln2gT", [128, 16])
    DI("ln2bT", [128, 16])
    if stage >= 8:
        DI("wple", [D, D])
        DI("wpp", [256, D])
    c_ident = DI("c_ident", [128, 128])
    c_tri = DI("c_tri", [128, 128])
    c_iota = DI("c_iota", [1, 256])
    c_negm = DI("c_negm", [128, 128])

    y_o = DO("y", [NT, D])
    hfin_o = DO("hfin", [2048, 128])
    cvp_o = DO("cvp", [3, 3072])
    scp_o = DO("scp", [2, 2048])
    hs_o = DO("hs", [NS, 2048, 128])
    cvs_o = DO("cvs", [NS, 3, 3072])
    scs_o = DO("scs", [NS, 2, 2048])
    dbg = {}
    for name, (shape, dt) in DEBUG.items():
        dbg[name] = DO(name, shape, dt)

    x1s = nc.dram_tensor("x1s", [D, NT], F32).ap()
    gsc = nc.dram_tensor("gsc", [128, 9, 128, 128], BF16).ap()

    with ExitStack() as glob:
        Prog.G = {"stack": glob}

        def gsb(name, shape, dt=F32):
            return glob.enter_context(nc.sbuf_tensor("g_" + name, list(shape), dt))

        kb.NW = 3
        kb.wbufs = [gsb(f"wb{i}", [128, 16, 256], BF16) for i in range(kb.NW)]
        identf = gsb("identf", [128, 128])
        identb = gsb("identb", [128, 128], BF16)
        trif = gsb("trif", [128, 128])
        onesf = gsb("onesf", [128, 128])
        negm = gsb("negm", [128, 128])
        onesb = gsb("onesb", [128, 128], BF16)
        flag_sb = gsb("flag", [128, 1])
        hmid = gsb("hmid", [128, 2048])
        cwT = gsb("cwT", [128, 24, 4])
        cbT = gsb("cbT", [128, 24])
        swT = gsb("swT", [128, 16, 3])
        dtb_bc = gsb("dtb_bc", [128, 32])
        A_bc = gsb("A_bc", [128, 32])
        D_bc = gsb("D_bc", [128, 32])
        cvc = gsb("cvc", [128, 24, 19])
        ucc = gsb("ucc", [128, 16, 18])

        ph = Phase(kb, "p0")
        ph.op("sync", lambda e: e.dma_start(out=identf[:], in_=c_ident[:, :]), w=[ph.R("identf")], dma="c0")
        ph.op("sync", lambda e: e.dma_start(out=trif[:], in_=c_tri[:, :]), w=[ph.R("trif")], dma="c1")
        ph.op("sync", lambda e: e.dma_start(out=negm[:], in_=c_negm[:, :]), w=[ph.R("negm")], dma="c1b")
        ph.op("gpsimd", lambda e: e.dma_start(out=identb[:], in_=c_ident[:, :]), w=[ph.R("identb")], dma="c2")
        ph.op("sync", lambda e: e.dma_start(out=flag_sb[:], in_=flag[:, :]), w=[ph.R("flag")], dma="c3")
        ph.op("sync", lambda e: e.dma_start(out=cwT[:], in_=convwT.rearrange("p (k j) -> p k j", j=4)), w=[ph.R("cwT")], dma="c4")
        ph.op("sync", lambda e: e.dma_start(out=cbT[:], in_=convb[:, :]), w=[ph.R("cbT")], dma="c5")
        ph.op("sync", lambda e: e.dma_start(out=swT[:], in_=scwT.rearrange("p (k j) -> p k j", j=3)), w=[ph.R("swT")], dma="c6")
        ph.op("sync", lambda e: e.dma_start(out=dtb_bc[:], in_=dtb.partition_broadcast(128)), w=[ph.R("dtb")], dma="c7")
        ph.op("sync", lambda e: e.dma_start(out=A_bc[:], in_=alog.partition_broadcast(128)), w=[ph.R("A")], dma="c8")
        ph.op("sync", lambda e: e.dma_start(out=D_bc[:], in_=dsk.partition_broadcast(128)), w=[ph.R("Dsk")], dma="c9")
        ph.op("vector", lambda e: e.memset(onesf[:], 1.0), w=[ph.R("onesf")])
        ph.op("vector", lambda e: e.memset(onesb[:], 1.0), w=[ph.R("onesb")])
        ph.op("scalar", lambda e: e.activation(out=A_bc[:], in_=A_bc[:], func=AF.Exp), r=[ph.R("A")], w=[ph.R("A")])
        ph.op("vector", lambda e: e.tensor_scalar(out=A_bc[:], in0=A_bc[:], scalar1=-1.0, scalar2=None, op0=ALU.mult),
              r=[ph.R("A")], w=[ph.R("A")])
        ph.end()

        consts = dict(identf=identf, identb=identb, trif=trif, onesf=onesf, onesb=onesb, flag=flag_sb, negm=negm,
                      cwT=cwT, cbT=cbT, swT=swT, dtb_bc=dtb_bc, A_bc=A_bc, D_bc=D_bc)
        kb.c = consts

        mixT = nc.dram_tensor("mixs", [D, NT], BF16).ap()
        with ExitStack() as s1:
            def s1sb(name, shape, dt=F32):
                return s1.enter_context(nc.sbuf_tensor("g_" + name, list(shape), dt))
            kb.xT = s1sb("xT", [128, 16, NEXT], BF16)
            y_aT = s1sb("y_aT", [128, 16, NT], BF16)
            if stage >= 1:
                phase_prefix(kb, din, hmid)
            if stage >= 2:
                phase_ssd(kb, din, dout, dbg, hmid, y_aT, cvc)
            if "d_yaT" in dbg:
                dbg_dump(kb, dbg["d_yaT"], y_aT)
            y_bT = s1sb("y_bT", [128, 16, NT], BF16)
            if stage >= 3:
                phase_sc(kb, din, y_bT, ucc)
            if "d_ybT" in dbg:
                dbg_dump(kb, dbg["d_ybT"], y_bT)
            if stage >= 4:
                phase_mix(kb, din, y_aT, y_bT, mixT)
            if "d_mixT" in dbg:
                ph = Phase(kb, "dbgmx")
                ph.op("sync", lambda e: e.dma_start(out=dbg["d_mixT"][:, :], in_=mixT[:, :]), dma="st")
                ph.end()
        accT = gsb("accT", [128, 16, NT])
        x1T = gsb("x1T", [128, 16, NT], BF16)
        if stage >= 5:
            phase_ln1(kb, din, mixT, x1T, x1s, accT)
        if "d_x1T" in dbg:
            ph = Phase(kb, "dbgx1")
            ph.op("sync", lambda e: e.dma_start(out=dbg["d_x1T"][:, :], in_=x1s[:, :]), dma="st")
            ph.end()
        if stage >= 6:
            phase_route(kb, din, x1T, gsc, accT)
        if stage >= 7:
            phase_peer(kb, din, x1T, gsc, accT)
        if "d_peT" in dbg:
            dbg_dump(kb, dbg["d_peT"], accT)
        if stage >= 8:
            phase_final(kb, din, dout, accT, x1T, x1s, cvc, ucc)
    return nc, din, dout


def dt_chain(ph, kb, tag, n, dt_ps, dt_ps_res, dtf, a_t, res_prefix):
    c = kb.c
    R = ph.R
    t0 = (res_prefix, "dtf")
    ph.op("vector", lambda e: e.tensor_tensor(out=dtf, in0=dt_ps, in1=c["dtb_bc"][0:n, :], op=ALU.add),
          r=[dt_ps_res], w=[R(*t0)])
    ph.op("scalar", lambda e: e.activation(out=dtf, in_=dtf, func=AF.Exp), r=[R(*t0)], w=[R(*t0)])
    ph.op("scalar", lambda e: e.activation(out=dtf, in_=dtf, func=AF.Ln, bias=1.0), r=[R(*t0)], w=[R(*t0)])
    ph.op("vector", lambda e: e.tensor_tensor(out=a_t, in0=dtf, in1=c["A_bc"][0:n, :], op=ALU.mult),
          r=[R(*t0)], w=[R(res_prefix, "a")])


def conv_silu(ph, kb, raw, raw_res, ntok, wcols, bias_col, out_bf, out_res, acc, acc_res, taps, lead):
    ph.op("vector", lambda e: e.tensor_scalar(out=acc, in0=raw[:, lead:lead + ntok], scalar1=wcols[:, taps - 1:taps],
                                              scalar2=None, op0=ALU.mult), r=[raw_res], w=[acc_res])
    for j in range(taps - 1):
        ph.op("vector", lambda e, j=j: e.scalar_tensor_tensor(out=acc, in0=raw[:, j + lead - (taps - 1):j + lead - (taps - 1) + ntok],
                                                           scalar=wcols[:, j:j + 1], in1=acc, op0=ALU.mult, op1=ALU.add),
              r=[raw_res, acc_res], w=[acc_res])
    if bias_col is not None:
        ph.op("scalar", lambda e: e.activation(out=out_bf, in_=acc, func=AF.Silu, bias=bias_col, scale=1.0),
              r=[acc_res], w=[out_res])


def phase_prefix(kb, din, hmid):
    nc = kb.nc
    c = kb.c
    ph = Phase(kb, "pf")
    R = ph.R
    w_in = din["w_in"]
    xT = ph.sb("xT", [128, 16, NOWN], BF16)
    xsrc = din["xprevT"].rearrange("(k p) t -> p k t", p=128)
    for q in range(4):
        ph.op("gpsimd", lambda e, q=q: e.dma_start(out=xT[:, 4 * q:4 * q + 4, :], in_=xsrc[:, 4 * q:4 * q + 4, :]),
              w=[R("xT", q)], dma=f"xT{q}")
    xT_res = [R("xT", q) for q in range(4)]

    dtf = ph.sb("dtf", [128, 8, 32])
    a_t = ph.sb("a", [128, 8, 32])
    acs = ph.sb("acs", [128, 8, 32])
    tot = ph.sb("tot", [128, 8, 32])
    dd = ph.sb("dd", [128, 8, 32])
    cdec = ph.sb("cdec", [128, 8, 32])
    ps_dt = [ph.ps(f"psdt{i}", [128, 512]) for i in range(2)]
    wdt, wdt_res = ph.wload(w_in[:, OFF_DT:OFF_DT + 32], 32)
    for tt in range(8):
        pb = ps_dt[tt % 2]
        pres = R("psdt", tt % 2)
        for k in range(16):
            ph.op("tensor", lambda e, k=k, tt=tt, pb=pb: e.matmul(pb[:, 0:32], xT[:, k, tt * 128:(tt + 1) * 128], wdt[:, k, 0:32],
                                                                  start=(k == 0), stop=(k == 15)),
                  r=[wdt_res] + xT_res, w=[pres], signal=(k == 15))
        dt_chain(ph, kb, "pf", 128, pb[:, 0:32], pres, dtf[:, tt, :], a_t[:, tt, :], ("dt", tt))
        ph.op("tensor", lambda e, tt=tt, pb=pb: e.matmul(pb[:, 32:64], c["trif"][:], a_t[:, tt, :], start=True, stop=True),
              r=[R(("dt", tt), "a")], w=[pres])
        ph.op("tensor", lambda e, tt=tt, pb=pb: e.matmul(pb[:, 64:96], c["onesf"][:], a_t[:, tt, :], start=True, stop=True),
              r=[R(("dt", tt), "a")], w=[pres])
        ph.op("vector", lambda e, tt=tt, pb=pb: e.tensor_copy(out=tot[:, tt, :], in_=pb[:, 64:96]), r=[pres], w=[R("tot", tt)])
        ph.op("vector", lambda e, tt=tt, pb=pb: e.tensor_tensor(out=acs[:, tt, :], in0=tot[:, tt, :], in1=pb[:, 32:64], op=ALU.subtract),
              r=[pres, R("tot", tt)], w=[R("acs", tt)])
        ph.op("scalar", lambda e, tt=tt: e.activation(out=dd[:, tt, :], in_=acs[:, tt, :], func=AF.Exp), r=[R("acs", tt)], w=[R("dd", tt)])
        ph.op("vector", lambda e, tt=tt: e.tensor_tensor(out=dd[:, tt, :], in0=dd[:, tt, :], in1=dtf[:, tt, :], op=ALU.mult),
              r=[R("dd", tt), R(("dt", tt), "dtf")], w=[R("dd", tt)])
        ph.op("scalar", lambda e, tt=tt: e.activation(out=cdec[:, tt, :], in_=tot[:, tt, :], func=AF.Exp), r=[R("tot", tt)], w=[R("cdec", tt)])

    if CUT == 1:
        ph.end()
        return
    raw_ = [ph.sb(f"raw{i}", [128, 3 + NOWN]) for i in range(2)]
    acc_ = [ph.sb(f"acc{i}", [128, NOWN]) for i in range(2)]
    ftc = [0]
    xc = ph.sb("xc", [128, 4, NOWN], BF16)
    bT = ph.sb("bT", [128, NOWN], BF16)
    xdec = [ph.sb(f"xdec{i}", [128, 512], BF16) for i in range(2)]
    btm = [ph.sb(f"btm{i}", [128, 128], BF16) for i in range(2)]
    hT = ph.sb("hT", [128, 512])
    ps_mm = [ph.ps(f"psmm{i}", [128, 512]) for i in range(2)]
    ps_tr = [ph.ps(f"pstr{i}", [128, 1024], BF16) for i in range(2)]
    ps_st = [ph.ps(f"psst{i}", [128, 512]) for i in range(2)]
    for i in range(2):
        ph.op("vector", lambda e, i=i: e.memset(raw_[i][:, 0:3], 0.0), w=[R("raw", i)])
    mmc = 0
    trc = 0

    def feat_tile(col0, ch_tile, out_bf, out_res):
        nonlocal mmc
        fi = ftc[0] % 2
        ftc[0] += 1
        raw, acc = raw_[fi], acc_[fi]
        wb, wres = ph.wload(w_in[:, col0:col0 + 128], 128)
        for nt in range(2):
            pb = ps_mm[mmc % 2]
            pres = R("psmm", mmc % 2)
            mmc += 1
            for k in range(16):
                ph.op("tensor", lambda e, k=k, nt=nt, pb=pb, wb=wb: e.matmul(pb[:, :], wb[:, k, 0:128], xT[:, k, nt * 512:(nt + 1) * 512],
                                                                           start=(k == 0), stop=(k == 15)),
                      r=[wres] + xT_res, w=[pres], signal=(k == 15))
            ph.op("scalar", lambda e, nt=nt, pb=pb: e.activation(out=raw[:, 3 + nt * 512:3 + (nt + 1) * 512], in_=pb[:, :], func=AF.Copy),
                  r=[pres], w=[R("raw", fi)])
        conv_silu(ph, kb, raw, R("raw", fi), NOWN, c["cwT"][:, ch_tile, :], c["cbT"][:, ch_tile:ch_tile + 1], out_bf, out_res,
                  acc[:, :], R("acc", fi), 4, 3)

    for g in range(4):
        for m in range(4):
            feat_tile(OFF_XBC + 512 * g + 128 * m, 4 * g + m, xc[:, m, :], R("xc", m))
        feat_tile(OFF_XBC + 2048 + 128 * g, 16 + g, bT[:, :], R("bT"))
        if g == 2:
            xsrc2 = din["xextT"].rearrange("(k p) t -> p k t", p=128)
            for q in range(4):
                ph.op("gpsimd", lambda e, q=q: e.dma_start(out=kb.xT[:, 4 * q:4 * q + 4, :], in_=xsrc2[:, 4 * q:4 * q + 4, :]),
                      w=[R("xTe", q)], dma=f"xTe{q}")
        if CUT == 2:
            break
        for ck in range(8):
            tsl = slice(ck * 128, (ck + 1) * 128)
            pt = ps_tr[trc % 2]
            ptres = R("pstr", trc % 2)
            xd = xdec[trc % 2]
            xdres = R("xdec", trc % 2)
            bt = btm[trc % 2]
            btres = R("btm", trc % 2)
            pst = ps_st[trc % 2]
            pstres = R("psst", trc % 2)
            trc += 1
            for m in range(4):
                ph.op("tensor", lambda e, m=m, pt=pt, tsl=tsl: e.transpose(pt[:, m * 128:(m + 1) * 128], xc[:, m, tsl], c["identb"][:]),
                      r=[R("xc", m)], w=[ptres], signal=False)
            ph.op("tensor", lambda e, pt=pt, tsl=tsl: e.transpose(pt[:, 512:640], bT[:, tsl], c["identb"][:]), r=[R("bT")], w=[ptres])
            ph.op("vector", lambda e, pt=pt, xd=xd, ck=ck, g=g: e.tensor_tensor(
                out=xd[:].rearrange("p (h q) -> p h q", h=8), in0=pt[:, 0:512].rearrange("p (h q) -> p h q", h=8),
                in1=dd[:, ck, 8 * g:8 * g + 8].unsqueeze(2).to_broadcast([128, 8, 64]), op=ALU.mult),
                r=[ptres, R("dd", ck)], w=[xdres])
            if CUT == 3:
                continue
            ph.op("scalar", lambda e, pt=pt, bt=bt: e.activation(out=bt[:], in_=pt[:, 512:640], func=AF.Copy), r=[ptres], w=[btres, ptres])
            if CUT == 4:
                continue
            ph.op("tensor", lambda e, pst=pst, bt=bt, xd=xd: e.matmul(pst[:, :], bt[:], xd[:], start=True, stop=True),
                  r=[btres, xdres], w=[pstres])
            if CUT == 5:
                continue
            if ck == 0:
                ph.op("vector", lambda e, pst=pst: e.tensor_copy(out=hT[:], in_=pst[:, :]), r=[pstres], w=[R("hT")])
            else:
                ph.op("vector", lambda e, ck=ck, g=g: e.tensor_tensor(
                    out=hT[:].rearrange("p (h q) -> p h q", h=8), in0=hT[:].rearrange("p (h q) -> p h q", h=8),
                    in1=cdec[:, ck, 8 * g:8 * g + 8].unsqueeze(2).to_broadcast([128, 8, 64]), op=ALU.mult),
                    r=[R("hT"), R("cdec", ck)], w=[R("hT")])
                ph.op("vector", lambda e, pst=pst: e.tensor_tensor(out=hT[:], in0=hT[:], in1=pst[:, :], op=ALU.add),
                      r=[R("hT"), pstres], w=[R("hT")])
        ph.op("vector", lambda e, g=g: e.tensor_scalar(out=hmid[:, 512 * g:512 * (g + 1)], in0=hT[:], scalar1=c["flag"][:, 0:1],
                                                       scalar2=None, op0=ALU.mult), r=[R("hT")], w=[R("hmid", g)])
    ph.end()


def phase_ssd(kb, din, dout, dbg, hmid, y_aT, cvc):
    nc = kb.nc
    c = kb.c
    ph = Phase(kb, "sd")
    R = ph.R
    w_in = din["w_in"]
    xT = kb.xT
    xT_res = []
    normw_bc = ph.sb("normw", [128, 2048])
    ph.op("sync", lambda e: e.dma_start(out=normw_bc[:], in_=din["normw"].partition_broadcast(128)), w=[R("normw")], dma="nw")
    cvT = ph.sb("cvT", [128, 24, NS, 3])
    ph.op("sync", lambda e: e.dma_start(out=cvT[:].rearrange("p k b j -> p k (b j)"),
                                        in_=din["cv0T"].rearrange("p (k f) -> p k f", k=24)), w=[R("cvT")], dma="cvT")

    banks = [ph.ps(f"bk{i}", [128, 512]) for i in (0, 1, 4, 5, 6, 7)]
    bk = {0: banks[0], 1: banks[1], 4: banks[2], 5: banks[3], 6: banks[4], 7: banks[5]}
    bkt = {2: ph.ps("bk2", [128, 1024], BF16), 3: ph.ps("bk3", [128, 1024], BF16)}
    BR = lambda i: R("bank", i)

    tiles = [(3 + 128 * t, 128) for t in range(8)] + [(3 + NOWN, NS)]

    dtf = ph.sb("dtf", [128, 9, 32])
    a_t = ph.sb("a", [128, 9, 32])
    acs = ph.sb("acs", [128, 9, 32])
    eacs = ph.sb("eacs", [128, 9, 32])
    dd = ph.sb("dd", [128, 9, 32])
    cdec = ph.sb("cdec", [128, 9, 32])
    tmp32 = ph.sb("tmp32", [128, 32])
    na_t = ph.sb("na", [128, 9, 32])
    wdt, wdt_res = ph.wload(w_in[:, OFF_DT:OFF_DT + 32], 32)
    for tt, (t0, n) in enumerate(tiles):
        pb = bk[tt % 2]
        pres = BR(tt % 2)
        for k in range(16):
            ph.op("tensor", lambda e, k=k, t0=t0, n=n, pb=pb: e.matmul(pb[0:n, 0:32], xT[:, k, t0:t0 + n], wdt[:, k, 0:32],
                                                                        start=(k == 0), stop=(k == 15)),
                  r=[wdt_res] + xT_res, w=[pres], signal=(k == 15))
        dt_chain(ph, kb, "sd", n, pb[0:n, 0:32], pres, dtf[0:n, tt, :], a_t[0:n, tt, :], ("dt", tt))
        if tt < 8:
            ph.op("tensor", lambda e, tt=tt, pb=pb: e.matmul(pb[:, 32:64], c["trif"][:], a_t[:, tt, :], start=True, stop=True),
                  r=[R(("dt", tt), "a")], w=[pres])
            ph.op("tensor", lambda e, tt=tt, pb=pb: e.matmul(pb[:, 64:96], c["onesf"][:], a_t[:, tt, :], start=True, stop=True),
                  r=[R(("dt", tt), "a")], w=[pres])
            ph.op("vector", lambda e, tt=tt: e.tensor_scalar(out=na_t[:, tt, :], in0=a_t[:, tt, :], scalar1=-1.0, scalar2=None, op0=ALU.mult),
                  r=[R(("dt", tt), "a")], w=[R("na", tt)])
            ph.op("vector", lambda e, tt=tt, pb=pb: e.tensor_copy(out=acs[:, tt, :], in_=pb[:, 32:64]), r=[pres], w=[R("acs", tt)])
            ph.op("scalar", lambda e, tt=tt: e.activation(out=eacs[:, tt, :], in_=acs[:, tt, :], func=AF.Exp), r=[R("acs", tt)], w=[R("eacs", tt)])
            ph.op("scalar", lambda e, tt=tt, pb=pb: e.activation(out=cdec[:, tt, :], in_=pb[:, 64:96], func=AF.Exp), r=[pres], w=[R("cdec", tt)])
            ph.op("vector", lambda e, tt=tt, pb=pb: e.tensor_tensor(out=tmp32[:], in0=pb[:, 64:96], in1=acs[:, tt, :], op=ALU.subtract),
                  r=[pres, R("acs", tt)], w=[R("tmp32")])
            ph.op("scalar", lambda e, tt=tt: e.activation(out=dd[:, tt, :], in_=tmp32[:], func=AF.Exp), r=[R("tmp32")], w=[R("dd", tt)])
            ph.op("vector", lambda e, tt=tt: e.tensor_tensor(out=dd[:, tt, :], in0=dd[:, tt, :], in1=dtf[:, tt, :], op=ALU.mult),
                  r=[R("dd", tt), R(("dt", tt), "dtf")], w=[R("dd", tt)])
        else:
            ph.op("scalar", lambda e: e.activation(out=cdec[0:NS, 8, :], in_=a_t[0:NS, 8, :], func=AF.Exp), r=[R(("dt", 8), "a")], w=[R("cdec", 8)])

    rawst = ph.sb("rawst", [128, 2 * NEXT])
    raw_ = [rawst[:, 0:NEXT], rawst[:, NEXT:2 * NEXT]]
    ftc = [0]
    ph.st2 = rawst[:, 0:2048].rearrange("p (b k n) -> p b k n", b=4, k=4)
    acc = ph.sb("acc", [128, NT])
    accs = ph.sb("accs", [128, NS])
    xc = ph.sb("xc", [128, 4, NT], BF16)
    bT = ph.sb("bT", [128, NT], BF16)
    cT = ph.sb("cT", [128, NT], BF16)
    zs = ph.sb("zs", [128, 9, 512], BF16)
    Xb = [ph.sb(f"X{i}", [128, 512], BF16) for i in range(2)]
    Xd = [ph.sb(f"Xd{i}", [128, 512], BF16) for i in range(2)]
    ysk = [ph.sb(f"ysk{i}", [128, 512]) for i in range(2)]
    btm = [ph.sb(f"btm{i}", [128, 128], BF16) for i in range(2)]
    cbm_ = [ph.sb(f"cbm{i}", [128, 128]) for i in range(2)]
    Ebuf_ = [ph.sb(f"E{i}", [128, 8, 128]) for i in range(2)]
    MT_ = [ph.sb(f"MT{i}", [128, 8, 128], BF16) for i in range(2)]
    yt_ = [ph.sb(f"yt{i}", [128, 512]) for i in range(2)]
    junk = ph.sb("junk", [128, 512], BF16)
    ssum_ = [ph.sb(f"ssum{i}", [128, 1]) for i in range(2)]
    yn_ = [ph.sb(f"yn{i}", [128, 512], BF16) for i in range(2)]
    gn_cnt = [0]
    hT = ph.sb("hT", [128, 512])
    hTb = ph.sb("hTb", [128, 512], BF16)
    nsplit = split_even(NEXT, 512)
    mmc = 0

    def feat_tile(col0, ch_tile, out_bf, out_res):
        nonlocal mmc
        fi = ftc[0] % 2
        ftc[0] += 1
        raw = raw_[fi]
        rraw = R("raw", fi)
        wb, wres = ph.wload(w_in[:, col0:col0 + 128], 128)
        for (s0, sz) in nsplit:
            pb = bk[mmc % 2]
            pres = BR(mmc % 2)
            mmc += 1
            for k in range(16):
                ph.op("tensor", lambda e, k=k, s0=s0, sz=sz, pb=pb, wb=wb: e.matmul(pb[:, 0:sz], wb[:, k, 0:128], xT[:, k, s0:s0 + sz],
                                                                                  start=(k == 0), stop=(k == 15)),
                      r=[wres] + xT_res, w=[pres], signal=(k == 15))
            ph.op("scalar", lambda e, s0=s0, sz=sz, pb=pb: e.activation(out=raw[:, s0:s0 + sz], in_=pb[:, 0:sz], func=AF.Copy),
                  r=[pres], w=[rraw])
        ph.op("vector", lambda e: e.tensor_copy(out=cvc[:, ch_tile, :], in_=raw[:, NOWN:NEXT]), r=[rraw], w=[R("cvc", ch_tile)])
        conv_silu(ph, kb, raw, rraw, NT, c["cwT"][:, ch_tile, :], c["cbT"][:, ch_tile:ch_tile + 1], out_bf[:, 0:NT], out_res,
                  acc[:, :], R("acc"), 4, 3)
        wc = c["cwT"][:, ch_tile, :]
        ph.op("vector", lambda e: e.tensor_scalar(out=accs[:], in0=raw[:, 3 + NOWN:NEXT], scalar1=wc[:, 3:4], scalar2=None, op0=ALU.mult),
              r=[rraw], w=[R("accs")])
        for j in range(3):
            ph.op("vector", lambda e, j=j: e.scalar_tensor_tensor(out=accs[:], in0=cvT[:, ch_tile, :, j], scalar=wc[:, j:j + 1], in1=accs[:],
                                                               op0=ALU.mult, op1=ALU.add), r=[R("cvT"), R("accs")], w=[R("accs")])
        ph.op("scalar", lambda e: e.activation(out=out_bf[:, NOWN:NT], in_=accs[:], func=AF.Silu, bias=c["cbT"][:, ch_tile:ch_tile + 1], scale=1.0),
              r=[R("accs"), out_res], w=[out_res])

    def gate_norm(n, g, tt, y_ap, y_res, tok_lo):
        gi = gn_cnt[0] % 2
        gn_cnt[0] += 1
        ssum, yn = ssum_[gi], yn_[gi]
        rs, ry = R("ssum", gi), R("yn", gi)
        ph.op("vector", lambda e: e.tensor_tensor(out=y_ap, in0=y_ap, in1=zs[0:n, tt, :], op=ALU.mult), r=[y_res, R("zs", tt)], w=[y_res])
        ph.op("scalar", lambda e: e.activation(out=junk[0:n, :], in_=y_ap, func=AF.Square, accum_out=ssum[0:n, :]), r=[y_res], w=[rs])
        ph.op("vector", lambda e: e.tensor_scalar(out=ssum[0:n, :], in0=ssum[0:n, :], scalar1=1.0 / 512, scalar2=RMS_EPS, op0=ALU.mult, op1=ALU.add),
              r=[rs], w=[rs])
        ph.op("scalar", lambda e: e.activation(out=ssum[0:n, :], in_=ssum[0:n, :], func=AF.Sqrt), r=[rs], w=[rs])
        ph.op("vector", lambda e: e.reciprocal(out=ssum[0:n, :], in_=ssum[0:n, :]), r=[rs], w=[rs])
        ph.op("vector", lambda e: e.scalar_tensor_tensor(out=yn[0:n, :], in0=y_ap, scalar=ssum[0:n, 0:1], in1=normw_bc[0:n, 512 * g:512 * (g + 1)],
                                                         op0=ALU.mult, op1=ALU.mult), r=[y_res, rs, R("normw")], w=[ry])
        pt = bkt[3]
        for m in range(4):
            ph.op("tensor", lambda e, m=m: e.transpose(pt[:, m * 128:m * 128 + n], yn[0:n, m * 128:(m + 1) * 128], c["identb"][0:n, 0:n]),
                  r=[ry], w=[BR(3)], signal=(m == 3))
        ph.op("scalar", lambda e: e.activation(out=y_aT[:, 4 * g:4 * g + 4, tok_lo:tok_lo + n],
                                               in_=pt[:, 0:512].rearrange("p (m t) -> p m t", m=4)[:, :, 0:n], func=AF.Copy),
              r=[BR(3)], w=[R("yaT", g, tt)])

    for g in range(4):
        for m in range(4):
            feat_tile(OFF_XBC + 512 * g + 128 * m, 4 * g + m, xc[:, m, :], R("xc", m))
        feat_tile(OFF_XBC + 2048 + 128 * g, 16 + g, bT, R("bT"))
        feat_tile(OFF_XBC + 2560 + 128 * g, 20 + g, cT, R("cT"))
        wz = [ph.wload(w_in[:, OFF_Z + 512 * g + 256 * j:OFF_Z + 512 * g + 256 * (j + 1)], 256) for j in range(2)]
        for tt, (t0, n) in enumerate(tiles):
            pb = bk[mmc % 2]
            pres = BR(mmc % 2)
            mmc += 1
            for j in range(2):
                for k in range(16):
                    ph.op("tensor", lambda e, k=k, j=j, t0=t0, n=n, pb=pb: e.matmul(pb[0:n, 256 * j:256 * (j + 1)], xT[:, k, t0:t0 + n], wz[j][0][:, k, 0:256],
                                                                                   start=(k == 0), stop=(k == 15)),
                          r=[wz[j][1]] + xT_res, w=[pres], signal=(k == 15 and j == 1))
            ph.op("scalar", lambda e, tt=tt, n=n, pb=pb: e.activation(out=zs[0:n, tt, :], in_=pb[0:n, :], func=AF.Silu), r=[pres], w=[R("zs", tt)])
        ph.op("vector", lambda e, g=g: e.tensor_copy(out=hT[:], in_=hmid[:, 512 * g:512 * (g + 1)]), w=[R("hT")])
        ph.op("scalar", lambda e: e.activation(out=hTb[:], in_=hT[:], func=AF.Copy), r=[R("hT")], w=[R("hTb")])
        def front(ck, g=g):
            tsl = slice(ck * 128, (ck + 1) * 128)
            i2 = ck % 2
            pt = bkt[2]
            cbm, Ebuf, MT = cbm_[i2], Ebuf_[i2], MT_[i2]
            rcbm, rE, rMT = R("cbm", i2), R("E", i2), R("MT", i2)
            for m in range(4):
                ph.op("tensor", lambda e, m=m: e.transpose(pt[:, m * 128:(m + 1) * 128], xc[:, m, tsl], c["identb"][:]),
                      r=[R("xc", m)], w=[BR(2)], signal=False)
            ph.op("tensor", lambda e: e.transpose(pt[:, 512:640], bT[:, tsl], c["identb"][:]), r=[R("bT")], w=[BR(2)])
            ptx = pt[:, 0:512].rearrange("p (h q) -> p h q", h=8)

            def bc(t):
                return t[:, ck, 8 * g:8 * g + 8].unsqueeze(2).to_broadcast([128, 8, 64])
            ph.op("vector", lambda e: e.tensor_tensor(out=Xb[i2][:].rearrange("p (h q) -> p h q", h=8), in0=ptx, in1=bc(dtf), op=ALU.mult),
                  r=[BR(2), R(("dt", ck), "dtf")], w=[R("X", i2)])
            ph.op("vector", lambda e: e.tensor_tensor(out=Xd[i2][:].rearrange("p (h q) -> p h q", h=8), in0=ptx, in1=bc(dd), op=ALU.mult),
                  r=[BR(2), R("dd", ck)], w=[R("Xd", i2)])
            ph.op("vector", lambda e: e.tensor_tensor(out=ysk[i2][:].rearrange("p (h q) -> p h q", h=8), in0=ptx,
                                                      in1=c["D_bc"][:, 8 * g:8 * g + 8].unsqueeze(2).to_broadcast([128, 8, 64]), op=ALU.mult),
                  r=[BR(2)], w=[R("ysk", i2)])
            ph.op("scalar", lambda e: e.activation(out=btm[i2][:], in_=pt[:, 512:640], func=AF.Copy), r=[BR(2)], w=[R("btm", i2)])
            ph.op("tensor", lambda e: e.matmul(bk[1][:, 0:128], bT[:, tsl], cT[:, tsl], start=True, stop=True),
                  r=[R("bT"), R("cT")], w=[BR(1)])
            ph.op("vector", lambda e: e.tensor_tensor(out=cbm[:], in0=bk[1][:, 0:128], in1=c["trif"][:], op=ALU.mult), r=[BR(1)], w=[rcbm])
            for hh in range(8):
                b_i = 4 + hh // 4
                reg = bk[b_i][:, (hh % 4) * 128:(hh % 4 + 1) * 128]
                col = 8 * g + hh
                ph.op("tensor", lambda e, reg=reg, col=col: e.matmul(reg, a_t[:, ck, col:col + 1].to_broadcast([128, 128]), c["trif"][:],
                                                                     start=True, stop=False),
                      r=[R(("dt", ck), "a")], w=[BR(b_i)], signal=False)
                ph.op("tensor", lambda e, reg=reg, col=col: e.matmul(reg, c["trif"][:], na_t[:, ck, col:col + 1].to_broadcast([128, 128]),
                                                                     start=False, stop=False),
                      r=[R("na", ck)], w=[BR(b_i)], signal=False)
                ph.op("tensor", lambda e, reg=reg: e.matmul(reg, c["identf"][:], c["negm"][:], start=False, stop=True),
                      w=[BR(b_i)], signal=(hh % 4 == 3))
            for q in range(2):
                ph.op("scalar", lambda e, q=q: e.activation(out=Ebuf[:, 4 * q:4 * q + 4, :].rearrange("p h l -> p (h l)"), in_=bk[4 + q][:, :], func=AF.Exp),
                      r=[BR(4 + q)], w=[rE])
            ph.op("vector", lambda e: e.tensor_tensor(out=MT[:], in0=Ebuf[:], in1=cbm[:].unsqueeze(1).to_broadcast([128, 8, 128]), op=ALU.mult),
                  r=[rE, rcbm], w=[rMT])

        def back(ck, g=g):
            tsl = slice(ck * 128, (ck + 1) * 128)
            i2 = ck % 2
            MT, yt = MT_[i2], yt_[i2]
            rMT, ryt = R("MT", i2), R("yt", i2)

            def bc(t):
                return t[:, ck, 8 * g:8 * g + 8].unsqueeze(2).to_broadcast([128, 8, 64])
            for hh in range(8):
                ph.op("tensor", lambda e, hh=hh: e.matmul(bk[6][:, hh * 64:(hh + 1) * 64], MT[:, hh, :], Xb[i2][:, hh * 64:(hh + 1) * 64],
                                                          start=True, stop=True), r=[rMT, R("X", i2)], w=[BR(6)], signal=(hh == 7))
            ph.op("tensor", lambda e: e.matmul(bk[7][:, :], cT[:, tsl], hTb[:], start=True, stop=True), r=[R("cT"), R("hTb")], w=[BR(7)])
            ph.op("vector", lambda e: e.tensor_tensor(out=yt[:].rearrange("p (h q) -> p h q", h=8), in0=bk[7][:, :].rearrange("p (h q) -> p h q", h=8),
                                                      in1=bc(eacs), op=ALU.mult), r=[BR(7), R("eacs", ck)], w=[ryt])
            ph.op("vector", lambda e: e.tensor_tensor(out=yt[:], in0=yt[:], in1=bk[6][:, :], op=ALU.add), r=[ryt, BR(6)], w=[ryt])
            ph.op("vector", lambda e: e.tensor_tensor(out=yt[:], in0=yt[:], in1=ysk[i2][:], op=ALU.add), r=[ryt, R("ysk", i2)], w=[ryt])
            gate_norm(128, g, ck, yt[:, :], ryt, ck * 128)
            ph.op("tensor", lambda e: e.matmul(bk[0][:, :], btm[i2][:], Xd[i2][:], start=True, stop=True), r=[R("btm", i2), R("Xd", i2)], w=[BR(0)])
            ph.op("vector", lambda e: e.tensor_tensor(out=hT[:].rearrange("p (h q) -> p h q", h=8), in0=hT[:].rearrange("p (h q) -> p h q", h=8),
                                                      in1=bc(cdec), op=ALU.mult), r=[R("hT"), R("cdec", ck)], w=[R("hT")])
            ph.op("vector", lambda e: e.tensor_tensor(out=hT[:], in0=hT[:], in1=bk[0][:, :], op=ALU.add), r=[R("hT"), BR(0)], w=[R("hT")])
            ph.op("scalar", lambda e: e.activation(out=hTb[:], in_=hT[:], func=AF.Copy), r=[R("hT")], w=[R("hTb")])

        front(0)
        for ck in range(8):
            if ck < 7:
                front(ck + 1)
            back(ck)
        for m in range(4):
            ph.op("tensor", lambda e, m=m: e.transpose(bk[6][:, m * 128:(m + 1) * 128], hT[:, m * 128:(m + 1) * 128], c["identf"][:]),
                  r=[R("hT")], w=[BR(6)], signal=(m == 3))
        hfs = yt_[1]
        ph.op("vector", lambda e: e.tensor_copy(out=hfs[:], in_=bk[6][:, :]), r=[BR(6)], w=[R("yt", 1)])
        ph.op("sync", lambda e, g=g: e.dma_start(out=dout["hfin"][512 * g:512 * (g + 1), :].rearrange("(m p) n -> p m n", p=128),
                                                 in_=hfs[:].rearrange("p (m n) -> p m n", m=4)),
              r=[R("yt", 1)], w=[R("yt", 1)], dma="hfst")
        sample_ssd(ph, kb, din, dout, g, xc, bT, cT, dtf, cdec, bk, bkt, BR, gate_norm)
    ph.end()


def sample_ssd(ph, kb, din, dout, g, xc, bT, cT, dtf, cdec, bk, bkt, BR, gate_norm):
    c = kb.c
    R = ph.R
    S = slice(NOWN, NT)
    if g == 0:
        ph.sXs = ph.sb("sXs", [NS, 512])
        ph.sdA = ph.sb("sdA", [NS, 512])
        ph.sysk = ph.sb("sysk", [NS, 512])
        ph.sB = ph.sb("sB", [NS, 128], BF16)
        ph.sC = ph.sb("sC", [NS, 128], BF16)
        ph.sXT = ph.sb("sXT", [128, 4, NS])
        ph.sdAT = ph.sb("sdAT", [128, 4, NS])
        ph.sbd = ph.sb("sbd", [NS, 2, 4, 128], BF16)
        ph.sh0 = ph.sb("sh0", [128, 4, 4, 128])
        ph.syT = ph.sb("syT", [128, 4, NS])
        ph.sy = ph.sdA
    sXs, sdA, sysk, sB, sC, sXT, sdAT, sbd, sh0, st2, syT, sy = (ph.sXs, ph.sdA, ph.sysk, ph.sB, ph.sC, ph.sXT, ph.sdAT,
                                                                ph.sbd, ph.sh0, ph.st2, ph.syT, ph.sy)
    pt = bkt[2]
    for m in range(4):
        ph.op("tensor", lambda e, m=m: e.transpose(pt[0:NS, m * 128:(m + 1) * 128], xc[:, m, S], c["identb"][:]), r=[R("xc", m)], w=[BR(2)], signal=False)
    ph.op("tensor", lambda e: e.transpose(pt[0:NS, 512:640], bT[:, S], c["identb"][:]), r=[R("bT")], w=[BR(2)], signal=False)
    ph.op("tensor", lambda e: e.transpose(pt[0:NS, 640:768], cT[:, S], c["identb"][:]), r=[R("cT")], w=[BR(2)])
    ptx = pt[0:NS, 0:512].rearrange("p (h q) -> p h q", h=8)

    def bc(t):
        return t[0:NS, 8, 8 * g:8 * g + 8].unsqueeze(2).to_broadcast([NS, 8, 64])
    ph.op("vector", lambda e: e.tensor_tensor(out=sXs[:].rearrange("p (h q) -> p h q", h=8), in0=ptx, in1=bc(dtf), op=ALU.mult),
          r=[BR(2), R(("dt", 8), "dtf")], w=[R("sXs")])
    ph.op("vector", lambda e: e.tensor_tensor(out=sysk[:].rearrange("p (h q) -> p h q", h=8), in0=ptx,
                                              in1=c["D_bc"][0:NS, 8 * g:8 * g + 8].unsqueeze(2).to_broadcast([NS, 8, 64]), op=ALU.mult),
          r=[BR(2)], w=[R("sysk")])
    ph.op("vector", lambda e: e.tensor_copy(out=sdA[:].rearrange("p (h q) -> p h q", h=8), in_=bc(cdec)), r=[R("cdec", 8)], w=[R("sdA")])
    ph.op("scalar", lambda e: e.activation(out=sB[:], in_=pt[0:NS, 512:640], func=AF.Copy), r=[BR(2)], w=[R("sB")])
    ph.op("scalar", lambda e: e.activation(out=sC[:], in_=pt[0:NS, 640:768], func=AF.Copy), r=[BR(2)], w=[R("sC")])
    for m in range(4):
        ph.op("tensor", lambda e, m=m: e.transpose(bk[4][:, m * NS:(m + 1) * NS], sXs[:, m * 128:(m + 1) * 128], c["identf"][0:NS, 0:NS]),
              r=[R("sXs")], w=[BR(4)], signal=False)
    for m in range(4):
        ph.op("tensor", lambda e, m=m: e.transpose(bk[4][:, 64 + m * NS:64 + (m + 1) * NS], sdA[:, m * 128:(m + 1) * 128], c["identf"][0:NS, 0:NS]),
              r=[R("sdA")], w=[BR(4)], signal=(m == 3))
    ph.op("vector", lambda e: e.tensor_copy(out=sXT[:].rearrange("p m b -> p (m b)"), in_=bk[4][:, 0:64]), r=[BR(4)], w=[R("sXT")])
    ph.op("vector", lambda e: e.tensor_copy(out=sdAT[:].rearrange("p m b -> p (m b)"), in_=bk[4][:, 64:128]), r=[BR(4)], w=[R("sdAT")])
    for hb in range(4):
        b0 = hb * 4
        for which, src in ((0, sB), (1, sC)):
            ph.op("vector", lambda e, which=which, src=src, b0=b0: e.tensor_tensor(
                out=sbd[:, which, :, :], in0=src[:].unsqueeze(1).to_broadcast([NS, 4, 128]),
                in1=c["identf"][0:NS, b0:b0 + 4].unsqueeze(2).to_broadcast([NS, 4, 128]), op=ALU.mult),
                r=[R("sB" if which == 0 else "sC")], w=[R("sbd", which)])
        for which, bi in ((0, 4), (1, 6)):
            ph.op("tensor", lambda e, which=which, bi=bi: e.matmul(bk[bi][:, :], c["onesb"][0:NS, :],
                                                                 sbd[:, which, :, :].rearrange("p b n -> p (b n)"),
                                                                 start=True, stop=True), r=[R("sbd", which)], w=[BR(bi)])
        for bb in range(4):
            o_ = ph.op("sync", lambda e, b0=b0, bb=bb: e.dma_start(out=sh0[:, bb, :, :], in_=din["ssm0"][b0 + bb, 512 * g:512 * (g + 1), :].rearrange("(k p) n -> p k n", p=128)),
                       w=([R("sh0")] if bb == 0 else []), dma="sh0")
        R("sh0").lw = o_

        def v84(t, b0=b0):
            return t[:, :, b0:b0 + 4].rearrange("p k b -> p b k").unsqueeze(3).to_broadcast([128, 4, 4, 128])

        def pbc(bi):
            return bk[bi][:, :].rearrange("p (b n) -> p b n", b=4).unsqueeze(2).to_broadcast([128, 4, 4, 128])
        ph.op("gpsimd", lambda e, v84=v84: e.tensor_tensor(out=sh0[:], in0=sh0[:], in1=v84(sdAT), op=ALU.mult), r=[R("sh0"), R("sdAT")], w=[R("sh0")])
        ph.op("vector", lambda e, v84=v84, pbc=pbc: e.tensor_tensor(out=st2[:], in0=pbc(4), in1=v84(sXT), op=ALU.mult),
              r=[BR(4), R("sXT")], w=[R("raw", 0), R("raw", 1)])
        ph.op("gpsimd", lambda e: e.tensor_tensor(out=sh0[:], in0=sh0[:], in1=st2[:], op=ALU.add), r=[R("sh0"), R("raw", 0), R("raw", 1)], w=[R("sh0")])
        for bb in range(4):
            ph.op("sync", lambda e, b0=b0, bb=bb: e.dma_start(out=dout["hs"][b0 + bb, 512 * g:512 * (g + 1), :].rearrange("(k p) n -> p k n", p=128), in_=sh0[:, bb, :, :]),
                  r=[R("sh0")], dma="hsst")
        ph.op("vector", lambda e, pbc=pbc: e.tensor_tensor(out=st2[:], in0=pbc(6), in1=sh0[:], op=ALU.mult),
              r=[BR(6), R("sh0")], w=[R("raw", 0), R("raw", 1)])
        ph.op("vector", lambda e, b0=b0: e.reduce_sum(out=syT[:, :, b0:b0 + 4].rearrange("p k b -> p b k"), in_=st2[:], axis=AX.X),
              r=[R("raw", 0), R("raw", 1)], w=[R("syT")])
    for m in range(4):
        ph.op("tensor", lambda e, m=m: e.transpose(bk[0][0:NS, m * 128:(m + 1) * 128], syT[:, m, :], c["identf"][:]), r=[R("syT")], w=[BR(0)], signal=(m == 3))
    ph.op("vector", lambda e: e.tensor_tensor(out=sy[:], in0=bk[0][0:NS, :], in1=sysk[:], op=ALU.add), r=[BR(0), R("sysk")], w=[R("sdA")])
    gate_norm(NS, g, 8, sy[:, :], R("sdA"), NOWN)


def dbg_dump(kb, dst, src):
    ph = Phase(kb, "dbg%d" % Prog.UID)
    for k in range(16):
        ph.op("sync", lambda e, k=k: e.dma_start(out=dst[k * 128:(k + 1) * 128, :], in_=src[:, k, :]), dma="st")
    ph.end()


def load_xT(ph, din):
    return ph.kb.xT, []


def mm16(ph, out_ap, out_res, wb, wres, ncols, rhs_fn, rhs_res):
    for k in range(16):
        ph.op("tensor", lambda e, k=k: e.matmul(out_ap, wb[:, k, 0:ncols], rhs_fn(k), start=(k == 0), stop=(k == 15)),
              r=[wres] + list(rhs_res), w=[out_res], signal=(k == 15))


def phase_sc(kb, din, y_bT, ucc):
    c = kb.c
    ph = Phase(kb, "sc")
    R = ph.R
    w_in = din["w_in"]
    xT, xT_res = load_xT(ph, din)
    scT = ph.sb("scT", [128, 16, NS, 2])
    ph.op("sync", lambda e: e.dma_start(out=scT[:].rearrange("p k b j -> p k (b j)"), in_=din["sc0T"].rearrange("p (k f) -> p k f", k=16)),
          w=[R("scT")], dma="scT")
    craw = ph.sb("craw", [128, NEXT])
    ubuf = ph.sb("ubuf", [128, NEXT])
    acc = ph.sb("acc", [128, NT])
    banks = [ph.ps(f"bk{i}", [128, 512]) for i in range(4)]
    bc_ = 0
    ns_ext = split_even(NEXT, 512)
    ns_own = split_even(NT, 512)
    for i in range(16):
        wc, wc_r = ph.wload(w_in[:, OFF_SCC + 128 * i:OFF_SCC + 128 * (i + 1)], 128)
        wh, wh_r = ph.wload(w_in[:, OFF_SCH + 128 * i:OFF_SCH + 128 * (i + 1)], 128)
        for (s0, sz) in ns_ext:
            pb, pr = banks[bc_ % 4], R("bank", bc_ % 4)
            bc_ += 1
            mm16(ph, pb[:, 0:sz], pr, wc, wc_r, 128, lambda k, s0=s0, sz=sz: xT[:, k, s0:s0 + sz], xT_res)
            ph.op("scalar", lambda e: e.activation(out=craw[:, s0:s0 + sz], in_=pb[:, 0:sz], func=AF.Copy), r=[pr], w=[R("craw")])
        for (s0, sz) in ns_ext:
            pb, pr = banks[bc_ % 4], R("bank", bc_ % 4)
            bc_ += 1
            mm16(ph, pb[:, 0:sz], pr, wh, wh_r, 128, lambda k, s0=s0, sz=sz: xT[:, k, s0:s0 + sz], xT_res)
            ph.op("vector", lambda e: e.tensor_tensor(out=ubuf[:, s0:s0 + sz], in0=craw[:, s0:s0 + sz], in1=pb[:, 0:sz], op=ALU.mult),
                  r=[pr, R("craw")], w=[R("ubuf")])
        wb_, wb_r = ph.wload(w_in[:, OFF_SCB + 128 * i:OFF_SCB + 128 * (i + 1)], 128)
        ph.op("scalar", lambda e: e.activation(out=ucc[:, i, :], in_=ubuf[:, NOWN + 1:NEXT], func=AF.Copy), r=[R("ubuf")], w=[R("ucc", i)])
        conv_silu(ph, kb, ubuf, R("ubuf"), NT, c["swT"][:, i, :], None, None, None, acc[:, :], R("acc"), 3, 3)
        wsc = c["swT"][:, i, :]
        ph.op("vector", lambda e: e.tensor_scalar(out=acc[:, NOWN:NT], in0=ubuf[:, 3 + NOWN:NEXT], scalar1=wsc[:, 2:3], scalar2=None, op0=ALU.mult),
              r=[R("ubuf"), R("acc")], w=[R("acc")])
        for j in range(2):
            ph.op("vector", lambda e, j=j: e.scalar_tensor_tensor(out=acc[:, NOWN:NT], in0=scT[:, i, :, j], scalar=wsc[:, j:j + 1], in1=acc[:, NOWN:NT],
                                                               op0=ALU.mult, op1=ALU.add), r=[R("scT"), R("acc")], w=[R("acc")])
        for (s0, sz) in ns_own:
            pb, pr = banks[bc_ % 4], R("bank", bc_ % 4)
            bc_ += 1
            mm16(ph, pb[:, 0:sz], pr, wb_, wb_r, 128, lambda k, s0=s0, sz=sz: xT[:, k, 3 + s0:3 + s0 + sz], xT_res)
            ph.op("vector", lambda e: e.tensor_tensor(out=y_bT[:, i, s0:s0 + sz], in0=acc[:, s0:s0 + sz], in1=pb[:, 0:sz], op=ALU.mult),
                  r=[pr, R("acc")], w=[R("ybT", i)])
    ph.end()


def phase_mix(kb, din, y_aT, y_bT, mixT):
    ph = Phase(kb, "mx")
    R = ph.R
    w_in = din["w_in"]
    xT, xT_res = load_xT(ph, din)
    sga = ph.sb("sga", [128, NT])
    sgb = ph.sb("sgb", [128, NT])
    mtmp = ph.sb("mtmp", [128, NT])
    mout = [ph.sb(f"mout{i}", [128, NT], BF16) for i in range(2)]
    banks = [ph.ps(f"bk{i}", [128, 512]) for i in range(4)]
    bc_ = 0
    ns_own = split_even(NT, 512)
    for i in range(16):
        cs = slice(128 * i, 128 * (i + 1))
        wga, wga_r = ph.wload(w_in[:, OFF_GA + 128 * i:OFF_GA + 128 * (i + 1)], 128)
        wgb, wgb_r = ph.wload(w_in[:, OFF_GB + 128 * i:OFF_GB + 128 * (i + 1)], 128)
        for (wt, wr, dst, dres) in ((wga, wga_r, sga, "sga"), (wgb, wgb_r, sgb, "sgb")):
            for (s0, sz) in ns_own:
                pb, pr = banks[bc_ % 4], R("bank", bc_ % 4)
                bc_ += 1
                mm16(ph, pb[:, 0:sz], pr, wt, wr, 128, lambda k, s0=s0, sz=sz: xT[:, k, 3 + s0:3 + s0 + sz], xT_res)
                ph.op("scalar", lambda e: e.activation(out=dst[:, s0:s0 + sz], in_=pb[:, 0:sz], func=AF.Sigmoid), r=[pr], w=[R(dres)])
        wa, wa_r = ph.wload(din["w_bssd"][:, cs], 128)
        for (s0, sz) in ns_own:
            pb, pr = banks[bc_ % 4], R("bank", bc_ % 4)
            bc_ += 1
            mm16(ph, pb[:, 0:sz], pr, wa, wa_r, 128, lambda k, s0=s0, sz=sz: y_aT[:, k, s0:s0 + sz], [])
            ph.op("vector", lambda e: e.tensor_tensor(out=mtmp[:, s0:s0 + sz], in0=sga[:, s0:s0 + sz], in1=pb[:, 0:sz], op=ALU.mult),
                  r=[pr, R("sga")], w=[R("mtmp")])
        wb_, wb_r = ph.wload(din["w_bsc"][:, cs], 128)
        for (s0, sz) in ns_own:
            pb, pr = banks[bc_ % 4], R("bank", bc_ % 4)
            bc_ += 1
            mm16(ph, pb[:, 0:sz], pr, wb_, wb_r, 128, lambda k, s0=s0, sz=sz: y_bT[:, k, s0:s0 + sz], [])
            ph.op("vector", lambda e: e.tensor_tensor(out=sgb[:, s0:s0 + sz], in0=sgb[:, s0:s0 + sz], in1=pb[:, 0:sz], op=ALU.mult),
                  r=[pr, R("sgb")], w=[R("sgb")])
        mo, mr = mout[i % 2], R("mout", i % 2)
        ph.op("vector", lambda e: e.tensor_tensor(out=mo[:], in0=mtmp[:], in1=sgb[:], op=ALU.add), r=[R("mtmp"), R("sgb")], w=[mr])
        ph.op("sync", lambda e: e.dma_start(out=mixT[128 * i:128 * (i + 1), :], in_=mo[:]), r=[mr], dma=f"mo{i % 2}")
    ph.end()


def ln_feature_major(ph, kb, pre, pre_res_fn, S1b, S2b, tbuf, tres, emit_fn, tag):
    R = ph.R
    ns = split_even(NT, 512)
    mean = ph.sb(tag + "mean", [128, NT])
    rstd = ph.sb(tag + "rstd", [128, NT])
    for j, (s0, sz) in enumerate(ns):
        ph.op("vector", lambda e: e.tensor_scalar(out=mean[:, s0:s0 + sz], in0=S1b[j][0][:, 0:sz], scalar1=1.0 / D, scalar2=None, op0=ALU.mult),
              r=[S1b[j][1]], w=[R(tag + "mean")])
        ph.op("vector", lambda e: e.tensor_tensor(out=rstd[:, s0:s0 + sz], in0=mean[:, s0:s0 + sz], in1=mean[:, s0:s0 + sz], op=ALU.mult),
              r=[R(tag + "mean")], w=[R(tag + "rstd")])
        ph.op("vector", lambda e: e.scalar_tensor_tensor(out=rstd[:, s0:s0 + sz], in0=S2b[j][0][:, 0:sz], scalar=1.0 / D, in1=rstd[:, s0:s0 + sz],
                                                         op0=ALU.mult, op1=ALU.subtract), r=[S2b[j][1], R(tag + "rstd")], w=[R(tag + "rstd")])
    ph.op("vector", lambda e: e.tensor_scalar(out=rstd[:], in0=rstd[:], scalar1=LN_EPS, scalar2=None, op0=ALU.add), r=[R(tag + "rstd")], w=[R(tag + "rstd")])
    ph.op("scalar", lambda e: e.activation(out=rstd[:], in_=rstd[:], func=AF.Sqrt), r=[R(tag + "rstd")], w=[R(tag + "rstd")])
    ph.op("vector", lambda e: e.reciprocal(out=rstd[:], in_=rstd[:]), r=[R(tag + "rstd")], w=[R(tag + "rstd")])
    for i in range(16):
        t = tbuf[i % 2]
        tr = tres[i % 2]
        ph.op("vector", lambda e: e.tensor_tensor(out=t[:], in0=pre[:, i, :], in1=mean[:], op=ALU.subtract), r=[pre_res_fn(i), R(tag + "mean")], w=[tr])
        ph.op("vector", lambda e: e.tensor_tensor(out=t[:], in0=t[:], in1=rstd[:], op=ALU.mult), r=[tr, R(tag + "rstd")], w=[tr])
        emit_fn(i, t, tr)


def ln_stats_accum(ph, kb, i, src_ap_fn, src_res, sq, sq_res, S1b, S2b):
    c = kb.c
    ns = split_even(NT, 512)
    ph.op("scalar", lambda e: e.activation(out=sq[:], in_=src_ap_fn(0, NT), func=AF.Square), r=[src_res], w=[sq_res])
    for j, (s0, sz) in enumerate(ns):
        ph.op("tensor", lambda e: e.matmul(S1b[j][0][:, 0:sz], c["onesf"][:], src_ap_fn(s0, sz), start=(i == 0), stop=(i == 15)),
              r=[src_res], w=[S1b[j][1]], signal=(i == 15))
        ph.op("tensor", lambda e: e.matmul(S2b[j][0][:, 0:sz], c["onesf"][:], sq[:, s0:s0 + sz], start=(i == 0), stop=(i == 15)),
              r=[sq_res], w=[S2b[j][1]], signal=True)


def phase_ln1(kb, din, mixs, x1T, x1s, pre):
    ph = Phase(kb, "l1")
    R = ph.R
    mixT = ph.sb("mixT", [128, 16, NT], BF16)
    for q in range(4):
        ph.op("sync", lambda e, q=q: e.dma_start(out=mixT[:, 4 * q:4 * q + 4, :], in_=mixs.rearrange("(k p) t -> p k t", p=128)[:, 4 * q:4 * q + 4, :]),
              w=[R("mixT", q)], dma=f"mx{q}")
    mix_res = [R("mixT", q) for q in range(4)]
    xf = [ph.sb(f"xf{i}", [128, NT]) for i in range(2)]
    sq = [ph.sb(f"sq{i}", [128, NT]) for i in range(2)]
    gb = ph.sb("gb", [128, 4, 16])
    ph.op("sync", lambda e: e.dma_start(out=gb[:, 0, :], in_=din["ln1gT"][:, :]), w=[R("gb")], dma="g0")
    ph.op("sync", lambda e: e.dma_start(out=gb[:, 1, :], in_=din["ln1bT"][:, :]), w=[R("gb")], dma="g1")
    ph.op("vector", lambda e: e.tensor_scalar(out=gb[:, 2:4, :], in0=gb[:, 0:2, :], scalar1=ALPHA, scalar2=None, op0=ALU.mult), r=[R("gb")], w=[R("gb")])
    banks = [ph.ps(f"bk{i}", [128, 512]) for i in range(2)]
    S1b = [(ph.ps(f"s1_{j}", [128, 512]), R("S1", j)) for j in range(3)]
    S2b = [(ph.ps(f"s2_{j}", [128, 512]), R("S2", j)) for j in range(3)]
    ns = split_even(NT, 512)
    bc_ = 0
    for i in range(16):
        w, wr = ph.wload(din["w_out"][:, 128 * i:128 * (i + 1)], 128)
        xfi, xfr = xf[i % 2], R("xf", i % 2)
        ph.op("sync", lambda e: e.dma_start(out=xfi[:], in_=din["xextT"][128 * i:128 * (i + 1), 3:NEXT]), w=[xfr], dma=f"xf{i % 2}")
        for (s0, sz) in ns:
            pb, pr = banks[bc_ % 2], R("bank", bc_ % 2)
            bc_ += 1
            mm16(ph, pb[:, 0:sz], pr, w, wr, 128, lambda k, s0=s0, sz=sz: mixT[:, k, s0:s0 + sz], mix_res)
            ph.op("vector", lambda e: e.scalar_tensor_tensor(out=pre[:, i, s0:s0 + sz], in0=xfi[:, s0:s0 + sz], scalar=ALPHA, in1=pb[:, 0:sz],
                                                             op0=ALU.mult, op1=ALU.add), r=[pr, xfr], w=[R("pre", i)])
        ln_stats_accum(ph, kb, i, lambda s0, sz: pre[:, i, s0:s0 + sz], R("pre", i), sq[i % 2], R("sq", i % 2), S1b, S2b)
    stg = sq

    def emit_fn(i, t, tr):
        ph.op("scalar", lambda e: e.activation(out=x1T[:, i, :], in_=t[:], func=AF.Identity, scale=gb[:, 0, i:i + 1], bias=gb[:, 1, i:i + 1]),
              r=[tr, R("gb")], w=[R("x1T", i)])
        st, sr = stg[i % 2], R("sq", i % 2)
        ph.op("scalar", lambda e: e.activation(out=st[:], in_=t[:], func=AF.Identity, scale=gb[:, 2, i:i + 1], bias=gb[:, 3, i:i + 1]),
              r=[tr, R("gb")], w=[sr])
        ph.op("sync", lambda e: e.dma_start(out=x1s[128 * i:128 * (i + 1), :], in_=st[:]), r=[sr], dma=f"stg{i % 2}")
    ln_feature_major(ph, kb, pre, lambda i: R("pre", i), S1b, S2b, xf, [R("xf", 0), R("xf", 1)], emit_fn, "ln")
    ph.end()


def phase_route(kb, din, x1T, gsc, accT):
    c = kb.c
    ph = Phase(kb, "rt")
    R = ph.R
    tiles = [(128 * t, 128) for t in range(8)] + [(NOWN, NS)]
    ns = split_even(NT, 512)
    keys = ph.sb("keys", [128, 8, 2, 128], BF16)
    ph.op("gpsimd", lambda e: e.dma_start(out=keys[:, :, 0, :], in_=din["k1T"].rearrange("h d k -> d h k")), w=[R("keys")], dma="k1")
    ph.op("gpsimd", lambda e: e.dma_start(out=keys[:, :, 1, :], in_=din["k2T"].rearrange("h d k -> d h k")), w=[R("keys")], dma="k2")
    iota = ph.sb("iota", [128, 256])
    ph.op("sync", lambda e: e.dma_start(out=iota[:], in_=din["c_iota"].partition_broadcast(128)), w=[R("iota")], dma="io")
    iota_rep = ph.sb("iota_rep", [128, 32, 128], BF16)
    ph.op("vector", lambda e: e.tensor_copy(out=iota_rep[:], in_=iota[:, 0:128].unsqueeze(1).to_broadcast([128, 32, 128])), r=[R("iota")], w=[R("iota_rep")])
    thr16 = ph.sb("thr16", [128, 16])
    ph.op("vector", lambda e: e.tensor_scalar(out=thr16[:], in0=iota[:, 0:16], scalar1=1.0, scalar2=16.0, op0=ALU.add, op1=ALU.mult), r=[R("iota")], w=[R("thr16")])
    TK = ph.sb("TK", [128, 9, 8, 4, 16])
    qT = ph.sb("qT", [128, 2, NT], BF16)
    sc_ = [ph.sb(f"sc{i}", [128, 256]) for i in range(2)]
    sc2_ = [ph.sb(f"sc2{i}", [128, 256]) for i in range(2)]
    jua = ph.sb("jua", [128, 9, 8, 2, 16], U32)
    juc = ph.sb("juc", [128, 8, 16], U32)
    bk = [ph.ps(f"bk{i}", [128, 512]) for i in range(8)]
    BR = lambda i: R("bank", i)
    qc = 0
    scn = 0

    def top16_ops(n, src, src2, tv, jo, res_src, res_src2, res_out, res_j):
        return [
            lambda: ph.op("vector", lambda e: e.max(out=tv[:, 0:8], in_=src), r=[res_src], w=[res_out]),
            lambda: ph.op("vector", lambda e: e.max_index(out=jo[:, 0:8], in_max=tv[:, 0:8], in_values=src), r=[res_src, res_out], w=[res_j]),
            lambda: ph.op("vector", lambda e: e.match_replace(out=src2, in_to_replace=tv[:, 0:8], in_values=src, imm_value=-1e30), r=[res_src, res_out], w=[res_src2]),
            lambda: ph.op("vector", lambda e: e.max(out=tv[:, 8:16], in_=src2), r=[res_src2], w=[res_out]),
            lambda: ph.op("vector", lambda e: e.max_index(out=jo[:, 8:16], in_max=tv[:, 8:16], in_values=src2), r=[res_src2, res_out], w=[res_j]),
        ]

    def interleave(*chains):
        for group in zip(*chains):
            for th in group:
                th()

    for h in range(8):
        wqb, wqr = ph.wload(din["wq"][:, 256 * h:256 * (h + 1)], 256)
        for half in range(2):
            for (s0, sz) in ns:
                pb, pr = bk[qc % 2], BR(qc % 2)
                qc += 1
                for k in range(16):
                    ph.op("tensor", lambda e, k=k: e.matmul(pb[:, 0:sz], wqb[:, k, 128 * half:128 * (half + 1)], x1T[:, k, s0:s0 + sz],
                                                            start=(k == 0), stop=(k == 15)), r=[wqr], w=[pr], signal=(k == 15))
                ph.op("scalar", lambda e: e.activation(out=qT[:, half, s0:s0 + sz], in_=pb[:, 0:sz], func=AF.Copy), r=[pr], w=[R("qT", half)])
        for tt, (t0, n) in enumerate(tiles):
            pb, pr = bk[2 + scn % 2], BR(2 + scn % 2)
            scn += 1
            for half in range(2):
                ph.op("tensor", lambda e, half=half: e.matmul(pb[0:n, 128 * half:128 * (half + 1)], qT[:, half, t0:t0 + n], keys[:, h, half, :],
                                                              start=True, stop=True), r=[R("qT", half), R("keys")], w=[pr])
            si = scn % 2
            sc, sc2 = sc_[si], sc2_[si]
            ph.op("scalar", lambda e: e.activation(out=sc[0:n, :], in_=pb[0:n, 0:256], func=AF.Copy), r=[pr], w=[R("sc", si)])
            interleave(*[top16_ops(n, sc[0:n, 128 * half:128 * (half + 1)], sc2[0:n, 128 * half:128 * (half + 1)],
                                   TK[0:n, tt, h, 2 * half, :], jua[0:n, tt, h, half, :],
                                   R("sc", si), R("sc2", si, half), R("TKh", tt, h, half), R("juah", tt, h, half)) for half in range(2)])

    cand = ph.sb("cand", [128, 8, 256])
    cand2 = ph.sb("cand2", [128, 8, 256])
    top = ph.sb("top", [128, 8, 16])
    posf = ph.sb("posf", [128, 8, 16])
    af = ph.sb("af", [128, 8, 16])
    bf = ph.sb("bf", [128, 8, 16])
    rz = ph.sb("rz", [128, 8])
    PK = ph.sb("PK", [128, 3, 128])
    T3 = ph.sb("T3", [128, 3, 128])
    accflat = accT[:].rearrange("p a t -> p (a t)")
    OH2_ = [accflat[:, 2048 * i:2048 * (i + 1)].bitcast(BF16).rearrange("p (t i) -> p t i", i=128) for i in range(2)]
    gOH_ = [accflat[:, 4096 + 2048 * i:4096 + 2048 * (i + 1)].bitcast(BF16).rearrange("p (t i) -> p t i", i=128) for i in range(2)]
    obn = [0]
    GS_ = [accflat[:, 8192 + 4096 * i:8192 + 4096 * (i + 1)].bitcast(BF16).rearrange("p (i t) -> p i t", t=64) for i in range(2)]
    gcn = 0
    evn = 0
    T3_ = [T3, ph.sb("T3b", [128, 3, 128])]

    def r2top(tt):
        t0, n = tiles[tt]
        T3c, rT3 = T3_[tt % 2], R("T3", tt % 2)
        t1 = TK[0:n, tt, :, 0, :]
        j1 = TK[0:n, tt, :, 1, :]
        t2 = TK[0:n, tt, :, 2, :]
        j2 = TK[0:n, tt, :, 3, :]
        c4 = cand[0:n].rearrange("p h (a b) -> p h a b", a=16)
        c24 = cand2[0:n].rearrange("p h (a b) -> p h a b", a=16)

        def heads(h0, h1):
            for h in range(h0, h1, 2):
                interleave(*[top16_ops(n, cand[0:n, hh, :], cand2[0:n, hh, :], top[0:n, hh, :], juc[0:n, hh, :],
                                       R("cand"), R("cand2h", hh), R("toph", hh), R("juch", hh)) for hh in (h, h + 1)])

        def ca():
            ph.op("vector", lambda e: e.tensor_copy(out=TK[0:n, tt, :, 1::2, :], in_=jua[0:n, tt, :, :, :]), r=[R("juah", tt, h_, f_) for h_ in range(8) for f_ in range(2)], w=[R("TK", tt)])
            ph.op("vector", lambda e: e.tensor_tensor(out=c4, in0=t1.unsqueeze(3).to_broadcast([n, 8, 16, 16]),
                                                      in1=t2.unsqueeze(2).to_broadcast([n, 8, 16, 16]), op=ALU.add), r=[R("TK", tt)] + [R("TKh", tt, h_, f_) for h_ in range(8) for f_ in range(2)], w=[R("cand")])
            heads(0, 4)

        def cb():
            heads(4, 8)

        def cc():
            ph.op("vector", lambda e: e.tensor_copy(out=posf[0:n], in_=juc[0:n]), r=[R("juch", h_) for h_ in range(8)] + [R("toph", h_) for h_ in range(8)], w=[R("top")])
            thr = thr16[0:n, :].unsqueeze(1).unsqueeze(1).to_broadcast([n, 8, 16, 16])
            ph.op("vector", lambda e: e.tensor_tensor(out=c24, in0=posf[0:n].unsqueeze(3).to_broadcast([n, 8, 16, 16]), in1=thr, op=ALU.is_ge),
                  r=[R("top"), R("thr16")], w=[R("cand2")] + [R("cand2h", h_) for h_ in range(8)])
            ph.op("vector", lambda e: e.reduce_sum(out=af[0:n], in_=c24, axis=AX.X), r=[R("cand2")], w=[R("ab")])
            ph.op("vector", lambda e: e.scalar_tensor_tensor(out=bf[0:n], in0=af[0:n], scalar=-16.0, in1=posf[0:n], op0=ALU.mult, op1=ALU.add),
                  r=[R("ab"), R("top")], w=[R("ab")])
            io16 = iota[0:n, 0:16].unsqueeze(1).unsqueeze(1).to_broadcast([n, 8, 16, 16])
            for (sel, jt, q) in ((af, j1, 0), (bf, j2, 1)):
                ph.op("vector", lambda e, sel=sel: e.tensor_tensor(out=c24, in0=sel[0:n].unsqueeze(3).to_broadcast([n, 8, 16, 16]), in1=io16, op=ALU.is_equal),
                      r=[R("ab"), R("iota")], w=[R("cand2")])
                ph.op("vector", lambda e, jt=jt: e.tensor_tensor(out=c24, in0=c24, in1=jt.unsqueeze(2).to_broadcast([n, 8, 16, 16]), op=ALU.mult),
                      r=[R("cand2"), R("TK", tt)], w=[R("cand2")])
                ph.op("vector", lambda e, q=q: e.reduce_sum(out=PK[0:n, q, :].rearrange("p (h k) -> p h k", h=8), in_=c24, axis=AX.X),
                      r=[R("cand2")], w=[R("PK")])

        def cd():
            pk2 = PK[0:n, 2, :].rearrange("p (h k) -> p h k", h=8)
            ph.op("vector", lambda e: e.tensor_tensor(out=pk2, in0=top[0:n], in1=top[0:n, :, 0:1].to_broadcast([n, 8, 16]), op=ALU.subtract),
                  r=[R("top")], w=[R("PK")])
            ph.op("scalar", lambda e: e.activation(out=pk2, in_=pk2, func=AF.Exp), r=[R("PK")], w=[R("PK")])
            ph.op("vector", lambda e: e.reduce_sum(out=rz[0:n], in_=pk2, axis=AX.X), r=[R("PK")], w=[R("rz")])
            ph.op("vector", lambda e: e.reciprocal(out=rz[0:n], in_=rz[0:n]), r=[R("rz")], w=[R("rz")])
            ph.op("vector", lambda e: e.tensor_tensor(out=pk2, in0=pk2, in1=rz[0:n].unsqueeze(2).to_broadcast([n, 8, 16]), op=ALU.mult),
                  r=[R("PK"), R("rz")], w=[R("PK")])
            for q in range(3):
                ph.op("tensor", lambda e, q=q: e.transpose(bk[4][:, q * 128:q * 128 + n], PK[0:n, q, :], c["identf"][0:n, 0:n]), r=[R("PK")], w=[BR(4)],
                      signal=(q == 2))
            ph.op("vector", lambda e: e.tensor_copy(out=T3c[:, :, 0:n], in_=bk[4][:, 0:384].rearrange("p (q t) -> p q t", q=3)[:, :, 0:n]), r=[BR(4)], w=[rT3])
        return [ca, cb, cc, cd]

    def scatter(tt, hooks):
        nonlocal gcn, evn
        t0, n = tiles[tt]
        T3c, rT3 = T3_[tt % 2], R("T3", tt % 2)
        subs = [(th0, min(32, n - th0)) for th0 in range(0, n, 32)]
        obs = []
        for _ in subs:
            obs.append(obn[0] % 2)
            obn[0] += 1

        def prep(j):
            th0, nh = subs[j]
            ob = obs[j]
            OH2, gOH = OH2_[ob], gOH_[ob]
            rO, rG = R("OH2", ob), R("gOH", ob)
            ph.op("scalar", lambda e: e.activation(out=OH2[:, 0:nh, :], in_=T3c[:, 1, th0:th0 + nh].unsqueeze(2).to_broadcast([128, nh, 128]), func=AF.Copy),
                  r=[rT3], w=[rO])
            ph.op("scalar", lambda e: e.activation(out=gOH[:, 0:nh, :], in_=T3c[:, 0, th0:th0 + nh].unsqueeze(2).to_broadcast([128, nh, 128]), func=AF.Copy),
                  r=[rT3], w=[rG])
            ph.op("vector", lambda e: e.tensor_tensor(out=OH2[:, 0:nh, :], in0=OH2[:, 0:nh, :], in1=iota_rep[:, 0:nh, :], op=ALU.is_equal),
                  r=[R("iota_rep")], w=[rO])
            ph.op("vector", lambda e: e.tensor_tensor(out=gOH[:, 0:nh, :], in0=gOH[:, 0:nh, :], in1=iota_rep[:, 0:nh, :], op=ALU.is_equal),
                  r=[R("iota_rep")], w=[rG])
            ph.op("gpsimd", lambda e: e.tensor_tensor(out=gOH[:, 0:nh, :], in0=gOH[:, 0:nh, :], in1=T3c[:, 2, th0:th0 + nh].unsqueeze(2).to_broadcast([128, nh, 128]),
                                                      op=ALU.mult), r=[rT3, rG], w=[rG])

        def consume(j):
            nonlocal gcn, evn
            th0, nh = subs[j]
            ob = obs[j]
            OH2, gOH = OH2_[ob], gOH_[ob]
            rO, rG = R("OH2", ob), R("gOH", ob)
            for tq in range(0, nh, 4):
                pb, pr = bk[5 + gcn % 3], BR(5 + gcn % 3)
                gcn += 1
                for t in range(4):
                    ph.op("tensor", lambda e, t=t: e.matmul(pb[:, t * 128:(t + 1) * 128], OH2[:, tq + t, :], gOH[:, tq + t, :], start=True, stop=True),
                          r=[rO, rG], w=[pr], signal=(t == 3))
                src = pb[:, :].rearrange("p (t i) -> p i t", t=4)
                gh = (th0 + tq) // 64
                dst = GS_[gh][:, :, (th0 + tq) % 64:(th0 + tq) % 64 + 4]
                ph.op("scalar", lambda e: e.activation(out=dst, in_=src, func=AF.Copy), r=[pr], w=[R("GS", gh)])
                evn += 1
            if (th0 + nh) % 64 == 0 or th0 + nh == n:
                gh = th0 // 64
                for q in range(4):
                    ph.op("sync", lambda e, q=q: e.dma_start(out=gsc[32 * q:32 * (q + 1), tt, :, 64 * gh:64 * (gh + 1)].rearrange("i p t -> p i t"),
                                                             in_=GS_[gh][:, 32 * q:32 * (q + 1), :]), r=[R("GS", gh)], w=[R("GS", gh)], dma=f"gs{gh}")

        hooks = list(hooks)
        prep(0)
        for j in range(len(subs)):
            if j + 1 < len(subs):
                prep(j + 1)
            if hooks:
                hooks.pop(0)()
            consume(j)
        for hk in hooks:
            hk()

    for ch_ in r2top(0):
        ch_()
    for tt in range(len(tiles)):
        scatter(tt, r2top(tt + 1) if tt + 1 < len(tiles) else [])
    ph.end()


def phase_peer(kb, din, x1T, gsc, accT):
    ph = Phase(kb, "pe")
    R = ph.R
    ns = split_even(NT, 512)
    GC = 4
    NV = 2 * GC
    vb = [ph.sb(f"vb{i}", [128, D], BF16) for i in range(NV)]
    cf = [ph.sb(f"cf{i}", [128, NT], BF16) for i in range(NV)]
    gb = [ph.sb(f"gb{i}", [128, 9, 128], BF16) for i in range(3)]
    hg = [ph.sb(f"hg{i}", [128, NT]) for i in range(2)]
    bk = [ph.ps(f"bk{i}", [128, 512]) for i in range(8)]
    hc = 0
    oc = 0
    for grp in range(128 // GC):
        for ci in range(GC):
            ch = grp * GC + ci
            slot = ch % NV
            ub, ur = ph.wload(din["uT"][:, 128 * ch:128 * (ch + 1)], 128)
            ph.op("gpsimd", lambda e: e.dma_start(out=vb[slot][:], in_=din["vtab"][128 * ch:128 * (ch + 1), :]), w=[R("vb", slot)], dma=f"vb{slot}")
            g_, gr = gb[ch % 3], R("gb", ch % 3)
            ph.op("sync", lambda e: e.dma_start(out=g_[:, 0:8, :], in_=gsc[ch, 0:8, :, :].rearrange("tt p t -> p tt t")), w=[gr], dma=f"gb{ch % 3}")
            o_ = ph.op("sync", lambda e: e.dma_start(out=g_[:, 8, 0:64], in_=gsc[ch, 8, :, 0:64]), dma=f"gb{ch % 3}")
            gr.lw = o_
            gflat = g_[:].rearrange("p a t -> p (a t)")
            hgi, hgr = hg[ch % 2], R("hg", ch % 2)
            for (s0, sz) in ns:
                pb, pr = bk[hc % 4], R("bank", hc % 4)
                hc += 1
                for k in range(16):
                    ph.op("tensor", lambda e, k=k: e.matmul(pb[:, 0:sz], ub[:, k, 0:128], x1T[:, k, s0:s0 + sz], start=(k == 0), stop=(k == 15)),
                          r=[ur], w=[pr], signal=(k == 15))
                ph.op("scalar", lambda e: e.activation(out=hgi[:, s0:s0 + sz], in_=pb[:, 0:sz], func=AF.Gelu), r=[pr], w=[hgr])
            ph.op("vector", lambda e: e.tensor_tensor(out=cf[slot][:], in0=hgi[:], in1=gflat[:, 0:NT], op=ALU.mult), r=[hgr, gr], w=[R("cf", slot)])
        slots = [(grp * GC + ci) % NV for ci in range(GC)]
        for j in range(16):
            for (s0, sz) in ns:
                pb, pr = bk[4 + oc % 4], R("bank", 4 + oc % 4)
                oc += 1
                for ci, slot in enumerate(slots):
                    ph.op("tensor", lambda e, slot=slot, ci=ci: e.matmul(pb[:, 0:sz], vb[slot][:, 128 * j:128 * (j + 1)], cf[slot][:, s0:s0 + sz],
                                                                       start=(ci == 0), stop=(ci == GC - 1)),
                          r=[R("vb", slot), R("cf", slot)], w=[pr], signal=(ci == GC - 1))
                if grp == 0:
                    ph.op("vector", lambda e: e.tensor_copy(out=accT[:, j, s0:s0 + sz], in_=pb[:, 0:sz]), r=[pr], w=[R("acc", j)])
                else:
                    ph.op("vector", lambda e: e.tensor_tensor(out=accT[:, j, s0:s0 + sz], in0=accT[:, j, s0:s0 + sz], in1=pb[:, 0:sz], op=ALU.add),
                          r=[pr, R("acc", j)], w=[R("acc", j)])
    ph.end()


def phase_final(kb, din, dout, accT, x2T, x1s, cvc, ucc):
    c = kb.c
    ph = Phase(kb, "fn")
    R = ph.R
    ns = split_even(NT, 512)
    tiles = [(128 * t, 128) for t in range(8)] + [(NOWN, NS)]
    xl = [ph.sb(f"xl{i}", [128, NT]) for i in range(2)]
    sq = [ph.sb(f"sq{i}", [128, NT]) for i in range(2)]
    gb = ph.sb("gb", [128, 2, 16])
    ph.op("sync", lambda e: e.dma_start(out=gb[:, 0, :], in_=din["ln2gT"][:, :]), w=[R("gb")], dma="g0")
    ph.op("sync", lambda e: e.dma_start(out=gb[:, 1, :], in_=din["ln2bT"][:, :]), w=[R("gb")], dma="g1")
    pTs = ph.sb("pTs", [128, 2, NT], BF16)
    ph.op("gpsimd", lambda e: e.dma_start(out=pTs[:], in_=din["pT"].rearrange("(k p) t -> p k t", p=128)), w=[R("pTs")], dma="pT")
    wpps = ph.sb("wpps", [128, 2, D], BF16)
    ph.op("gpsimd", lambda e: e.dma_start(out=wpps[:], in_=din["wpp"].rearrange("(k p) d -> p k d", p=128)), w=[R("wpps")], dma="wpp")
    banks = [ph.ps(f"bk{i}", [128, 512]) for i in range(2)]
    S1b = [(ph.ps(f"s1_{j}", [128, 512]), R("S1", j)) for j in range(3)]
    S2b = [(ph.ps(f"s2_{j}", [128, 512]), R("S2", j)) for j in range(3)]
    for i in range(16):
        xli, xlr = xl[i % 2], R("xl", i % 2)
        ph.op("sync", lambda e: e.dma_start(out=xli[:], in_=x1s[128 * i:128 * (i + 1), :]), w=[xlr], dma=f"xl{i % 2}")
        ph.op("vector", lambda e: e.tensor_tensor(out=accT[:, i, :], in0=accT[:, i, :], in1=xli[:], op=ALU.add), r=[xlr], w=[R("acc", i)])
        ln_stats_accum(ph, kb, i, lambda s0, sz: accT[:, i, s0:s0 + sz], R("acc", i), sq[i % 2], R("sq", i % 2), S1b, S2b)

    def emit_fn(i, t, tr):
        ph.op("scalar", lambda e: e.activation(out=accT[:, i, :], in_=t[:], func=AF.Identity, scale=gb[:, 0, i:i + 1], bias=gb[:, 1, i:i + 1]),
              r=[tr, R("gb")], w=[R("acc", i)])
        ph.op("vector", lambda e: e.tensor_copy(out=x2T[:, i, :], in_=accT[:, i, :]), r=[R("acc", i)], w=[R("x2T", i)])
    ln_feature_major(ph, kb, accT, lambda i: R("acc", i), S1b, S2b, xl, [R("xl", 0), R("xl", 1)], emit_fn, "l2")
    x2_res = [R("x2T", i) for i in range(16)]
    sg = sq[0]
    yt = sq[1]
    stage = [ph.sb(f"stage{i}", [128, 9, 128]) for i in range(2)]
    bc_ = 0
    for i in range(16):
        w, wr = ph.wload(din["wple"][:, 128 * i:128 * (i + 1)], 128)
        for (s0, sz) in ns:
            pb, pr = banks[bc_ % 2], R("bank", bc_ % 2)
            bc_ += 1
            mm16(ph, pb[:, 0:sz], pr, w, wr, 128, lambda k, s0=s0, sz=sz: x2T[:, k, s0:s0 + sz], x2_res)
            ph.op("scalar", lambda e: e.activation(out=sg[:, s0:s0 + sz], in_=pb[:, 0:sz], func=AF.Sigmoid), r=[pr], w=[R("sq", 0)])
        for j, (s0, sz) in enumerate(ns):
            pb, pr = S2b[j]
            for kk in range(2):
                ph.op("tensor", lambda e, kk=kk: e.matmul(pb[:, 0:sz], wpps[:, kk, 128 * i:128 * (i + 1)], pTs[:, kk, s0:s0 + sz], start=(kk == 0), stop=(kk == 1)),
                      r=[R("wpps"), R("pTs")], w=[pr], signal=(kk == 1))
            ph.op("vector", lambda e: e.tensor_tensor(out=yt[:, s0:s0 + sz], in0=sg[:, s0:s0 + sz], in1=pb[:, 0:sz], op=ALU.mult), r=[pr, R("sq", 0)], w=[R("sq", 1)])
        ph.op("vector", lambda e: e.tensor_tensor(out=yt[:], in0=yt[:], in1=accT[:, i, :], op=ALU.add), r=[R("sq", 1), R("acc", i)], w=[R("sq", 1)])
        st, sr = stage[i % 2], R("stage", i % 2)
        for tt, (t0, n) in enumerate(tiles):
            pb, pr = S1b[tt // 4]
            ph.op("tensor", lambda e: e.transpose(pb[0:n, (tt % 4) * 128:(tt % 4 + 1) * 128], yt[:, t0:t0 + n], c["identf"][:]), r=[R("sq", 1)], w=[pr])
        for q in range(2):
            ph.op("vector" if q == 0 else "scalar",
                  (lambda e: e.tensor_copy(out=st[:, 0:4, :].rearrange("p a c -> p (a c)"), in_=S1b[0][0][:, :])) if q == 0 else
                  (lambda e: e.activation(out=st[:, 4:8, :].rearrange("p a c -> p (a c)"), in_=S1b[1][0][:, :], func=AF.Copy)),
                  r=[S1b[q][1]], w=[R("stage", i % 2, q)])
        ph.op("vector", lambda e: e.tensor_copy(out=st[0:NS, 8, :], in_=S1b[2][0][0:NS, 0:128]), r=[S1b[2][1]], w=[R("stage", i % 2, 2)])
        ph.op("sync", lambda e: e.dma_start(out=dout["y"][0:NOWN, 128 * i:128 * (i + 1)].rearrange("(a p) c -> p a c", p=128), in_=st[:, 0:8, :]),
              r=[R("stage", i % 2, 0), R("stage", i % 2, 1)], w=[R("stage", i % 2, 0), R("stage", i % 2, 1)], dma=f"yo{i % 2}")
        ph.op("sync", lambda e: e.dma_start(out=dout["y"][NOWN:NT, 128 * i:128 * (i + 1)], in_=st[0:NS, 8, :]),
              r=[R("stage", i % 2, 2)], w=[R("stage", i % 2, 2)], dma=f"yos{i % 2}")
    rows = ph.sb("rows", [32, 3072])
    for q in range(6):
        pb, pr = banks[q % 2], R("bank", q % 2)
        for m in range(4):
            ph.op("tensor", lambda e, m=m: e.transpose(pb[0:19, m * 128:(m + 1) * 128], cvc[:, 4 * q + m, :], c["identf"][:]), w=[pr], signal=(m == 3))
        ph.op("vector", lambda e: e.tensor_copy(out=rows[0:19, 512 * q:512 * (q + 1)], in_=pb[0:19, :]), r=[pr], w=[R("rows")])
    ph.op("sync", lambda e: e.dma_start(out=dout["cvp"][:, :], in_=rows[0:3, :]), r=[R("rows")], dma="cv")
    ph.op("sync", lambda e: e.dma_start(out=dout["cvs"][:, 2, :], in_=rows[3:19, :]), r=[R("rows")], dma="cv")
    ph.op("sync", lambda e: e.dma_start(out=dout["cvs"][:, 0:2, :], in_=din["cv0"][:, 1:3, :]), dma="cv")
    rows2 = ph.sb("rows2", [32, 2048])
    for q in range(4):
        pb, pr = banks[q % 2], R("bank", q % 2)
        for m in range(4):
            ph.op("tensor", lambda e, m=m: e.transpose(pb[0:18, m * 128:(m + 1) * 128], ucc[:, 4 * q + m, :], c["identf"][:]), w=[pr], signal=(m == 3))
        ph.op("vector", lambda e: e.tensor_copy(out=rows2[0:18, 512 * q:512 * (q + 1)], in_=pb[0:18, :]), r=[pr], w=[R("rows2")])
    ph.op("sync", lambda e: e.dma_start(out=dout["scp"][:, :], in_=rows2[0:2, :]), r=[R("rows2")], dma="cv")
    ph.op("sync", lambda e: e.dma_start(out=dout["scs"][:, 1, :], in_=rows2[2:18, :]), r=[R("rows2")], dma="cv")
    ph.op("sync", lambda e: e.dma_start(out=dout["scs"][:, 0, :], in_=din["sc0"][:, 1, :]), dma="cv")
    ph.end()


def _prep_inputs(inp):
    f = np.float32
    A = lambda x: np.ascontiguousarray(np.asarray(x, dtype=f))
    xp = A(inp["x_prompt"])
    xs = A(inp["x_sample"])[:, 0]
    pp = A(inp["p_prompt"])[0]
    psm = A(inp["p_sample"])[0][:, 0]
    ssm = A(inp["state_ssm"])[0]
    cv = A(inp["state_ssd_conv"])[0]
    sc = A(inp["state_shortconv"])[0]
    cw = A(inp["ssd_conv_w"])[0]
    scw = A(inp["sc_conv_w"])[0]
    shared = dict(
        w_in=A(inp["w_in"])[0],
        convwT=A(cw.T.reshape(24, 128, 4).transpose(1, 0, 2).reshape(128, 96)),
        convb=A(A(inp["ssd_conv_b"])[0].reshape(24, 128).T),
        dtb=A(inp["ssd_dt_bias"]).reshape(1, 32), alog=A(inp["ssd_a_log"]).reshape(1, 32), dsk=A(inp["ssd_d"]).reshape(1, 32),
        normw=A(inp["ssd_norm_w"]).reshape(1, 2048),
        scwT=A(scw.T.reshape(16, 128, 3).transpose(1, 0, 2).reshape(128, 48)),
        w_bssd=A(inp["w_branch_ssd"])[0], w_bsc=A(inp["w_branch_sc"])[0], w_out=A(inp["w_out"])[0],
        ln1gT=A(A(inp["ln1_g"]).reshape(16, 128).T), ln1bT=A(A(inp["ln1_b"]).reshape(16, 128).T),
        wq=A(inp["peer_wq"])[0],
        k1T=A(A(inp["peer_keys1"])[0].transpose(0, 2, 1)), k2T=A(A(inp["peer_keys2"])[0].transpose(0, 2, 1)),
        uT=A(A(inp["peer_u"])[0].T), vtab=A(inp["peer_v"])[0],
        ln2gT=A(A(inp["ln2_g"]).reshape(16, 128).T), ln2bT=A(A(inp["ln2_b"]).reshape(16, 128).T),
        wple=A(inp["ple_gate_w"])[0], wpp=A(inp["ple_proj_w"])[0],
        c_ident=np.eye(128, dtype=f), c_tri=np.triu(np.ones((128, 128), dtype=f)),
        c_iota=np.arange(256, dtype=f).reshape(1, 256),
        c_negm=np.ascontiguousarray(np.tril(np.full((128, 128), -30000.0, dtype=f), -1)),
    )
    maps = []
    for c in range(NCORES):
        s_, hf = c // 2, c % 2
        own = xp[s_, hf * NOWN:(hf + 1) * NOWN]
        smp = xs[c * NS:(c + 1) * NS]
        if hf:
            prev = xp[s_, 0:NOWN]
            halo = xp[s_, NOWN - 3:NOWN]
        else:
            prev = np.zeros((NOWN, D), f)
            halo = np.zeros((3, D), f)
        cvc_ = cv[c * NS:(c + 1) * NS]
        scc_ = sc[c * NS:(c + 1) * NS]
        m = dict(shared)
        m.update(
            xprevT=A(prev.T), xextT=A(np.concatenate([halo, own, smp], 0).T),
            xown=A(np.concatenate([own, smp], 0)),
            pT=A(np.concatenate([pp[s_, hf * NOWN:(hf + 1) * NOWN], psm[c * NS:(c + 1) * NS]], 0).T),
            flag=np.full((128, 1), float(hf), f),
            ssm0=A(ssm[c * NS:(c + 1) * NS].reshape(NS, 2048, 128)),
            cv0=A(cvc_), sc0=A(scc_),
            cv0T=A(cvc_.reshape(NS, 3, 24, 128).transpose(3, 2, 0, 1).reshape(128, 24 * NS * 3)),
            sc0T=A(scc_.reshape(NS, 2, 16, 128).transpose(3, 2, 0, 1).reshape(128, 16 * NS * 2)),
        )
        maps.append(m)
    return maps


_CACHE = {}


def run_cores(inp, stage=99, trace=False, only=None):
    key = (stage, tuple(sorted(DEBUG)), CUT)
    if key not in _CACHE:
        _CACHE[key] = build_program(stage)
    nc, din, dout = _CACHE[key]
    maps = _prep_inputs(inp)
    maps = [{k: v for k, v in m.items() if k in din} for m in maps]
    if only is not None:
        return run_bass_kernel_spmd(nc, [maps[only]], core_ids=[0], trace=trace)
    res = run_bass_kernel_spmd(nc, maps, core_ids=list(range(NCORES)), trace=trace)
    return res


def kernel(**inp):
    res = run_cores(inp).results
    f = np.float32
    yp = np.zeros((4, 2048, D), f)
    ys = np.zeros((128, 1, D), f)
    hp = np.zeros((1, 4, 32, 64, 128), f)
    cp = np.zeros((1, 4, 3, 3072), f)
    sp = np.zeros((1, 4, 2, 2048), f)
    hs = np.zeros((1, 128, 32, 64, 128), f)
    cs = np.zeros((1, 128, 3, 3072), f)
    ss = np.zeros((1, 128, 2, 2048), f)
    for c in range(NCORES):
        r = res[c]
        s_, hf = c // 2, c % 2
        y = np.asarray(r["y"])
        yp[s_, hf * NOWN:(hf + 1) * NOWN] = y[:NOWN]
        ys[c * NS:(c + 1) * NS, 0] = y[NOWN:]
        if hf:
            hp[0, s_] = np.asarray(r["hfin"]).reshape(32, 64, 128)
            cp[0, s_] = np.asarray(r["cvp"])
            sp[0, s_] = np.asarray(r["scp"])
        hs[0, c * NS:(c + 1) * NS] = np.asarray(r["hs"]).reshape(NS, 32, 64, 128)
        cs[0, c * NS:(c + 1) * NS] = np.asarray(r["cvs"])
        ss[0, c * NS:(c + 1) * NS] = np.asarray(r["scs"])
    return (yp, ys, hp, cp, sp, hs, cs, ss)
```
